# Optimizing a Trainium2 kernel written in Bass

```python
import jax, jax.numpy as jnp
from jax import lax
import numpy as np

D_MODEL = 1024
BATCH = 32
SEQ = 2048
DEPTH = 1

MEM_LEN = 256
GDN_HEADS = 8
GDN_DK = 128
GDN_DV = 128
GDN_CONV = 4
GDN_CHUNK = 64
GDN_QK_W = GDN_HEADS * GDN_DK
GDN_V_W = GDN_HEADS * GDN_DV
GDN_QKV_W = 2 * GDN_QK_W + GDN_V_W
SB_HEADS = 8
SB_DH = 128
SB_W = SB_HEADS * SB_DH
SB_BLOCK = 128
X_HEADS = 4
X_DH = D_MODEL // X_HEADS
D_FF = 2816
FFN_CONV = 3
EPS = 1e-6
REST_WIDTHS = (GDN_HEADS, GDN_HEADS, GDN_V_W, SB_W, SB_W, SB_W, D_MODEL, D_MODEL)
IN_W = GDN_QKV_W + sum(REST_WIDTHS)

kernel_name = "hybrid_gdn_stickbreak_memxattn_convffn"


def rmsnorm(x, g):
    xf = x.astype(jnp.float32)
    y = xf * lax.rsqrt(jnp.mean(xf * xf, axis=-1, keepdims=True) + EPS)
    return (y * g.astype(jnp.float32)).astype(x.dtype)


def l2norm(x):
    return x * lax.rsqrt(jnp.sum(x * x, axis=-1, keepdims=True) + EPS)


def causal_dwconv(x, w):
    k = w.shape[0]
    return lax.conv_general_dilated(
        x, w[:, None, :].astype(x.dtype), window_strides=(1,), padding=[(k - 1, 0)],
        dimension_numbers=("NWC", "WIO", "NWC"), feature_group_count=x.shape[-1])


def split_heads(t, h):
    return t.reshape(t.shape[:-1] + (h, t.shape[-1] // h))


def to_chunks(t):
    b, s, h = t.shape[:3]
    t = jnp.moveaxis(t, 2, 1)
    return t.reshape((b, h, s // GDN_CHUNK, GDN_CHUNK) + t.shape[3:])


def gated_deltanet(q, k, v, a, bgate, z, a_log, dt_bias, out_norm):
    b, s, h, _ = q.shape
    f32 = jnp.float32
    q = l2norm(q.astype(f32)) * (GDN_DK ** -0.5)
    k = l2norm(k.astype(f32))
    v = v.astype(f32)
    beta = jax.nn.sigmoid(bgate.astype(f32))
    g = -jnp.exp(a_log.astype(f32)) * jax.nn.softplus(a.astype(f32) + dt_bias.astype(f32))

    qc, kc, vc = to_chunks(q), to_chunks(k), to_chunks(v)
    betac, gc = to_chunks(beta), jnp.cumsum(to_chunks(g), axis=-1)

    idx = jnp.arange(GDN_CHUNK)
    incl = idx[:, None] >= idx[None, :]
    strict = idx[:, None] > idx[None, :]
    decay = jnp.exp(jnp.where(incl, gc[..., :, None] - gc[..., None, :], -jnp.inf))

    kk = jnp.einsum("bhncd,bhnmd->bhncm", kc, kc)
    m_strict = jnp.where(strict, betac[..., :, None] * kk * decay, 0.0)
    rhs = jnp.concatenate([vc * betac[..., None],
                           kc * (betac * jnp.exp(gc))[..., None]], axis=-1)
    sol = lax.linalg.triangular_solve(m_strict, rhs, left_side=True, lower=True,
                                      unit_diagonal=True)
    u, w = sol[..., :GDN_DV], sol[..., GDN_DV:]

    qk = jnp.einsum("bhncd,bhnmd->bhncm", qc, kc) * decay
    q_dec = qc * jnp.exp(gc)[..., None]
    k_dec = kc * jnp.exp(gc[..., -1:] - gc)[..., None]
    chunk_decay = jnp.exp(gc[..., -1])

    def step(state, inp):
        u_n, w_n, qk_n, qd_n, kd_n, cd_n = inp
        v_new = u_n - jnp.einsum("bhck,bhkv->bhcv", w_n, state)
        o_n = (jnp.einsum("bhck,bhkv->bhcv", qd_n, state)
               + jnp.einsum("bhcm,bhmv->bhcv", qk_n, v_new))
        state = (state * cd_n[..., None, None]
                 + jnp.einsum("bhck,bhcv->bhkv", kd_n, v_new))
        return state, o_n

    xs = tuple(jnp.moveaxis(t, 2, 0) for t in (u, w, qk, q_dec, k_dec, chunk_decay))
    state0 = jnp.zeros((b, h, GDN_DK, GDN_DV), f32)
    _, o = lax.scan(step, state0, xs)
    o = jnp.transpose(o, (1, 0, 3, 2, 4)).reshape(b, s, h, GDN_DV)
    o = rmsnorm(o, out_norm) * jax.nn.silu(z.astype(f32))
    return o.reshape(b, s, h * GDN_DV)


def stick_breaking_attention(q, k, v):
    b, s, h, d = q.shape
    q = q * (d ** -0.5)
    outs = []
    for blk in range(s // SB_BLOCK):
        t0 = blk * SB_BLOCK
        end = t0 + SB_BLOCK
        z = jnp.einsum("bthd,bshd->bhts", q[:, t0:end], k[:, :end]).astype(jnp.float32)
        t_idx = t0 + jnp.arange(SB_BLOCK)
        s_idx = jnp.arange(end)
        causal = s_idx[None, :] < t_idx[:, None]
        log_fail = jnp.where(causal, jax.nn.log_sigmoid(-z), 0.0)
        suffix = lax.cumsum(log_fail, axis=3, reverse=True)
        suffix_excl = jnp.pad(suffix[..., 1:], ((0, 0), (0, 0), (0, 0), (0, 1)))
        weights = jnp.where(causal, jnp.exp(jax.nn.log_sigmoid(z) + suffix_excl), 0.0)
        outs.append(jnp.einsum("bhts,bshd->bthd", weights.astype(v.dtype), v[:, :end]))
    return jnp.concatenate(outs, axis=1).reshape(b, s, h * d)


def hybrid_mixer(x, norm_mix, w_in, conv_gdn, a_log, dt_bias, gdn_out_norm,
                 w_proj_gdn, w_proj_sb, w_out):
    xn = rmsnorm(x, norm_mix)
    proj = xn @ w_in
    qkv = jax.nn.silu(causal_dwconv(proj[..., :GDN_QKV_W], conv_gdn))
    gq = split_heads(qkv[..., :GDN_QK_W], GDN_HEADS)
    gk = split_heads(qkv[..., GDN_QK_W:2 * GDN_QK_W], GDN_HEADS)
    gv = split_heads(qkv[..., 2 * GDN_QK_W:], GDN_HEADS)
    splits = tuple(int(i) for i in np.cumsum(REST_WIDTHS)[:-1])
    a, bg, z, sq, sk, sv, gate_a, gate_b = jnp.split(proj[..., GDN_QKV_W:], splits, axis=-1)
    o_a = gated_deltanet(gq, gk, gv, a, bg, split_heads(z, GDN_HEADS),
                         a_log, dt_bias, gdn_out_norm).astype(x.dtype)
    o_b = stick_breaking_attention(split_heads(sq, SB_HEADS), split_heads(sk, SB_HEADS),
                                   split_heads(sv, SB_HEADS))
    merged = (jax.nn.sigmoid(gate_a) * (o_a @ w_proj_gdn)
              + jax.nn.sigmoid(gate_b) * (o_b @ w_proj_sb))
    return merged @ w_out


def memory_cross_attention(h, mem, norm_x, norm_mem, w_xq, w_xkv, xq_norm, xk_norm, w_xo):
    b, s, _ = h.shape
    hn = rmsnorm(h, norm_x)
    mn = rmsnorm(mem, norm_mem)
    q = rmsnorm(split_heads(hn @ w_xq, X_HEADS), xq_norm)
    kv = mn @ w_xkv
    k = rmsnorm(split_heads(kv[..., :D_MODEL], X_HEADS), xk_norm)
    v = split_heads(kv[..., D_MODEL:], X_HEADS)
    scores = jnp.einsum("bshd,bmhd->bhsm", q, k).astype(jnp.float32) * (X_DH ** -0.5)
    p = jax.nn.softmax(scores, axis=-1)
    o = jnp.einsum("bhsm,bmhd->bshd", p.astype(v.dtype), v).reshape(b, s, D_MODEL)
    return o @ w_xo


def conv_gated_mlp(h, norm_ffn, w_up, conv_ffn, w_down):
    hn = rmsnorm(h, norm_ffn)
    u = causal_dwconv(hn @ w_up, conv_ffn)
    return (jax.nn.silu(u[..., :D_FF]) * u[..., D_FF:]) @ w_down


def setup_inputs(seed: int = 0) -> dict:
    key = jax.random.key(seed)
    ks = jax.random.split(key, 24)
    L = DEPTH

    def dense(k, fan_in, fan_out):
        return jax.random.normal(k, (L, fan_in, fan_out), jnp.float32) * fan_in ** -0.5

    def gain(k, n):
        return 1.0 + 0.02 * jax.random.normal(k, (L, n), jnp.float32)

    dt = jnp.exp(jax.random.uniform(ks[5], (L, GDN_HEADS), jnp.float32,
                                    np.log(1e-3), np.log(1e-1)))
    return {
        "x": jax.random.normal(ks[0], (BATCH, SEQ, D_MODEL), jnp.float32),
        "mem": jax.random.normal(ks[1], (BATCH, MEM_LEN, D_MODEL), jnp.float32),
        "norm_mix": gain(ks[2], D_MODEL),
        "w_in": dense(ks[3], D_MODEL, IN_W),
        "conv_gdn": jax.random.normal(ks[4], (L, GDN_CONV, GDN_QKV_W), jnp.float32) * GDN_CONV ** -0.5,
        "a_log": jnp.log(jax.random.uniform(ks[6], (L, GDN_HEADS), jnp.float32, 1.0, 16.0)),
        "dt_bias": dt + jnp.log(-jnp.expm1(-dt)),
        "gdn_out_norm": gain(ks[7], GDN_DV),
        "w_proj_gdn": dense(ks[8], GDN_V_W, D_MODEL),
        "w_proj_sb": dense(ks[9], SB_W, D_MODEL),
        "w_out": dense(ks[10], D_MODEL, D_MODEL),
        "norm_x": gain(ks[11], D_MODEL),
        "norm_mem": gain(ks[12], D_MODEL),
        "w_xq": dense(ks[13], D_MODEL, D_MODEL),
        "w_xkv": dense(ks[14], D_MODEL, 2 * D_MODEL),
        "xq_norm": gain(ks[15], X_DH),
        "xk_norm": gain(ks[16], X_DH),
        "w_xo": dense(ks[17], D_MODEL, D_MODEL),
        "norm_ffn": gain(ks[18], D_MODEL),
        "w_up": dense(ks[19], D_MODEL, 2 * D_FF),
        "conv_ffn": jax.random.normal(ks[20], (L, FFN_CONV, 2 * D_FF), jnp.float32) * FFN_CONV ** -0.5,
        "w_down": dense(ks[21], D_FF, D_MODEL),
    }


def reference(x, mem, norm_mix, w_in, conv_gdn, a_log, dt_bias, gdn_out_norm,
              w_proj_gdn, w_proj_sb, w_out, norm_x, norm_mem, w_xq, w_xkv,
              xq_norm, xk_norm, w_xo, norm_ffn, w_up, conv_ffn, w_down):
    h = x
    for l in range(DEPTH):
        h = h + hybrid_mixer(h, norm_mix[l], w_in[l], conv_gdn[l], a_log[l], dt_bias[l],
                             gdn_out_norm[l], w_proj_gdn[l], w_proj_sb[l], w_out[l])
        h = h + memory_cross_attention(h, mem, norm_x[l], norm_mem[l], w_xq[l], w_xkv[l],
                                       xq_norm[l], xk_norm[l], w_xo[l])
        h = h + conv_gated_mlp(h, norm_ffn[l], w_up[l], conv_ffn[l], w_down[l])
    return h
```

```python
import contextlib
import math
import numpy as np
import concourse.bass as bass
import concourse.mybir as mybir
from concourse.bass_utils import run_bass_kernel_spmd

F32 = mybir.dt.float32
BF16 = mybir.dt.bfloat16
AF = mybir.ActivationFunctionType
ALU = mybir.AluOpType

D = 1024
SEQ = 2048
NT = 16
NG = 4
NH = 8
DFF = 2816
MEM = 256
IN_W = 9232
C_GQ, C_GK, C_GV, C_A, C_Z, C_SQ, C_SK, C_SV, C_GA, C_GB = 0, 1024, 2048, 3072, 3088, 4112, 5136, 6160, 7184, 8208
EPS = 1e-6
NEG = -30000.0

PARAM_NAMES = ["norm_mix", "w_in", "conv_gdn", "a_log", "dt_bias", "gdn_out_norm", "w_proj_gdn",
               "w_proj_sb", "w_out", "norm_x", "norm_mem", "w_xq", "w_xkv", "xq_norm", "xk_norm",
               "w_xo", "norm_ffn", "w_up", "conv_ffn", "w_down"]
PARAM_SHAPES = {
    "norm_mix": [1, 1024], "w_in": [1, 1024, IN_W], "conv_gdn": [1, 4, 3072], "a_log": [1, 8],
    "dt_bias": [1, 8], "gdn_out_norm": [1, 128], "w_proj_gdn": [1, 1024, 1024],
    "w_proj_sb": [1, 1024, 1024], "w_out": [1, 1024, 1024], "norm_x": [1, 1024],
    "norm_mem": [1, 1024], "w_xq": [1, 1024, 1024], "w_xkv": [1, 1024, 2048], "xq_norm": [1, 256],
    "xk_norm": [1, 256], "w_xo": [1, 1024, 1024], "norm_ffn": [1, 1024], "w_up": [1, 1024, 2 * DFF],
    "conv_ffn": [1, 3, 2 * DFF], "w_down": [1, DFF, 1024],
}
BIG_W = ["w_in", "w_proj_gdn", "w_proj_sb", "w_out", "w_xq", "w_xkv", "w_xo", "w_up", "w_down"]


class _Eng:
    def __init__(self, name, h, sem):
        self.name = name
        self.h = h
        self.sem = sem
        self.n = 0
        self.waited = {}


class DmaSem:
    def __init__(self, sem):
        self.sem = sem
        self.total = 0


class Sched:
    def __init__(self, nc):
        self.nc = nc
        self.es = contextlib.ExitStack()
        self.eng = {}
        for name, h in (("pe", nc.tensor), ("act", nc.scalar), ("dve", nc.vector),
                        ("pool", nc.gpsimd), ("sp", nc.sync)):
            sem = self.es.enter_context(nc.semaphore("s_" + name))
            self.eng[name] = _Eng(name, h, sem)
        self.lastw = {}
        self.readers = {}
        self.nsem = 0
        self.ntile = 0
        self.psum_tiles = []
        self.psum_rr = 0
        self.n_rr = 6
        self.stream_rr = {}

    def dsem(self):
        self.nsem += 1
        return DmaSem(self.es.enter_context(self.nc.semaphore("d%d" % self.nsem)))

    def sb(self, shape, dtype, name=None):
        self.ntile += 1
        nm = "%s_%d" % (name or "t", self.ntile)
        return self.es.enter_context(self.nc.sbuf_tensor(nm, list(shape), dtype))

    def init_psum(self, n=8, n_rr=6):
        for i in range(n):
            t = self.es.enter_context(self.nc.psum_tensor("ps%d" % i, [128, 512], F32))
            self.psum_tiles.append(t)
        self.n_rr = n_rr

    STREAM_BANKS = {"neu": [0, 1, 2], "rec": [3], "sb": [4, 5], "xa0": [0, 1, 2, 3], "xa1": [4, 5, 6, 7]}

    def psum(self, stream=None):
        if stream is None:
            i = self.psum_rr % self.n_rr
            self.psum_rr += 1
        else:
            banks = self.STREAM_BANKS[stream]
            cnt = self.stream_rr.get(stream, 0)
            self.stream_rr[stream] = cnt + 1
            i = banks[cnt % len(banks)]
        return self.psum_tiles[i], ("ps", i)

    def psum_fixed(self, i):
        return self.psum_tiles[i], ("ps", i)

    def _collect(self, eng, reads, writes):
        waits = {}

        def need(dep, raw):
            if dep[0] == "eng":
                _, fname, cnt = dep
                if fname == eng:
                    if eng in ("pe", "sp") or not raw:
                        return
                F = self.eng[fname]
                key = id(F.sem)
                if waits.get(key, (None, 0))[1] < cnt:
                    waits[key] = (F.sem, cnt)
            else:
                _, ds, val = dep
                key = id(ds.sem)
                if waits.get(key, (None, 0))[1] < val:
                    waits[key] = (ds.sem, val)

        for k in reads:
            d = self.lastw.get(k)
            if d is not None:
                need(d, True)
        for k in writes:
            d = self.lastw.get(k)
            if d is not None:
                need(d, False)
            for r in self.readers.get(k, ()):
                need(r, False)
        return waits

    def _emit_waits(self, E, waits):
        for key, (sem, val) in waits.items():
            if E.waited.get(key, 0) < val:
                E.h.wait_ge(sem, val)
                E.waited[key] = val

    def _record(self, me, reads, writes):
        for k in reads:
            lst = self.readers.setdefault(k, [])
            if me[0] == "eng":
                lst[:] = [r for r in lst if not (r[0] == "eng" and r[1] == me[1])]
            else:
                lst[:] = [r for r in lst if not (r[0] == "dma" and r[1] is me[1])]
            lst.append(me)
        for k in writes:
            self.lastw[k] = me
            self.readers[k] = []

    def op(self, eng, fn, reads=(), writes=()):
        E = self.eng[eng]
        psr = [k for k in reads if isinstance(k, tuple) and k[0] == "ps"]
        if psr:
            writes = list(writes) + [k for k in psr if k not in writes]
        self._emit_waits(E, self._collect(eng, reads, writes))
        ins = fn(E.h)
        E.n += 1
        ins.then_inc(E.sem, 1)
        self._record(("eng", eng, E.n), reads, writes)
        return ins

    def dma(self, queue, out, in_, ds, reads=(), writes=(), **kw):
        Q = self.eng[queue]
        self._emit_waits(Q, self._collect(queue, reads, writes))
        ins = Q.h.dma_start(out=out, in_=in_, **kw)
        ds.total += 16
        ins.then_inc(ds.sem, 16)
        self._record(("dma", ds, ds.total), reads, writes)
        return ins

    def wait_all(self, eng, keys):
        E = self.eng[eng]
        self._emit_waits(E, self._collect(eng, keys, ()))

    def close(self):
        self.es.close()


class Builder:
    def __init__(self, nseq=4, dbg=False):
        self.nseq = nseq
        self.dbg = dbg
        self.nheads = NH
        nc = self.nc = bass.Bass("TRN2", target_bir_lowering=False)
        self.x = nc.dram_tensor("x", [nseq, SEQ, D], F32, kind="ExternalInput").ap()
        self.mem = nc.dram_tensor("mem", [nseq, MEM, D], F32, kind="ExternalInput").ap()
        self.P = {}
        for n in PARAM_NAMES:
            self.P[n] = nc.dram_tensor(n, PARAM_SHAPES[n], F32, kind="ExternalInput").ap()
        self.out = nc.dram_tensor("out", [nseq, SEQ, D], F32, kind="ExternalOutput").ap()
        self.WB = {}
        for n in BIG_W:
            shp = PARAM_SHAPES[n][1:]
            self.WB[n] = nc.dram_tensor(n + "_bf", shp, BF16, kind="Internal").ap()
        if dbg:
            self.dbg_oa = nc.dram_tensor("dbg_oa", [128, 8 * SEQ], F32, kind="ExternalOutput").ap()
            self.dbg_ob = nc.dram_tensor("dbg_ob", [128, 8 * SEQ], F32, kind="ExternalOutput").ap()
            self.dbg_h1 = nc.dram_tensor("dbg_h1", [SEQ, D], F32, kind="ExternalOutput").ap()
            self.dbg_h2 = nc.dram_tensor("dbg_h2", [SEQ, D], F32, kind="ExternalOutput").ap()
        self.S = Sched(nc)
        self.S.init_psum(8, 6)

    def mm(self, out, lhsT, rhs, start, stop, reads, writes):
        self.S.op("pe", lambda e: e.matmul(out, lhsT=lhsT, rhs=rhs, start=start, stop=stop), reads, writes)

    def act(self, out, in_, func, reads, writes, scale=1.0, bias=0.0, accum_out=None):
        if accum_out is None:
            self.S.op("act", lambda e: e.activation(out=out, in_=in_, func=func, bias=bias, scale=scale), reads, writes)
        else:
            self.S.op("act", lambda e: e.activation(out=out, in_=in_, func=func, bias=bias, scale=scale,
                                                    accum_out=accum_out), reads, writes)

    def tt(self, eng, out, in0, in1, op, reads, writes):
        self.S.op(eng, lambda e: e.tensor_tensor(out=out, in0=in0, in1=in1, op=op), reads, writes)

    def ts(self, eng, out, in0, s1, s2, op0, op1, reads, writes):
        if s2 is None:
            self.S.op(eng, lambda e: e.tensor_scalar(out=out, in0=in0, scalar1=s1, scalar2=None, op0=op0), reads, writes)
        else:
            self.S.op(eng, lambda e: e.tensor_scalar(out=out, in0=in0, scalar1=s1, scalar2=s2, op0=op0, op1=op1),
                      reads, writes)

    def stt(self, out, in0, scalar, in1, op0, op1, reads, writes):
        self.S.op("dve", lambda e: e.scalar_tensor_tensor(out=out, in0=in0, scalar=scalar, in1=in1, op0=op0, op1=op1),
                  reads, writes)

    def cp(self, eng, out, in_, reads, writes):
        if eng == "act":
            self.act(out, in_, AF.Copy, reads, writes)
        else:
            self.S.op(eng, lambda e: e.tensor_copy(out=out, in_=in_), reads, writes)

    def setup_weights(self):
        S = self.S
        self.cast_list = []
        self.cast_sem = {}
        self.cast_done = []
        order = ["w_in", "w_proj_gdn", "w_proj_sb", "w_out", "w_xq", "w_xkv", "w_xo", "w_up", "w_down"]
        for n in order:
            src = self.P[n][0]
            dst = self.WB[n]
            self.cast_sem[dst.name] = S.dsem()
            rows, cols = src.shape
            for r0 in range(0, rows, 128):
                for c0 in range(0, cols, 2048):
                    c1 = min(cols, c0 + 2048)
                    self.cast_list.append((dst[r0:r0 + 128, c0:c1], src[r0:r0 + 128, c0:c1], dst.name))
        self.cast_pos = 0
        n_in = sum(1 for c in self.cast_list if c[2] == self.WB["w_in"].name)
        self.emit_casts(n_in)
        self.NSLOT = 3
        self.wslots = [S.sb([128, 8, 512], BF16, "wslot") for _ in range(self.NSLOT)]
        self.wsem = [S.dsem() for _ in range(self.NSLOT)]
        self.wrr = 0

    def emit_casts(self, n):
        S = self.S
        for _ in range(n):
            if self.cast_pos >= len(self.cast_list):
                return
            dst, src, nm = self.cast_list[self.cast_pos]
            k = self.cast_pos
            self.cast_pos += 1
            if k >= 4:
                sem, tot = self.cast_done[k - 4]
                S.eng["pool"].h.wait_ge(sem, tot)
            ds = self.cast_sem[nm]
            S.dma("pool", dst, src, ds, writes=[("wcast", nm)])
            self.cast_done.append((ds.sem, ds.total))

    def wslot(self):
        i = self.wrr % self.NSLOT
        self.wrr += 1
        return i

    def wload(self, i, src, nk, ncols, col_off=0, k_off=0):
        self.S.dma("sp", self.wslots[i][:, k_off:k_off + nk, col_off:col_off + ncols],
                   src.rearrange("(kc p) n -> p kc n", p=128), self.wsem[i],
                   reads=[("wcast", src.name)], writes=[("w", i)])

    def setup_consts(self):
        S = self.S
        sb = S.sb
        P = self.P
        self.ones_f = sb([128, 128], F32, "ones_f")
        self.ones_b = sb([128, 128], BF16, "ones_b")
        self.ident_f = sb([128, 128], F32, "ident_f")
        self.ident_b = sb([128, 128], BF16, "ident_b")
        self.LTincl_b = sb([128, 128], BF16, "LTincl")
        self.LT_f = sb([128, 128], F32, "LT_f")
        self.negmaskT = sb([128, 128], F32, "negmaskT")
        CK = ["consts"]
        S.op("pool", lambda e: e.memset(self.ones_f[:], 1.0), writes=CK)
        S.op("pool", lambda e: e.memset(self.negmaskT[:], 0.0), writes=CK)
        S.op("pool", lambda e: e.memset(self.ones_b[:], 1.0), writes=CK)

        def asel(out, in_, pattern, cm, cmp, fill):
            S.op("pool", lambda e: e.affine_select(out=out, in_=in_, pattern=pattern, compare_op=cmp, fill=fill,
                                                    base=0, channel_multiplier=cm), reads=CK, writes=CK)
        asel(self.ident_f[:], self.ones_f[:], [[-1, 128]], 1, ALU.is_equal, 0.0)
        asel(self.ident_b[:], self.ones_f[:], [[-1, 128]], 1, ALU.is_equal, 0.0)
        asel(self.LTincl_b[:], self.ones_f[:], [[-1, 128]], 1, ALU.is_ge, 0.0)
        asel(self.LT_f[:], self.ones_f[:], [[1, 128]], -1, ALU.is_ge, 0.0)
        asel(self.negmaskT[:], self.negmaskT[:], [[1, 128]], -1, ALU.is_ge, NEG)

        dpar = S.dsem()
        R = sb([128, 128], F32, "Rst")
        self.colsR1 = sb([128, 128], F32, "colsR1")
        self.convg = sb([128, 96], F32, "convg")
        self.convf = sb([128, 132], F32, "convf")

        def rows(p):
            return p.rearrange("o (r q) -> (o r) q", q=128)
        self.col_off = {}

        def finish(nrow, dst):
            ps, pk = S.psum()
            self.mm(ps[:, 0:nrow], R[0:nrow, :], self.ident_f[0:nrow, 0:nrow], True, True, ["Rst"] + CK, [pk])
            self.cp("dve", dst, ps[:, 0:nrow], [pk], CK)
        S.op("pool", lambda e: e.memset(R[:], 0.0), writes=["Rst"])
        r = 0
        for n, nr in (("norm_mix", 8), ("norm_x", 8), ("norm_ffn", 8), ("norm_mem", 8), ("gdn_out_norm", 1),
                      ("xq_norm", 2), ("xk_norm", 2)):
            S.dma("sp", R[r:r + nr, :], rows(P[n]), dpar, reads=["Rst"], writes=["Rst"])
            self.col_off[n] = r
            r += nr
        finish(128, self.colsR1[:])
        S.dma("sp", R[0:96, :], P["conv_gdn"][0].rearrange("i (c q) -> (i c) q", q=128), dpar, reads=["Rst"], writes=["Rst"])
        finish(96, self.convg[:, 0:96])
        S.dma("sp", R[0:88, :], P["conv_ffn"][0, 0:2, :].rearrange("i (c q) -> (i c) q", q=128), dpar, reads=["Rst"], writes=["Rst"])
        finish(88, self.convf[:, 0:88])
        S.dma("sp", R[0:44, :], P["conv_ffn"][0, 2:3, :].rearrange("i (c q) -> (i c) q", q=128), dpar, reads=["Rst"], writes=["Rst"])
        finish(44, self.convf[:, 88:132])
        self.dtb_row = sb([128, 8], F32, "dtb_row")
        self.nAexp_row = sb([128, 8], F32, "nAexp_row")
        S.dma("sp", self.dtb_row[:], P["dt_bias"][0].partition_broadcast(128), dpar, writes=["Rst2"])
        S.dma("sp", self.nAexp_row[:], P["a_log"][0].partition_broadcast(128), dpar, writes=["Rst2"])
        self.act(self.nAexp_row[:], self.nAexp_row[:], AF.Exp, ["Rst2"], CK)
        self.ts("dve", self.nAexp_row[:], self.nAexp_row[:], -1.0, None, ALU.mult, None, CK, CK)
        self.CK = CK

    def gcol(self, name, c):
        o = self.col_off[name] + c
        return self.colsR1[:, o:o + 1]

    def setup_tiles(self):
        sb = self.S.sb
        S = self.S
        self.xnT = sb([128, 8, SEQ], BF16, "xnT")
        self.oaT = sb([128, 8, SEQ], BF16, "oaT")
        self.obT = sb([128, 8, SEQ], BF16, "obT")
        self.h = self.xnT[:, 0:4, :].rearrange("p c t -> p (c t)").bitcast(F32).rearrange("p (a n) -> p a n", a=4)
        self.KT = self.xnT[:, 4, :].rearrange("p (c m) -> p c m", m=MEM)
        self.Vm = self.xnT[:, 5, :].rearrange("p (c m) -> p c m", m=1024)
        self.rawc = [self.xnT[:, 6 + i // 3, (i % 3) * 520:(i % 3) * 520 + 516] for i in range(6)]
        self.rawcrr = 0
        self.alias_keys = ([("xnT", g) for g in range(NG)] + [("h", t) for t in range(4)] + ["KT", "Vm"]
                           + [("rawc", i) for i in range(6)] + [("xt", 0)] + [(("raw", 1), g_) for g_ in range(NG)]
                           + [("xh", 0)] + [(("raw", 0), g_) for g_ in range(NG)])
        self.dummy = sb([128, 2], F32, "dummy")
        self.xt = [sb([128, 1024], F32, "xt") for _ in range(1)]
        self.xsem = [S.dsem() for _ in range(1)]
        self.xrr = 0
        self.stat = sb([128, 8], F32, "stat")
        self.xhrr = 0
        self.osem = [S.dsem() for _ in range(4)]
        self.dbgsem = [S.dsem() for _ in range(4)]
        self.hsem = [S.dsem() for _ in range(4)]
        self.FB = [sb([128, SEQ], BF16, "FB") for _ in range(7)]
        self.T5 = [sb([128, 512], F32, "T5") for _ in range(8)]
        self.t5rr = 0
        self.B5 = [sb([128, 512], BF16, "B5") for _ in range(8)]
        self.b5rr = 0
        self.G5 = [sb([128, 512], BF16, "G5") for _ in range(3)]
        self.H5 = [[sb([128, 512], BF16, "H5") for _ in range(5)] for _ in range(2)]
        self.carry = sb([128, 512], BF16, "carry")
        self.nsq = sb([128, 512], BF16, "nsq")
        self.gbt = [sb([128, 128], F32, "gbt") for _ in range(2)]
        self.gbrr = 0
        self.COLS = sb([128, NT, 48], F32, "COLS")
        self.Sf = sb([128, 128], F32, "Sf")
        self.Sb = sb([128, 128], BF16, "Sb")
        self.Vn = [sb([128, 128], BF16, "Vn") for _ in range(2)]
        self.sv = self.FB[5][:].rearrange("p (c t) -> p c t", t=128)
        self.raw = [sb([128, SEQ], BF16, "raw"), self.xt[0][:].bitcast(BF16)]
        self.xh = [self.raw[0][:, 0:1024]]
        self.wab = sb([128, 8, 16], BF16, "wab")
        self.wabsem = S.dsem()
        self.mnT = self.FB[6][:].rearrange("p (c m) -> p c m", m=MEM)
        self.halo = sb([128, 44, 2], BF16, "halo")

    def fence(self):
        self.S.op("pool", lambda e: e.memset(self.dummy[:], 0.0), writes=self.alias_keys)

    def actT_chunk(self, c):
        return self.FB[4 + c // 4][:, (c % 4) * 512:(c % 4 + 1) * 512], ("FB%d" % (4 + c // 4), c % 4)

    def t5(self):
        i = self.t5rr % len(self.T5)
        self.t5rr += 1
        return self.T5[i], ("T5", i)

    def b5(self):
        i = self.b5rr % len(self.B5)
        self.b5rr += 1
        return self.B5[i], ("B5", i)

    def norm_tile_to_T(self, src_ap, src_keys, gname, dstT, dst_col0, dst_keys, width=1024):
        S = self.S
        st = self.stat
        i = 0
        xh = self.xh[i]
        self.act(xh[:], src_ap, AF.Square, src_keys, [("xh", i), "stat"], accum_out=st[:, 0:1])
        self.act(st[:, 1:2], st[:, 0:1], AF.Ln, ["stat"], ["stat"], scale=1.0 / width, bias=EPS)
        self.act(st[:, 2:3], st[:, 1:2], AF.Exp, ["stat"], ["stat"], scale=-0.5)
        self.act(xh[:], src_ap, AF.Copy, src_keys + ["stat"], [("xh", i)], scale=st[:, 2:3])
        for half in range(2):
            ps, pk = S.psum()
            for c4 in range(4):
                c = half * 4 + c4
                self.mm(ps[:, c4 * 128:(c4 + 1) * 128], xh[:, c * 128:(c + 1) * 128], self.ident_b[:], True, True,
                        [("xh", i)] + self.CK, [pk])
            o = self.col_off[gname] + half * 4
            gb = self.colsR1[:, o:o + 4].unsqueeze(2).to_broadcast([128, 4, 128])
            self.tt("dve", dstT[:, half * 4:(half + 1) * 4, dst_col0:dst_col0 + 128],
                    ps[:].rearrange("p (c t) -> p c t", t=128), gb, ALU.mult, [pk] + self.CK, dst_keys)

    def load_x_tile(self, src):
        i = 0
        self.S.dma("sp", self.xt[i][:], src, self.xsem[i], writes=[("xt", i)])
        return self.xt[i], ("xt", i)

    def sigmoid(self, out, in_, tmp, in_keys, tmp_key, out_keys):
        self.act(tmp, in_, AF.Exp, in_keys, [tmp_key], scale=-1.0)
        self.act(tmp, tmp, AF.Ln, [tmp_key], [tmp_key], bias=1.0)
        self.act(out, tmp, AF.Exp, [tmp_key], out_keys, scale=-1.0)

    def stage_mem(self, b):
        S = self.S
        for mt in range(2):
            xt, xk = self.load_x_tile(self.mem[b, mt * 128:(mt + 1) * 128, :])
            self.norm_tile_to_T(xt[:], [xk], "norm_mem", self.mnT, mt * 128, [("FB6", g_) for g_ in range(NG)])
        wkv = self.WB["w_xkv"]
        st = self.stat
        for q in range(4):
            wi = self.wslot()
            self.wload(wi, wkv[:, q * 512:(q + 1) * 512], 8, 512)
            for mt in range(2):
                ps, pk = S.psum()
                for kc in range(8):
                    self.mm(ps[:], self.mnT[:, kc, mt * 128:(mt + 1) * 128], self.wslots[wi][:, kc, :], kc == 0, kc == 7,
                            [("FB6", g_) for g_ in range(NG)] + [("w", wi)], [pk])
                if q >= 2:
                    self.cp("act", self.Vm[:, mt, (q - 2) * 512:(q - 1) * 512], ps[:], [pk], ["Vm"])
                else:
                    kh, kk = self.b5()
                    for hh in range(2):
                        self.act(kh[:, hh * 256:(hh + 1) * 256], ps[:, hh * 256:(hh + 1) * 256], AF.Square, [pk], [kk, "stat"],
                                 accum_out=st[:, 4:5])
                        self.act(st[:, 5:6], st[:, 4:5], AF.Ln, ["stat"], ["stat"], scale=1.0 / 256, bias=EPS)
                        self.act(st[:, 6:7], st[:, 5:6], AF.Exp, ["stat"], ["stat"], scale=-0.5)
                        self.act(kh[:, hh * 256:(hh + 1) * 256], ps[:, hh * 256:(hh + 1) * 256], AF.Copy, [pk, "stat"], [kk],
                                 scale=st[:, 6:7])
                    ps2, pk2 = S.psum()
                    for j in range(4):
                        self.mm(ps2[:, j * 128:(j + 1) * 128], kh[:, j * 128:(j + 1) * 128], self.ident_b[:], True, True,
                                [kk] + self.CK, [pk2])
                    for j in range(4):
                        self.ts("dve", self.KT[:, q * 4 + j, mt * 128:(mt + 1) * 128], ps2[:, j * 128:(j + 1) * 128],
                                self.gcol("xk_norm", j % 2), None, ALU.mult, None, [pk2] + self.CK, ["KT"])

    def stage_A(self, b):
        for t in range(NT):
            xt, xk = self.load_x_tile(self.x[b, t * 128:(t + 1) * 128, :])
            self.norm_tile_to_T(xt[:], [xk], "norm_mix", self.xnT, t * 128, [("xnT", t // 4)])

    def stage_gdn_prep(self, b):
        S = self.S
        S.dma("sp", self.wab[:], self.WB["w_in"][:, C_A:C_A + 16].rearrange("(kc p) n -> p kc n", p=128), self.wabsem,
              reads=[("wcast", self.WB["w_in"].name)], writes=["wab"])
        for t in range(NT):
            C = self.COLS
            ck = ("COLS", t)
            ps, pk = S.psum()
            for kc in range(8):
                self.mm(ps[:, 0:16], self.xnT[:, kc, t * 128:(t + 1) * 128], self.wab[:, kc, :], kc == 0, kc == 7,
                        [("xnT", t // 4), "wab"], [pk])
            tmp, tk = self.t5()
            self.tt("dve", tmp[:, 0:8], ps[:, 0:8], self.dtb_row[:], ALU.add, [pk] + self.CK, [tk])
            self.act(tmp[:, 0:8], tmp[:, 0:8], AF.Exp, [tk], [tk])
            self.act(tmp[:, 0:8], tmp[:, 0:8], AF.Ln, [tk], [tk], bias=1.0)
            self.tt("dve", tmp[:, 0:8], tmp[:, 0:8], self.nAexp_row[:], ALU.mult, [tk] + self.CK, [tk])
            self.act(tmp[:, 8:16], ps[:, 8:16], AF.Exp, [pk], [tk], scale=-1.0)
            self.act(tmp[:, 8:16], tmp[:, 8:16], AF.Ln, [tk], [tk], bias=1.0)
            self.act(C[:, t, 32:40], tmp[:, 8:16], AF.Exp, [tk], [ck], scale=-1.0)
            self.ts("dve", C[:, t, 40:48], C[:, t, 32:40], -1.0, None, ALU.mult, None, [ck], [ck])
            ps2, pk2 = S.psum()
            self.mm(ps2[:, 0:8], self.LT_f[:], tmp[:, 0:8], True, True, [tk] + self.CK, [pk2])
            self.mm(ps2[:, 8:16], self.ones_f[:], tmp[:, 0:8], True, True, [tk] + self.CK, [pk2])
            self.cp("dve", tmp[:, 16:32], ps2[:, 0:16], [pk2], [tk])
            self.cp("act", C[:, t, 0:8], tmp[:, 16:24], [tk], [ck])
            self.act(C[:, t, 8:16], tmp[:, 16:24], AF.Exp, [tk], [ck])
            self.tt("dve", tmp[:, 32:40], tmp[:, 24:32], tmp[:, 16:24], ALU.subtract, [tk], [tk])
            self.act(C[:, t, 16:24], tmp[:, 32:40], AF.Exp, [tk], [ck])
            self.act(C[:, t, 24:32], tmp[:, 24:32], AF.Exp, [tk], [ck])

    def projT(self, wi, col, evac, src=None, srckey="xnT"):
        S = self.S
        src = self.xnT if src is None else src
        for g in range(NG):
            ps, pk = S.psum()
            for kc in range(8):
                self.mm(ps[:], self.wslots[wi][:, kc, col * 128:(col + 1) * 128], src[:, kc, g * 512:(g + 1) * 512],
                        kc == 0, kc == 7, [(srckey, g), ("w", wi)], [pk])
            evac(g, ps, pk)

    def projT_gen(self, wi, col, evac):
        S = self.S
        src = self.xnT
        for g in range(NG):
            ps, pk = S.psum()
            for kc in range(8):
                self.mm(ps[:], self.wslots[wi][:, kc, col * 128:(col + 1) * 128], src[:, kc, g * 512:(g + 1) * 512],
                        kc == 0, kc == 7, [("xnT", g), ("w", wi)], [pk])
            evac(g, ps, pk)
            yield

    def make_diag(self, cols, nk):
        t, tk = self.b5()
        dg = t[:].rearrange("p (i c) -> p i c", c=128)
        for j, c in enumerate(cols):
            self.ts("pool", dg[:, j, :], self.ident_f[:], c, None, ALU.mult, None, self.CK, [tk])
        return dg, tk

    def gdn_front_gen(self, b, h):
        S = self.S
        FB = self.FB
        Win = self.WB["w_in"]
        wi = self.wslot()
        for j, c0 in enumerate((C_GQ, C_GK, C_GV, C_Z)):
            self.wload(wi, Win[:, c0 + h * 128:c0 + (h + 1) * 128], 8, 128, col_off=j * 128)
        qT, kT, vT = FB[0], FB[1], FB[2]
        ysq = FB[5]

        def k4(name):
            return [(name, g) for g in range(NG)]

        plan = ((qT, "FB0"), (kT, "FB1"), (vT, "FB2"))

        def proj(j):
            raw = self.raw[j % 2]
            rk = ("raw", j % 2)

            def ev(g, ps, pk):
                self.cp("act", raw[:, g * 512:(g + 1) * 512], ps[:], [pk], [(rk, g)])
            yield from self.projT_gen(wi, j, ev)

        def convnorm(j):
            dst, dname = plan[j]
            raw = self.raw[j % 2]
            rk = ("raw", j % 2)
            ch = j * 8 + h
            dg, dk = self.make_diag([self.convg[:, i * 24 + ch:i * 24 + ch + 1] for i in range(4)], 4)
            for g in range(NG):
                ps, pk = S.psum()
                if g == 0:
                    self.mm(ps[:], dg[:, 3, :], raw[:, 0:512], True, False, [dk, (rk, 0)], [pk])
                    for i in (2, 1, 0):
                        off = 3 - i
                        self.mm(ps[:, off:512], dg[:, i, :], raw[:, 0:512 - off], False, i == 0, [dk, (rk, 0)], [pk])
                else:
                    for i in range(4):
                        self.mm(ps[:], dg[:, i, :], raw[:, g * 512 - 3 + i:g * 512 - 3 + i + 512], i == 0, i == 3,
                                [dk, (rk, g), (rk, g - 1)], [pk])
                sg, sk_ = self.t5()
                self.sigmoid(sg[:], ps[:], sg[:], [pk], sk_, [sk_])
                self.tt("dve", dst[:, g * 512:(g + 1) * 512], ps[:], sg[:], ALU.mult, [pk, sk_], [(dname, g)])
                yield
            if j < 2:
                self.act(ysq[:], dst[:], AF.Square, k4(dname), k4("FB5"))
                for g in range(NG):
                    ps, pk = S.psum()
                    self.mm(ps[:], self.ones_b[:], ysq[:, g * 512:(g + 1) * 512], True, True, [("FB5", g)] + self.CK, [pk])
                    tmp, tk = self.t5()
                    self.act(tmp[:], ps[:], AF.Ln, [pk], [tk], bias=EPS)
                    bias = -0.5 * math.log(128.0) if j == 0 else 0.0
                    self.act(tmp[:], tmp[:], AF.Exp, [tk], [tk], scale=-0.5, bias=bias)
                    self.tt("dve", dst[:, g * 512:(g + 1) * 512], dst[:, g * 512:(g + 1) * 512], tmp[:], ALU.mult,
                            [(dname, g), tk], [(dname, g)])
                    yield
        yield from proj(0)
        yield from proj(1)
        yield from convnorm(0)
        yield from proj(2)
        yield from convnorm(1)
        yield from convnorm(2)
        self._gdn_wi = wi

    def gdn_loop_gen(self, b, h):
        S = self.S
        FB = self.FB
        qT, kT, vT, OT = FB[0], FB[1], FB[2], FB[6]
        C = self.COLS
        def rowbc(g, fld):
            ps, pk = S.psum("neu")
            for tt_ in range(4):
                t = g * 4 + tt_
                gi = self.gbrr % 2
                self.gbrr += 1
                gb = self.gbt[gi]
                self.ts("pool", gb[:], self.ones_f[:], C[:, t, fld * 8 + h:fld * 8 + h + 1], None, ALU.mult, None,
                        [("COLS", t)] + self.CK, [("gbt", gi)])
                self.mm(ps[:, tt_ * 128:(tt_ + 1) * 128], gb[:], self.ident_f[:], True, True, [("gbt", gi)] + self.CK, [pk])
            return ps, pk
        S.op("pool", lambda e: e.memset(self.Sf[:], 0.0), writes=["Sf"])
        S.op("pool", lambda e: e.memset(self.Sb[:], 0.0), writes=["Sb"])
        C = self.COLS
        sl = lambda cc: slice(cc * 128, (cc + 1) * 128)
        tok = lambda c: slice(c * 128, (c + 1) * 128)
        v3 = lambda ap: ap.rearrange("p (c t) -> p c t", t=128)
        hand = {}

        def neu(bt):
            gk = bt
            par = bt % 2
            cs = [bt * 4 + cc for cc in range(4)]

            def colb(f):
                return C[:, bt * 4:(bt + 1) * 4, f * 8 + h:f * 8 + h + 1].to_broadcast([128, 4, 128])
            ckeys = [("COLS", c) for c in cs]
            pk_, pkk_ = S.psum("neu")
            for cc, c in enumerate(cs):
                self.mm(pk_[:, sl(cc)], kT[:, tok(c)], self.ident_b[:], True, True, [("FB1", gk)] + self.CK, [pkk_])
            Kg, Kgk = self.G5[0], 'G50'
            Kd, Kdk = self.H5[par][0], ('H5', par, 0)
            self.tt("dve", v3(Kg[:]), v3(pk_[:]), colb(1), ALU.mult, [pkk_] + ckeys, [Kgk])
            self.tt("dve", v3(Kd[:]), v3(pk_[:]), colb(2), ALU.mult, [pkk_] + ckeys, [Kdk])
            pv_, pvk_ = S.psum("neu")
            for cc, c in enumerate(cs):
                self.mm(pv_[:, sl(cc)], vT[:, tok(c)], self.ident_b[:], True, True, [("FB2", gk)] + self.CK, [pvk_])
            Vt, Vtk = self.G5[2], 'G52'
            self.cp("act", Vt[:], pv_[:], [pvk_], [Vtk])
            yield
            peg, pegk = rowbc(bt, 1)
            QdT, QdTk = self.H5[par][3], ('H5', par, 3)
            self.tt("dve", QdT[:], qT[:, bt * 512:(bt + 1) * 512], peg[:], ALU.mult, [pegk, ("FB0", bt)], [QdTk])
            yield
            E, Ek = self.t5()
            pgc, pgck = rowbc(bt, 0)
            for cc, c in enumerate(cs):
                self.stt(E[:, sl(cc)], pgc[:, sl(cc)], C[:, c, h:h + 1], self.negmaskT[:], ALU.subtract, ALU.add,
                         [pgck, ("COLS", c)] + self.CK, [Ek])
            self.act(E[:], E[:], AF.Exp, [Ek], [Ek])
            yield
            QKm, QKmk = self.H5[par][1], ('H5', par, 1)
            pqk, pqkk = S.psum("neu")
            for cc, c in enumerate(cs):
                self.mm(pqk[:, sl(cc)], kT[:, tok(c)], qT[:, tok(c)], True, True, [("FB1", gk), ("FB0", gk)], [pqkk])
            self.tt("dve", QKm[:], pqk[:], E[:], ALU.mult, [pqkk, Ek], [QKmk])
            Es, Esk = self.t5()
            S.op("pool", lambda e: e.affine_select(out=v3(Es[:]), in_=v3(E[:]), pattern=[[0, 4], [1, 128]],
                                                    compare_op=ALU.is_gt, fill=0.0, base=0, channel_multiplier=-1),
                 reads=[Ek], writes=[Esk])
            yield
            X, Xk = self.t5()
            pkk, pkkk = S.psum("neu")
            for cc, c in enumerate(cs):
                self.mm(pkk[:, sl(cc)], kT[:, tok(c)], kT[:, tok(c)], True, True, [("FB1", gk)], [pkkk])
            for cc, c in enumerate(cs):
                self.stt(X[:, sl(cc)], pkk[:, sl(cc)], C[:, c, 32 + h:32 + h + 1], Es[:, sl(cc)], ALU.mult, ALU.mult,
                         [pkkk, Esk, ("COLS", c)], [Xk])
            yield
            pt, ptk = S.psum("neu")
            for cc in range(4):
                self.mm(pt[:, sl(cc)], X[:, sl(cc)], self.ident_f[:], True, True, [Xk] + self.CK, [ptk])
            XT, XTk = self.t5()
            self.cp("act", XT[:], pt[:], [ptk], [XTk])
            yield
            Y, Yk = self.t5()
            self.tt("dve", v3(Y[:]), self.ident_f[:].unsqueeze(1).to_broadcast([128, 4, 128]), v3(X[:]), ALU.subtract,
                    [Xk] + self.CK, [Yk])
            for p in (2, 4, 8, 16, 32, 64):
                pb, pbk = S.psum("neu")
                for cc in range(4):
                    self.mm(pb[:, sl(cc)], X[:, sl(cc)], XT[:, sl(cc)], True, True, [Xk, XTk], [pbk])
                XTn, XTnk = self.t5()
                if p < 64:
                    pa, pak = S.psum("neu")
                    for cc in range(4):
                        self.mm(pa[:, sl(cc)], XT[:, sl(cc)], X[:, sl(cc)], True, True, [Xk, XTk], [pak])
                    Xn, Xnk = self.t5()
                    self.cp("act", Xn[:], pa[:], [pak], [Xnk])
                self.cp("dve", XTn[:], pb[:], [pbk], [XTnk])
                if p < 64:
                    X, Xk = Xn, Xnk
                XT, XTk = XTn, XTnk
                yield
                py, pyk = S.psum("neu")
                for cc in range(4):
                    self.mm(py[:, sl(cc)], XT[:, sl(cc)], Y[:, sl(cc)], True, True, [XTk, Yk], [pyk])
                Yn, Ynk = self.t5()
                self.tt("dve", Yn[:], py[:], Y[:], ALU.add, [pyk, Yk], [Ynk])
                Y, Yk = Yn, Ynk
                yield
            Yb, Ybk = self.b5()
            self.cp("act", Yb[:], Y[:], [Yk], [Ybk])
            Y, Yk = Yb, Ybk
            yield
            pu, puk = S.psum("neu")
            for cc in range(4):
                self.mm(pu[:, sl(cc)], Y[:, sl(cc)], Vt[:, sl(cc)], True, True, [Yk, Vtk], [puk])
            Upp, Uppk = self.H5[par][4], ('H5', par, 4)
            self.tt("dve", v3(Upp[:]), v3(pu[:]), colb(4), ALU.mult, [puk] + ckeys, [Uppk])
            pw, pwk = S.psum("neu")
            for cc in range(4):
                self.mm(pw[:, sl(cc)], Kg[:, sl(cc)], Y[:, sl(cc)], True, True, [Kgk, Yk], [pwk])
            WT, WTk = self.H5[par][2], ('H5', par, 2)
            self.cp("act", WT[:], pw[:], [pwk], [WTk])
            hand[bt] = (Kd, Kdk, QdT, QdTk, QKm, QKmk, Upp, Uppk, WT, WTk)
            yield

        def rec(bt):
            cs = [bt * 4 + cc for cc in range(4)]
            Kd, Kdk, QdT, QdTk, QKm, QKmk, Upp, Uppk, WT, WTk = hand[bt]
            po, pok = S.psum_fixed(6)
            for cc, c in enumerate(cs):
                pws, pwsk = S.psum("rec")
                self.mm(pws[:, 0:128], WT[:, sl(cc)], self.Sb[:], True, True, [WTk, "Sb"], [pwsk])
                Vn = self.Vn[c % 2]
                Vnk = ("Vn", c % 2)
                self.stt(Vn[:], pws[:, 0:128], C[:, c, 40 + h:40 + h + 1], Upp[:, sl(cc)], ALU.mult, ALU.add,
                         [pwsk, Uppk, ("COLS", c)], [Vnk])
                yield
                self.mm(po[:, sl(cc)], self.Sb[:], QdT[:, sl(cc)], True, False, ["Sb", QdTk], [pok])
                self.mm(po[:, sl(cc)], Vn[:], QKm[:, sl(cc)], False, True, [Vnk, QKmk], [pok])
                psn, psnk = S.psum("rec")
                self.mm(psn[:, 0:128], Kd[:, sl(cc)], Vn[:], True, True, [Kdk, Vnk], [psnk])
                self.stt(self.Sf[:], self.Sf[:], C[:, c, 24 + h:24 + h + 1], psn[:, 0:128], ALU.mult, ALU.add,
                         [psnk, "Sf", ("COLS", c)], ["Sf"])
                self.cp("act", self.Sb[:], self.Sf[:], ["Sf"], ["Sb"])
                yield
            self.cp("act", OT[:, bt * 512:(bt + 1) * 512], po[:], [pok], [("FB6", bt)])
            yield

        def inter(gs):
            gs = list(gs)
            while gs:
                for gn in list(gs):
                    try:
                        next(gn)
                        yield
                    except StopIteration:
                        gs.remove(gn)
        for bt in range(NG):
            gs = [neu(bt)] + ([rec(bt - 1)] if bt > 0 else [])
            yield from inter(gs)
        yield from inter([rec(NG - 1)])

    def gdn_tail_gen(self, b, h):
        S = self.S
        FB = self.FB
        OT, zs, ysq = FB[6], FB[3], FB[4]
        wi = self._gdn_wi

        def k4(name):
            return [(name, g) for g in range(NG)]

        def evz(g, ps, pk):
            sg, sk_ = self.t5()
            self.sigmoid(sg[:], ps[:], sg[:], [pk], sk_, [sk_])
            self.tt("dve", zs[:, g * 512:(g + 1) * 512], ps[:], sg[:], ALU.mult, [pk, sk_], [("FB3", g)])
        yield from self.projT_gen(wi, 3, evz)
        self.act(ysq[:], OT[:], AF.Square, k4("FB6"), k4("FB4"))
        gn = self.gcol("gdn_out_norm", 0)
        for g in range(NG):
            ps, pk = S.psum()
            self.mm(ps[:], self.ones_b[:], ysq[:, g * 512:(g + 1) * 512], True, True, [("FB4", g)] + self.CK, [pk])
            tmp, tk = self.t5()
            self.act(tmp[:], ps[:], AF.Ln, [pk], [tk], scale=1.0 / 128, bias=EPS)
            self.act(tmp[:], tmp[:], AF.Exp, [tk], [tk], scale=-0.5)
            t2, t2k = self.t5()
            self.stt(t2[:], OT[:, g * 512:(g + 1) * 512], gn, tmp[:], ALU.mult, ALU.mult, [("FB6", g), tk] + self.CK, [t2k])
            self.tt("dve", self.oaT[:, h, g * 512:(g + 1) * 512], t2[:], zs[:, g * 512:(g + 1) * 512], ALU.mult,
                    [t2k, ("FB3", g)], [("oaT", g)])
            yield

    def sb_front(self, b, h):
        S = self.S
        FB = self.FB
        Win = self.WB["w_in"]
        wi = self.wslot()
        for j, c0 in enumerate((C_SQ, C_SK, C_SV)):
            self.wload(wi, Win[:, c0 + h * 128:c0 + (h + 1) * 128], 8, 128, col_off=j * 128)
        sqT, skT = FB[3], FB[4]
        scl = 128.0 ** -0.5

        def evq(g, ps, pk):
            self.act(sqT[:, g * 512:(g + 1) * 512], ps[:], AF.Copy, [pk], [("FB3", g)], scale=scl)
        self.projT(wi, 0, evq)

        def evk(g, ps, pk):
            self.cp("act", skT[:, g * 512:(g + 1) * 512], ps[:], [pk], [("FB4", g)])
        self.projT(wi, 1, evk)
        for t4 in range(4):
            ps, pk = S.psum()
            for tt_ in range(4):
                t = t4 * 4 + tt_
                for kc in range(8):
                    self.mm(ps[:, tt_ * 128:(tt_ + 1) * 128], self.xnT[:, kc, t * 128:(t + 1) * 128],
                            self.wslots[wi][:, kc, 256:384], kc == 0, kc == 7, [("xnT", t4), ("w", wi)], [pk])
            self.cp("dve", self.sv[:, t4 * 4:(t4 + 1) * 4, :].rearrange("p c t -> p (c t)"), ps[:], [pk], [("FB5", t4)])

    def sb_loop_gen(self, b, h):
        S = self.S
        FB = self.FB
        sqT, skT = FB[3], FB[4]
        for qg in range(NG):
            acc, acck = S.psum_fixed(7)
            carry, ck = self.carry, 'carry'
            nsq, nsqk = self.nsq, 'nsq'
            self.act(nsq[:], sqT[:, qg * 512:(qg + 1) * 512], AF.Copy, [("FB3", qg)], [nsqk], scale=-1.0)
            S.op("pool", lambda e: e.memset(carry[:], 0.0), writes=[ck])
            blocks = list(range(4 * qg + 3, -1, -1))
            st1 = {}

            def stage1(kb):
                r = kb - 4 * qg
                c0 = max(r, 0) * 128
                pz, pzk = S.psum("sb")
                self.mm(pz[:, c0:512], skT[:, kb * 128:(kb + 1) * 128], sqT[:, qg * 512 + c0:(qg + 1) * 512], True, True,
                        [("FB4", kb // 4), ("FB3", qg)], [pzk])
                e, ek = self.b5()
                self.act(e[:, c0:512], pz[:, c0:512], AF.Exp, [pzk], [ek])
                if r >= 0:
                    S.op("pool", lambda en: en.affine_select(out=e[:, c0:c0 + 128], in_=e[:, c0:c0 + 128], pattern=[[1, 128]],
                                                              compare_op=ALU.is_gt, fill=0.0, base=0, channel_multiplier=-1),
                         reads=[ek], writes=[ek])
                sp, spk = self.b5()
                self.act(sp[:, c0:512], e[:, c0:512], AF.Ln, [ek], [spk], bias=1.0)
                st1[kb] = (r, c0, sp, spk)

            st2 = {}

            def stage2a(kb, first):
                r, c0, sp, spk = st1.pop(kb)
                pc, pck = S.psum("sb")
                self.mm(pc[:, c0:512], self.LTincl_b[:], sp[:, c0:512], True, False, [spk] + self.CK, [pck])
                if not first:
                    self.mm(pc[:, c0:512], self.ones_b[:], carry[:, c0:512], False, False, [ck] + self.CK, [pck])
                self.mm(pc[:, c0:512], skT[:, kb * 128:(kb + 1) * 128], nsq[:, c0:512], False, True,
                        [("FB4", kb // 4), nsqk], [pck])
                w, wk = self.b5()
                self.act(w[:, c0:512], pc[:, c0:512], AF.Exp, [pck], [wk], scale=-1.0)
                if r >= 0:
                    S.op("pool", lambda en: en.affine_select(out=w[:, c0:c0 + 128], in_=w[:, c0:c0 + 128], pattern=[[1, 128]],
                                                              compare_op=ALU.is_gt, fill=0.0, base=0, channel_multiplier=-1),
                         reads=[wk], writes=[wk])
                st2[kb] = (r, c0, sp, spk, w, wk)

            def stage2b(kb, first, last):
                r, c0, sp, spk, w, wk = st2.pop(kb)
                self.mm(acc[:, c0:512], self.sv[:, kb, :], w[:, c0:512], first, last, [("FB5", kb // 4), wk], [acck])
                if not last:
                    self.tt("dve", carry[:, c0:512], carry[:, c0:512], sp[:, c0:512], ALU.add, [ck, spk], [ck])

            stage1(blocks[0])
            for n, kb in enumerate(blocks):
                if n + 1 < len(blocks):
                    stage1(blocks[n + 1])
                    yield
                stage2a(kb, n == 0)
                yield
                stage2b(kb, n == 0, n == len(blocks) - 1)
                yield
            self.cp("dve", self.obT[:, h, qg * 512:(qg + 1) * 512], acc[:], [acck], [("obT", qg)])

    def run_pipelined(self, gens, depth=2):
        active = []
        it = iter(gens)
        done = False
        while True:
            while not done and len(active) < depth:
                try:
                    active.append(next(it))
                except StopIteration:
                    done = True
            if not active:
                break
            for gn in list(active):
                try:
                    next(gn)
                except StopIteration:
                    active.remove(gn)

    def ffn_chunk_gen(self, g, c_lo, q0, nq, f, hnT, hkeys):
        S = self.S
        Wu = self.WB["w_up"]
        if f == 0:
            self._ffn_wis = []
            for which in range(2):
                wi = self.wslot()
                col = which * DFF + (c_lo + q0) * 128
                self.wload(wi, Wu[:, col:col + nq * 128], 8, nq * 128)
                self._ffn_wis.append(wi)
        wis = list(self._ffn_wis)
        c = c_lo + q0 + f
        raws = []
        for which in range(2):
            wi = wis[which]
            ch = which * 22 + c
            ps, pk = S.psum()
            for kc in range(8):
                self.mm(ps[:], self.wslots[wi][:, kc, f * 128:(f + 1) * 128], hnT[kc // 4][:, kc % 4, :],
                        kc == 0, kc == 7, hkeys + [("w", wi)], [pk])
            ri = self.rawcrr % 6
            self.rawcrr += 1
            rc = self.rawc[ri]
            rck = ("rawc", ri)
            hk = ("halo", ch)
            self.cp("dve", rc[:, 2:4], self.halo[:, ch, :], [hk], [rck])
            self.cp("act", rc[:, 4:516], ps[:], [pk], [rck])
            self.cp("dve", self.halo[:, ch, :], rc[:, 514:516], [rck], [hk])
            dg, dk = self.make_diag([self.convf[:, i * 44 + ch:i * 44 + ch + 1] for i in range(3)], 3)
            raws.append((rc, rck, dg, dk))
        yield
        cps = []
        for which in range(2):
            rc, rck, dg, dk = raws[which]
            pc, pck = S.psum()
            for i in range(3):
                self.mm(pc[:], dg[:, i, :], rc[:, 2 + i:2 + i + 512], i == 0, i == 2, [dk, rck], [pck])
            cps.append((pc, pck))
        sg, sgk = self.t5()
        self.sigmoid(sg[:], cps[0][0][:], sg[:], [cps[0][1]], sgk, [sgk])
        t2, t2k = self.t5()
        self.tt("dve", t2[:], cps[0][0][:], sg[:], ALU.mult, [cps[0][1], sgk], [t2k])
        a_ap, a_key = self.actT_chunk(q0 + f)
        self.tt("dve", a_ap, cps[1][0][:], t2[:], ALU.mult, [cps[1][1], t2k], [a_key])
        yield

    def down_proj_add(self, nkc, k_base):
        S = self.S
        W = self.WB["w_down"]
        for half in range(2):
            slots = []
            k = 0
            while k < nkc:
                nk = min(8, nkc - k)
                wi = self.wslot()
                self.wload(wi, W[(k_base + k) * 128:(k_base + k + nk) * 128, half * 512:(half + 1) * 512], nk, 512)
                slots.append((wi, k, nk))
                k += nk
            for tt_ in range(4):
                ps, pk = S.psum()
                first = True
                for (wi, k0, nk) in slots:
                    for kk in range(nk):
                        kc = k0 + kk
                        a_ap, a_key = self.actT_chunk(kc)
                        self.mm(ps[:], a_ap[:, tt_ * 128:(tt_ + 1) * 128], self.wslots[wi][:, kk, :], first,
                                kc == nkc - 1, [a_key, ("w", wi)], [pk])
                        first = False
                self.tt("dve", self.h[:, tt_, half * 512:(half + 1) * 512], ps[:], self.h[:, tt_, half * 512:(half + 1) * 512],
                        ALU.add, [pk, ("h", tt_)], [("h", tt_)])

    def stage_C(self, b, g):
        S = self.S
        FB = self.FB
        gs = slice(g * 512, (g + 1) * 512)
        v8 = lambda t: t[:].rearrange("p (c t) -> p c t", t=512)
        for tt_ in range(4):
            S.dma("sp", self.h[:, tt_, :], self.x[b, g * 512 + tt_ * 128:g * 512 + (tt_ + 1) * 128, :], self.hsem[tt_],
                  writes=[("h", tt_)])
        xnG = [v8(FB[4]), v8(FB[5])]
        xkeys = [("FB4", f) for f in range(4)] + [("FB5", f) for f in range(4)]
        self.norm_group(xnG, xkeys, "norm_mix")
        mT = [v8(FB[0]), v8(FB[1])]
        sig = v8(FB[2])
        m1 = v8(FB[3])
        Win = self.WB["w_in"]
        for quad in range(2):
            cs = slice(quad * 512, (quad + 1) * 512)
            for (gate_c0, pw, srcT, skey) in ((C_GA, "w_proj_gdn", self.oaT, "oaT"), (C_GB, "w_proj_sb", self.obT, "obT")):
                wi = self.wslot()
                self.wload(wi, Win[:, gate_c0 + quad * 512:gate_c0 + (quad + 1) * 512], 8, 512)
                for f in range(4):
                    ps, pk = S.psum()
                    for kc in range(8):
                        self.mm(ps[:], self.wslots[wi][:, kc, f * 128:(f + 1) * 128], xnG[kc // 4][:, kc % 4, :], kc == 0, kc == 7,
                                xkeys + [("w", wi)], [pk])
                    tmp, tk = self.t5()
                    self.sigmoid(sig[:, f, :], ps[:], tmp[:], [pk], tk, [("FB2", f)])
                wi = self.wslot()
                self.wload(wi, self.WB[pw][:, cs], 8, 512)
                for f in range(4):
                    ps, pk = S.psum()
                    for kc in range(8):
                        self.mm(ps[:], self.wslots[wi][:, kc, f * 128:(f + 1) * 128], srcT[:, kc, gs], kc == 0, kc == 7,
                                [(skey, g), ("w", wi)], [pk])
                    if gate_c0 == C_GA:
                        self.tt("dve", m1[:, f, :], ps[:], sig[:, f, :], ALU.mult, [pk, ("FB2", f)], [("FB3", f)])
                    else:
                        tmp, tk = self.t5()
                        self.tt("dve", tmp[:], ps[:], sig[:, f, :], ALU.mult, [pk, ("FB2", f)], [tk])
                        self.tt("dve", mT[quad][:, f, :], tmp[:], m1[:, f, :], ALU.add, [tk, ("FB3", f)], [("FB%d" % quad, f)])
        mTfull_keys = [("FB0", f) for f in range(4)] + [("FB1", f) for f in range(4)]

        class _TwoBuf:
            def __init__(s, a, b_):
                s.a, s.b = a, b_

            def __getitem__(s, idx):
                p, kc, t = idx
                return (s.a if kc < 4 else s.b)[p, kc % 4, t]
        hnT = [v8(FB[2]), v8(FB[3])]
        hkeys = [("FB2", f) for f in range(4)] + [("FB3", f) for f in range(4)]
        self.tokproj_norm(_TwoBuf(mT[0], mT[1]), mTfull_keys, "w_out", 8, hnT, hkeys, "norm_x")
        if self.dbg and b == 0:
            for tt_ in range(4):
                S.dma("pool", self.dbg_h1[g * 512 + tt_ * 128:g * 512 + (tt_ + 1) * 128, :], self.h[:, tt_, :], self.dbgsem[tt_],
                      reads=[("h", tt_)], writes=["dbgo"])
        qraw = [v8(FB[0]), v8(FB[1])]
        qsq = [v8(FB[4]), v8(FB[5])]
        Wq = self.WB["w_xq"]
        for quad in range(2):
            wi = self.wslot()
            self.wload(wi, Wq[:, quad * 512:(quad + 1) * 512], 8, 512)
            for f in range(4):
                ps, pk = S.psum()
                for kc in range(8):
                    self.mm(ps[:], self.wslots[wi][:, kc, f * 128:(f + 1) * 128], hnT[kc // 4][:, kc % 4, :], kc == 0, kc == 7,
                            hkeys + [("w", wi)], [pk])
                self.cp("act", qraw[quad][:, f, :], ps[:], [pk], [("FB%d" % quad, f)])
                self.act(qsq[quad][:, f, :], ps[:], AF.Square, [pk], [("FB%d" % (4 + quad), f)])
        oxT = [v8(FB[2]), v8(FB[3])]
        def xa_head(hx):
            quad, f0 = hx // 2, (hx % 2) * 2
            stream = 'xa%d' % (hx % 2)
            ps, pk = S.psum(stream)
            for dc in range(2):
                self.mm(ps[:], self.ones_b[:], qsq[quad][:, f0 + dc, :], dc == 0, dc == 1,
                        [("FB%d" % (4 + quad), f0 + dc)] + self.CK, [pk])
            rs, rsk = self.t5()
            self.act(rs[:], ps[:], AF.Ln, [pk], [rsk], scale=1.0 / 256, bias=EPS)
            self.act(rs[:], rs[:], AF.Exp, [rsk], [rsk], scale=-0.5, bias=-0.5 * math.log(16.0) * 2)
            yield
            qn = []
            for dc in range(2):
                qb, qbk = self.b5()
                self.stt(qb[:], qraw[quad][:, f0 + dc, :], self.gcol("xq_norm", dc), rs[:], ALU.mult, ALU.mult,
                         [("FB%d" % quad, f0 + dc), rsk] + self.CK, [qbk])
                qn.append((qb, qbk))
            yield
            PT = []
            for mt in range(2):
                pz, pzk = S.psum(stream)
                for dc in range(2):
                    self.mm(pz[:], self.KT[:, hx * 2 + dc, mt * 128:(mt + 1) * 128], qn[dc][0][:], dc == 0, dc == 1,
                            ["KT", qn[dc][1]], [pzk])
                pt_, ptk = self.b5()
                self.act(pt_[:], pz[:], AF.Exp, [pzk], [ptk])
                PT.append((pt_, ptk))
            yield
            pd, pdk = S.psum(stream)
            for mt in range(2):
                self.mm(pd[:], self.ones_b[:], PT[mt][0][:], mt == 0, mt == 1, [PT[mt][1]] + self.CK, [pdk])
            rd, rdk = self.t5()
            S.op("dve", lambda e: e.reciprocal(out=rd[:], in_=pd[:]), reads=[pdk], writes=[rdk])
            yield
            for dc in range(2):
                po_, pok_ = S.psum(stream)
                for mt in range(2):
                    self.mm(po_[:], self.Vm[:, mt, hx * 256 + dc * 128:hx * 256 + (dc + 1) * 128], PT[mt][0][:], mt == 0, mt == 1,
                            ["Vm", PT[mt][1]], [pok_])
                self.tt("dve", oxT[quad][:, f0 + dc, :], po_[:], rd[:], ALU.mult, [pok_, rdk], [("FB%d" % (2 + quad), f0 + dc)])
        self.run_pipelined([xa_head(hx) for hx in range(4)], 2)
        hnT2 = [v8(FB[0]), v8(FB[1])]
        hkeys2 = [("FB0", f) for f in range(4)] + [("FB1", f) for f in range(4)]
        self.tokproj_norm(_TwoBuf(oxT[0], oxT[1]), hkeys, "w_xo", 8, hnT2, hkeys2, "norm_ffn")
        if self.dbg and b == 0:
            for tt_ in range(4):
                S.dma("pool", self.dbg_h2[g * 512 + tt_ * 128:g * 512 + (tt_ + 1) * 128, :], self.h[:, tt_, :], self.dbgsem[tt_],
                      reads=[("h", tt_)], writes=["dbgo"])
        hnT, hkeys = hnT2, hkeys2
        Wu = self.WB["w_up"]
        if g == 0:
            S.op("pool", lambda e: e.memset(self.halo[:], 0.0), writes=[("halo", c_) for c_ in range(44)])
        S.n_rr = 8
        for half in range(2):
            c_lo = half * 11
            gens = []
            for q0 in range(0, 11, 4):
                nq = min(4, 11 - q0)
                for f in range(nq):
                    gens.append(self.ffn_chunk_gen(g, c_lo, q0, nq, f, hnT, hkeys))
            self.run_pipelined(gens, 3)
            self.down_proj_add(11, c_lo)
        for tt_ in range(4):
            S.dma("pool", self.out[b, g * 512 + tt_ * 128:g * 512 + (tt_ + 1) * 128, :], self.h[:, tt_, :], self.osem[tt_],
                  reads=[("h", tt_)], writes=[("out", tt_)])

    def _tokproj_multi(self, srcT, srckeys, wname, nkc):
        S = self.S
        W = self.WB[wname]
        for half in range(2):
            wi = self.wslot()
            self.wload(wi, W[:, half * 512:(half + 1) * 512], 8, 512)
            for tt_ in range(4):
                ps, pk = S.psum()
                for kc in range(nkc):
                    self.mm(ps[:], srcT[:, kc, tt_ * 128:(tt_ + 1) * 128], self.wslots[wi][:, kc, :], kc == 0, kc == nkc - 1,
                            srckeys + [("w", wi)], [pk])
                self.tt("dve", self.h[:, tt_, half * 512:(half + 1) * 512], ps[:], self.h[:, tt_, half * 512:(half + 1) * 512],
                        ALU.add, [pk, ("h", tt_)], [("h", tt_)])

    def norm_group(self, dstT2, dkeys, gname):
        for tt_ in range(4):
            self.norm_tile(tt_, dstT2, dkeys, gname)

    def norm_tile(self, tt_, dstT2, dkeys, gname):
        S = self.S
        st = self.stat
        src = self.h[:, tt_, :]
        sk = [("h", tt_)]
        i = 0
        xh = self.xh[i]
        self.act(xh[:], src, AF.Square, sk, [("xh", i), "stat"], accum_out=st[:, 0:1])
        self.act(st[:, 1:2], st[:, 0:1], AF.Ln, ["stat"], ["stat"], scale=1.0 / 1024, bias=EPS)
        self.act(st[:, 2:3], st[:, 1:2], AF.Exp, ["stat"], ["stat"], scale=-0.5)
        self.act(xh[:], src, AF.Copy, sk + ["stat"], [("xh", i)], scale=st[:, 2:3])
        for half in range(2):
            ps, pk = S.psum()
            for c4 in range(4):
                c = half * 4 + c4
                self.mm(ps[:, c4 * 128:(c4 + 1) * 128], xh[:, c * 128:(c + 1) * 128], self.ident_b[:], True, True,
                        [("xh", i)] + self.CK, [pk])
            o = self.col_off[gname] + half * 4
            gb = self.colsR1[:, o:o + 4].unsqueeze(2).to_broadcast([128, 4, 128])
            self.tt("dve", dstT2[half][:, :, tt_ * 128:(tt_ + 1) * 128], ps[:].rearrange("p (c t) -> p c t", t=128), gb,
                    ALU.mult, [pk] + self.CK, dkeys[half * 4:(half + 1) * 4])

    def tokproj_norm(self, srcT, srckeys, wname, nkc, dstT2, dkeys, gname):
        S = self.S
        W = self.WB[wname]
        wis = []
        for half in range(2):
            wi = self.wslot()
            self.wload(wi, W[:, half * 512:(half + 1) * 512], 8, 512)
            wis.append(wi)

        def proj(tt_):
            for half in range(2):
                wi = wis[half]
                ps, pk = S.psum()
                for kc in range(nkc):
                    self.mm(ps[:], srcT[:, kc, tt_ * 128:(tt_ + 1) * 128], self.wslots[wi][:, kc, :], kc == 0, kc == nkc - 1,
                            srckeys + [("w", wi)], [pk])
                self.tt("dve", self.h[:, tt_, half * 512:(half + 1) * 512], ps[:], self.h[:, tt_, half * 512:(half + 1) * 512],
                        ALU.add, [pk, ("h", tt_)], [("h", tt_)])
        proj(0)
        proj(1)
        self.norm_tile(0, dstT2, dkeys, gname)
        proj(2)
        self.norm_tile(1, dstT2, dkeys, gname)
        proj(3)
        self.norm_tile(2, dstT2, dkeys, gname)
        self.norm_tile(3, dstT2, dkeys, gname)

    def build(self, stages=("mem", "A", "B", "C")):
        S = self.S
        self.setup_weights()
        self.setup_consts()
        self.setup_tiles()
        for b in range(self.nseq):
            self.fence()
            if "A" in stages:
                self.stage_A(b)
            S.n_rr = 6
            self.fence()
            if "B" in stages or "P" in stages:
                self.stage_gdn_prep(b)
            doG = "B" in stages or "G" in stages
            doS = "B" in stages or "S" in stages
            if doG:
                self.run_pipelined([self.gdn_front_gen(b, 0)], 1)
            for h in range(self.nheads):
                gens = []
                if doG:
                    gens.append(self.gdn_loop_gen(b, h))
                if doS:
                    self.sb_front(b, h)
                    gens.append(self.sb_loop_gen(b, h))
                self.run_pipelined(gens, 2)
                if doG:
                    g2 = [self.gdn_tail_gen(b, h)]
                    if h + 1 < self.nheads:
                        g2.append(self.gdn_front_gen(b, h + 1))
                    self.run_pipelined(g2, 2)
                self.emit_casts(12)
            if "B" in stages:
                if self.dbg and b == 0:
                    S.dma("pool", self.dbg_oa, self.oaT[:].rearrange("p c t -> p (c t)"), self.dbgsem[0],
                          reads=[("oaT", g) for g in range(NG)], writes=["dbgo"])
                    S.dma("pool", self.dbg_ob, self.obT[:].rearrange("p c t -> p (c t)"), self.dbgsem[1],
                          reads=[("obT", g) for g in range(NG)], writes=["dbgo"])
            self.fence()
            self.emit_casts(len(self.cast_list))
            if "C" in stages:
                S.n_rr = 8
                self.stage_mem(b)
                for g in range(NG):
                    self.stage_C(b, g)
        fin = [("out", t) for t in range(4)] + ["dbgo"]
        S.wait_all("pool", fin)
        S.wait_all("sp", fin)
        self.counts = {k: v.n for k, v in S.eng.items()}
        S.close()
        return self.nc


_CACHE = {}


def kernel(**inputs):
    n_cores = 8
    nseq = 4
    if "nc" not in _CACHE:
        _CACHE["nc"] = Builder(nseq=nseq, dbg=False).build()
    nc = _CACHE["nc"]
    x = np.ascontiguousarray(np.asarray(inputs["x"], dtype=np.float32))
    mem = np.ascontiguousarray(np.asarray(inputs["mem"], dtype=np.float32))
    params = {n: np.ascontiguousarray(np.asarray(inputs[n], dtype=np.float32)) for n in PARAM_NAMES}
    in_maps = []
    for c in range(n_cores):
        m = {"x": x[c * nseq:(c + 1) * nseq], "mem": mem[c * nseq:(c + 1) * nseq]}
        m.update(params)
        in_maps.append(m)
    res = run_bass_kernel_spmd(nc, in_maps, core_ids=list(range(n_cores)))
    return np.concatenate([r["out"] for r in res.results], axis=0).astype(np.float32)
```

```python
import contextlib
import math
import numpy as np
import concourse.bass as bass
import concourse.mybir as mybir
from concourse.bass_utils import run_bass_kernel_spmd

F32 = mybir.dt.float32
BF16 = mybir.dt.bfloat16
AF = mybir.ActivationFunctionType
ALU = mybir.AluOpType

D = 1024
SEQ = 2048
NT = 16
NG = 4
NH = 8
DFF = 2816
MEM = 256
IN_W = 9232
C_GQ, C_GK, C_GV, C_A, C_Z, C_SQ, C_SK, C_SV, C_GA, C_GB = 0, 1024, 2048, 3072, 3088, 4112, 5136, 6160, 7184, 8208
EPS = 1e-6
NEG = -30000.0

PARAM_NAMES = ["norm_mix", "w_in", "conv_gdn", "a_log", "dt_bias", "gdn_out_norm", "w_proj_gdn",
               "w_proj_sb", "w_out", "norm_x", "norm_mem", "w_xq", "w_xkv", "xq_norm", "xk_norm",
               "w_xo", "norm_ffn", "w_up", "conv_ffn", "w_down"]
PARAM_SHAPES = {
    "norm_mix": [1, 1024], "w_in": [1, 1024, IN_W], "conv_gdn": [1, 4, 3072], "a_log": [1, 8],
    "dt_bias": [1, 8], "gdn_out_norm": [1, 128], "w_proj_gdn": [1, 1024, 1024],
    "w_proj_sb": [1, 1024, 1024], "w_out": [1, 1024, 1024], "norm_x": [1, 1024],
    "norm_mem": [1, 1024], "w_xq": [1, 1024, 1024], "w_xkv": [1, 1024, 2048], "xq_norm": [1, 256],
    "xk_norm": [1, 256], "w_xo": [1, 1024, 1024], "norm_ffn": [1, 1024], "w_up": [1, 1024, 2 * DFF],
    "conv_ffn": [1, 3, 2 * DFF], "w_down": [1, DFF, 1024],
}
BIG_W = ["w_in", "w_proj_gdn", "w_proj_sb", "w_out", "w_xq", "w_xkv", "w_xo", "w_up", "w_down"]


class _Eng:
    def __init__(self, name, h, sem):
        self.name = name
        self.h = h
        self.sem = sem
        self.n = 0
        self.waited = {}


class DmaSem:
    def __init__(self, sem):
        self.sem = sem
        self.total = 0


class Sched:
    def __init__(self, nc):
        self.nc = nc
        self.es = contextlib.ExitStack()
        self.eng = {}
        for name, h in (("pe", nc.tensor), ("act", nc.scalar), ("dve", nc.vector),
                        ("pool", nc.gpsimd), ("sp", nc.sync)):
            sem = self.es.enter_context(nc.semaphore("s_" + name))
            self.eng[name] = _Eng(name, h, sem)
        self.lastw = {}
        self.readers = {}
        self.nsem = 0
        self.ntile = 0
        self.psum_tiles = []
        self.psum_rr = 0
        self.n_rr = 6
        self.stream_rr = {}

    def dsem(self):
        self.nsem += 1
        return DmaSem(self.es.enter_context(self.nc.semaphore("d%d" % self.nsem)))

    def sb(self, shape, dtype, name=None):
        self.ntile += 1
        nm = "%s_%d" % (name or "t", self.ntile)
        return self.es.enter_context(self.nc.sbuf_tensor(nm, list(shape), dtype))

    def init_psum(self, n=8, n_rr=6):
        for i in range(n):
            t = self.es.enter_context(self.nc.psum_tensor("ps%d" % i, [128, 512], F32))
            self.psum_tiles.append(t)
        self.n_rr = n_rr

    STREAM_BANKS = {"neu": [0, 1, 2], "rec": [3], "sb": [4, 5], "xa0": [0, 1, 2, 3], "xa1": [4, 5, 6, 7]}

    def psum(self, stream=None):
        if stream is None:
            i = self.psum_rr % self.n_rr
            self.psum_rr += 1
        else:
            banks = self.STREAM_BANKS[stream]
            cnt = self.stream_rr.get(stream, 0)
            self.stream_rr[stream] = cnt + 1
            i = banks[cnt % len(banks)]
        return self.psum_tiles[i], ("ps", i)

    def psum_fixed(self, i):
        return self.psum_tiles[i], ("ps", i)

    def _collect(self, eng, reads, writes):
        waits = {}

        def need(dep, raw):
            if dep[0] == "eng":
                _, fname, cnt = dep
                if fname == eng:
                    if eng in ("pe", "sp") or not raw:
                        return
                F = self.eng[fname]
                key = id(F.sem)
                if waits.get(key, (None, 0))[1] < cnt:
                    waits[key] = (F.sem, cnt)
            else:
                _, ds, val = dep
                key = id(ds.sem)
                if waits.get(key, (None, 0))[1] < val:
                    waits[key] = (ds.sem, val)

        for k in reads:
            d = self.lastw.get(k)
            if d is not None:
                need(d, True)
        for k in writes:
            d = self.lastw.get(k)
            if d is not None:
                need(d, False)
            for r in self.readers.get(k, ()):
                need(r, False)
        return waits

    def _emit_waits(self, E, waits):
        for key, (sem, val) in waits.items():
            if E.waited.get(key, 0) < val:
                E.h.wait_ge(sem, val)
                E.waited[key] = val

    def _record(self, me, reads, writes):
        for k in reads:
            lst = self.readers.setdefault(k, [])
            if me[0] == "eng":
                lst[:] = [r for r in lst if not (r[0] == "eng" and r[1] == me[1])]
            else:
                lst[:] = [r for r in lst if not (r[0] == "dma" and r[1] is me[1])]
            lst.append(me)
        for k in writes:
            self.lastw[k] = me
            self.readers[k] = []

    def op(self, eng, fn, reads=(), writes=()):
        E = self.eng[eng]
        psr = [k for k in reads if isinstance(k, tuple) and k[0] == "ps"]
        if psr:
            writes = list(writes) + [k for k in psr if k not in writes]
        self._emit_waits(E, self._collect(eng, reads, writes))
        ins = fn(E.h)
        E.n += 1
        ins.then_inc(E.sem, 1)
        self._record(("eng", eng, E.n), reads, writes)
        return ins

    def dma(self, queue, out, in_, ds, reads=(), writes=(), **kw):
        Q = self.eng[queue]
        self._emit_waits(Q, self._collect(queue, reads, writes))
        ins = Q.h.dma_start(out=out, in_=in_, **kw)
        ds.total += 16
        ins.then_inc(ds.sem, 16)
        self._record(("dma", ds, ds.total), reads, writes)
        return ins

    def wait_all(self, eng, keys):
        E = self.eng[eng]
        self._emit_waits(E, self._collect(eng, keys, ()))

    def close(self):
        self.es.close()


class Builder:
    def __init__(self, nseq=4, dbg=False):
        self.nseq = nseq
        self.dbg = dbg
        self.nheads = NH
        nc = self.nc = bass.Bass("TRN2", target_bir_lowering=False)
        self.x = nc.dram_tensor("x", [nseq, SEQ, D], F32, kind="ExternalInput").ap()
        self.mem = nc.dram_tensor("mem", [nseq, MEM, D], F32, kind="ExternalInput").ap()
        self.P = {}
        for n in PARAM_NAMES:
            self.P[n] = nc.dram_tensor(n, PARAM_SHAPES[n], F32, kind="ExternalInput").ap()
        self.out = nc.dram_tensor("out", [nseq, SEQ, D], F32, kind="ExternalOutput").ap()
        self.WB = {}
        for n in BIG_W:
            shp = PARAM_SHAPES[n][1:]
            self.WB[n] = nc.dram_tensor(n + "_bf", shp, BF16, kind="Internal").ap()
        if dbg:
            self.dbg_oa = nc.dram_tensor("dbg_oa", [128, 8 * SEQ], F32, kind="ExternalOutput").ap()
            self.dbg_ob = nc.dram_tensor("dbg_ob", [128, 8 * SEQ], F32, kind="ExternalOutput").ap()
            self.dbg_h1 = nc.dram_tensor("dbg_h1", [SEQ, D], F32, kind="ExternalOutput").ap()
            self.dbg_h2 = nc.dram_tensor("dbg_h2", [SEQ, D], F32, kind="ExternalOutput").ap()
        self.S = Sched(nc)
        self.S.init_psum(8, 6)

    def mm(self, out, lhsT, rhs, start, stop, reads, writes):
        self.S.op("pe", lambda e: e.matmul(out, lhsT=lhsT, rhs=rhs, start=start, stop=stop), reads, writes)

    def act(self, out, in_, func, reads, writes, scale=1.0, bias=0.0, accum_out=None):
        if accum_out is None:
            self.S.op("act", lambda e: e.activation(out=out, in_=in_, func=func, bias=bias, scale=scale), reads, writes)
        else:
            self.S.op("act", lambda e: e.activation(out=out, in_=in_, func=func, bias=bias, scale=scale,
                                                    accum_out=accum_out), reads, writes)

    def tt(self, eng, out, in0, in1, op, reads, writes):
        self.S.op(eng, lambda e: e.tensor_tensor(out=out, in0=in0, in1=in1, op=op), reads, writes)

    def ts(self, eng, out, in0, s1, s2, op0, op1, reads, writes):
        if s2 is None:
            self.S.op(eng, lambda e: e.tensor_scalar(out=out, in0=in0, scalar1=s1, scalar2=None, op0=op0), reads, writes)
        else:
            self.S.op(eng, lambda e: e.tensor_scalar(out=out, in0=in0, scalar1=s1, scalar2=s2, op0=op0, op1=op1),
                      reads, writes)

    def stt(self, out, in0, scalar, in1, op0, op1, reads, writes):
        self.S.op("dve", lambda e: e.scalar_tensor_tensor(out=out, in0=in0, scalar=scalar, in1=in1, op0=op0, op1=op1),
                  reads, writes)

    def cp(self, eng, out, in_, reads, writes):
        if eng == "act":
            self.act(out, in_, AF.Copy, reads, writes)
        else:
            self.S.op(eng, lambda e: e.tensor_copy(out=out, in_=in_), reads, writes)

    def setup_weights(self):
        S = self.S
        self.cast_list = []
        self.cast_sem = {}
        self.cast_done = []
        order = ["w_in", "w_proj_gdn", "w_proj_sb", "w_out", "w_xq", "w_xkv", "w_xo", "w_up", "w_down"]
        for n in order:
            src = self.P[n][0]
            dst = self.WB[n]
            self.cast_sem[dst.name] = S.dsem()
            rows, cols = src.shape
            for r0 in range(0, rows, 128):
                for c0 in range(0, cols, 2048):
                    c1 = min(cols, c0 + 2048)
                    self.cast_list.append((dst[r0:r0 + 128, c0:c1], src[r0:r0 + 128, c0:c1], dst.name))
        self.cast_pos = 0
        n_in = sum(1 for c in self.cast_list if c[2] == self.WB["w_in"].name)
        self.emit_casts(n_in)
        self.NSLOT = 3
        self.wslots = [S.sb([128, 8, 512], BF16, "wslot") for _ in range(self.NSLOT)]
        self.wsem = [S.dsem() for _ in range(self.NSLOT)]
        self.wrr = 0

    def emit_casts(self, n):
        S = self.S
        for _ in range(n):
            if self.cast_pos >= len(self.cast_list):
                return
            dst, src, nm = self.cast_list[self.cast_pos]
            k = self.cast_pos
            self.cast_pos += 1
            if k >= 4:
                sem, tot = self.cast_done[k - 4]
                S.eng["pool"].h.wait_ge(sem, tot)
            ds = self.cast_sem[nm]
            S.dma("pool", dst, src, ds, writes=[("wcast", nm)])
            self.cast_done.append((ds.sem, ds.total))

    def wslot(self):
        i = self.wrr % self.NSLOT
        self.wrr += 1
        return i

    def wload(self, i, src, nk, ncols, col_off=0, k_off=0):
        self.S.dma("sp", self.wslots[i][:, k_off:k_off + nk, col_off:col_off + ncols],
                   src.rearrange("(kc p) n -> p kc n", p=128), self.wsem[i],
                   reads=[("wcast", src.name)], writes=[("w", i)])

    def setup_consts(self):
        S = self.S
        sb = S.sb
        P = self.P
        self.ones_f = sb([128, 128], F32, "ones_f")
        self.ones_b = sb([128, 128], BF16, "ones_b")
        self.ident_f = sb([128, 128], F32, "ident_f")
        self.ident_b = sb([128, 128], BF16, "ident_b")
        self.LTincl_b = sb([128, 128], BF16, "LTincl")
        self.LT_f = sb([128, 128], F32, "LT_f")
        self.negmaskT = sb([128, 128], F32, "negmaskT")
        CK = ["consts"]
        S.op("pool", lambda e: e.memset(self.ones_f[:], 1.0), writes=CK)
        S.op("pool", lambda e: e.memset(self.negmaskT[:], 0.0), writes=CK)
        S.op("pool", lambda e: e.memset(self.ones_b[:], 1.0), writes=CK)

        def asel(out, in_, pattern, cm, cmp, fill):
            S.op("pool", lambda e: e.affine_select(out=out, in_=in_, pattern=pattern, compare_op=cmp, fill=fill,
                                                    base=0, channel_multiplier=cm), reads=CK, writes=CK)
        asel(self.ident_f[:], self.ones_f[:], [[-1, 128]], 1, ALU.is_equal, 0.0)
        asel(self.ident_b[:], self.ones_f[:], [[-1, 128]], 1, ALU.is_equal, 0.0)
        asel(self.LTincl_b[:], self.ones_f[:], [[-1, 128]], 1, ALU.is_ge, 0.0)
        asel(self.LT_f[:], self.ones_f[:], [[1, 128]], -1, ALU.is_ge, 0.0)
        asel(self.negmaskT[:], self.negmaskT[:], [[1, 128]], -1, ALU.is_ge, NEG)

        dpar = S.dsem()
        R = sb([128, 128], F32, "Rst")
        self.colsR1 = sb([128, 128], F32, "colsR1")
        self.convg = sb([128, 96], F32, "convg")
        self.convf = sb([128, 132], F32, "convf")

        def rows(p):
            return p.rearrange("o (r q) -> (o r) q", q=128)
        self.col_off = {}

        def finish(nrow, dst):
            ps, pk = S.psum()
            self.mm(ps[:, 0:nrow], R[0:nrow, :], self.ident_f[0:nrow, 0:nrow], True, True, ["Rst"] + CK, [pk])
            self.cp("dve", dst, ps[:, 0:nrow], [pk], CK)
        S.op("pool", lambda e: e.memset(R[:], 0.0), writes=["Rst"])
        r = 0
        for n, nr in (("norm_mix", 8), ("norm_x", 8), ("norm_ffn", 8), ("norm_mem", 8), ("gdn_out_norm", 1),
                      ("xq_norm", 2), ("xk_norm", 2)):
            S.dma("sp", R[r:r + nr, :], rows(P[n]), dpar, reads=["Rst"], writes=["Rst"])
            self.col_off[n] = r
            r += nr
        finish(128, self.colsR1[:])
        S.dma("sp", R[0:96, :], P["conv_gdn"][0].rearrange("i (c q) -> (i c) q", q=128), dpar, reads=["Rst"], writes=["Rst"])
        finish(96, self.convg[:, 0:96])
        S.dma("sp", R[0:88, :], P["conv_ffn"][0, 0:2, :].rearrange("i (c q) -> (i c) q", q=128), dpar, reads=["Rst"], writes=["Rst"])
        finish(88, self.convf[:, 0:88])
        S.dma("sp", R[0:44, :], P["conv_ffn"][0, 2:3, :].rearrange("i (c q) -> (i c) q", q=128), dpar, reads=["Rst"], writes=["Rst"])
        finish(44, self.convf[:, 88:132])
        self.dtb_row = sb([128, 8], F32, "dtb_row")
        self.nAexp_row = sb([128, 8], F32, "nAexp_row")
        S.dma("sp", self.dtb_row[:], P["dt_bias"][0].partition_broadcast(128), dpar, writes=["Rst2"])
        S.dma("sp", self.nAexp_row[:], P["a_log"][0].partition_broadcast(128), dpar, writes=["Rst2"])
        self.act(self.nAexp_row[:], self.nAexp_row[:], AF.Exp, ["Rst2"], CK)
        self.ts("dve", self.nAexp_row[:], self.nAexp_row[:], -1.0, None, ALU.mult, None, CK, CK)
        self.CK = CK

    def gcol(self, name, c):
        o = self.col_off[name] + c
        return self.colsR1[:, o:o + 1]

    def setup_tiles(self):
        sb = self.S.sb
        S = self.S
        self.xnT = sb([128, 8, SEQ], BF16, "xnT")
        self.oaT = sb([128, 8, SEQ], BF16, "oaT")
        self.obT = sb([128, 8, SEQ], BF16, "obT")
        self.h = self.xnT[:, 0:4, :].rearrange("p c t -> p (c t)").bitcast(F32).rearrange("p (a n) -> p a n", a=4)
        self.KT = self.xnT[:, 4, :].rearrange("p (c m) -> p c m", m=MEM)
        self.Vm = self.xnT[:, 5, :].rearrange("p (c m) -> p c m", m=1024)
        self.rawc = [self.xnT[:, 6 + i // 3, (i % 3) * 520:(i % 3) * 520 + 516] for i in range(6)]
        self.rawcrr = 0
        self.alias_keys = ([("xnT", g) for g in range(NG)] + [("h", t) for t in range(4)] + ["KT", "Vm"]
                           + [("rawc", i) for i in range(6)] + [("xt", 0)] + [(("raw", 1), g_) for g_ in range(NG)]
                           + [("xh", 0)] + [(("raw", 0), g_) for g_ in range(NG)])
        self.dummy = sb([128, 2], F32, "dummy")
        self.xt = [sb([128, 1024], F32, "xt") for _ in range(1)]
        self.xsem = [S.dsem() for _ in range(1)]
        self.xrr = 0
        self.stat = sb([128, 8], F32, "stat")
        self.xhrr = 0
        self.osem = [S.dsem() for _ in range(4)]
        self.dbgsem = [S.dsem() for _ in range(4)]
        self.hsem = [S.dsem() for _ in range(4)]
        self.FB = [sb([128, SEQ], BF16, "FB") for _ in range(7)]
        self.T5 = [sb([128, 512], F32, "T5") for _ in range(8)]
        self.t5rr = 0
        self.B5 = [sb([128, 512], BF16, "B5") for _ in range(8)]
        self.b5rr = 0
        self.G5 = [sb([128, 512], BF16, "G5") for _ in range(3)]
        self.H5 = [[sb([128, 512], BF16, "H5") for _ in range(5)] for _ in range(2)]
        self.carry = sb([128, 512], BF16, "carry")
        self.nsq = sb([128, 512], BF16, "nsq")
        self.gbt = [sb([128, 128], F32, "gbt") for _ in range(2)]
        self.gbrr = 0
        self.COLS = sb([128, NT, 48], F32, "COLS")
        self.Sf = sb([128, 128], F32, "Sf")
        self.Sb = sb([128, 128], BF16, "Sb")
        self.Vn = [sb([128, 128], BF16, "Vn") for _ in range(2)]
        self.sv = self.FB[5][:].rearrange("p (c t) -> p c t", t=128)
        self.raw = [sb([128, SEQ], BF16, "raw"), self.xt[0][:].bitcast(BF16)]
        self.xh = [self.raw[0][:, 0:1024]]
        self.wab = sb([128, 8, 16], BF16, "wab")
        self.wabsem = S.dsem()
        self.mnT = self.FB[6][:].rearrange("p (c m) -> p c m", m=MEM)
        self.halo = sb([128, 44, 2], BF16, "halo")

    def fence(self):
        self.S.op("pool", lambda e: e.memset(self.dummy[:], 0.0), writes=self.alias_keys)

    def actT_chunk(self, c):
        return self.FB[4 + c // 4][:, (c % 4) * 512:(c % 4 + 1) * 512], ("FB%d" % (4 + c // 4), c % 4)

    def t5(self):
        i = self.t5rr % len(self.T5)
        self.t5rr += 1
        return self.T5[i], ("T5", i)

    def b5(self):
        i = self.b5rr % len(self.B5)
        self.b5rr += 1
        return self.B5[i], ("B5", i)

    def norm_tile_to_T(self, src_ap, src_keys, gname, dstT, dst_col0, dst_keys, width=1024):
        S = self.S
        st = self.stat
        i = 0
        xh = self.xh[i]
        self.act(xh[:], src_ap, AF.Square, src_keys, [("xh", i), "stat"], accum_out=st[:, 0:1])
        self.act(st[:, 1:2], st[:, 0:1], AF.Ln, ["stat"], ["stat"], scale=1.0 / width, bias=EPS)
        self.act(st[:, 2:3], st[:, 1:2], AF.Exp, ["stat"], ["stat"], scale=-0.5)
        self.act(xh[:], src_ap, AF.Copy, src_keys + ["stat"], [("xh", i)], scale=st[:, 2:3])
        for half in range(2):
            ps, pk = S.psum()
            for c4 in range(4):
                c = half * 4 + c4
                self.mm(ps[:, c4 * 128:(c4 + 1) * 128], xh[:, c * 128:(c + 1) * 128], self.ident_b[:], True, True,
                        [("xh", i)] + self.CK, [pk])
            o = self.col_off[gname] + half * 4
            gb = self.colsR1[:, o:o + 4].unsqueeze(2).to_broadcast([128, 4, 128])
            self.tt("dve", dstT[:, half * 4:(half + 1) * 4, dst_col0:dst_col0 + 128],
                    ps[:].rearrange("p (c t) -> p c t", t=128), gb, ALU.mult, [pk] + self.CK, dst_keys)

    def load_x_tile(self, src):
        i = 0
        self.S.dma("sp", self.xt[i][:], src, self.xsem[i], writes=[("xt", i)])
        return self.xt[i], ("xt", i)

    def sigmoid(self, out, in_, tmp, in_keys, tmp_key, out_keys):
        self.act(tmp, in_, AF.Exp, in_keys, [tmp_key], scale=-1.0)
        self.act(tmp, tmp, AF.Ln, [tmp_key], [tmp_key], bias=1.0)
        self.act(out, tmp, AF.Exp, [tmp_key], out_keys, scale=-1.0)

    def stage_mem(self, b):
        S = self.S
        for mt in range(2):
            xt, xk = self.load_x_tile(self.mem[b, mt * 128:(mt + 1) * 128, :])
            self.norm_tile_to_T(xt[:], [xk], "norm_mem", self.mnT, mt * 128, [("FB6", g_) for g_ in range(NG)])
        wkv = self.WB["w_xkv"]
        st = self.stat
        for q in range(4):
            wi = self.wslot()
            self.wload(wi, wkv[:, q * 512:(q + 1) * 512], 8, 512)
            for mt in range(2):
                ps, pk = S.psum()
                for kc in range(8):
                    self.mm(ps[:], self.mnT[:, kc, mt * 128:(mt + 1) * 128], self.wslots[wi][:, kc, :], kc == 0, kc == 7,
                            [("FB6", g_) for g_ in range(NG)] + [("w", wi)], [pk])
                if q >= 2:
                    self.cp("act", self.Vm[:, mt, (q - 2) * 512:(q - 1) * 512], ps[:], [pk], ["Vm"])
                else:
                    kh, kk = self.b5()
                    for hh in range(2):
                        self.act(kh[:, hh * 256:(hh + 1) * 256], ps[:, hh * 256:(hh + 1) * 256], AF.Square, [pk], [kk, "stat"],
                                 accum_out=st[:, 4:5])
                        self.act(st[:, 5:6], st[:, 4:5], AF.Ln, ["stat"], ["stat"], scale=1.0 / 256, bias=EPS)
                        self.act(st[:, 6:7], st[:, 5:6], AF.Exp, ["stat"], ["stat"], scale=-0.5)
                        self.act(kh[:, hh * 256:(hh + 1) * 256], ps[:, hh * 256:(hh + 1) * 256], AF.Copy, [pk, "stat"], [kk],
                                 scale=st[:, 6:7])
                    ps2, pk2 = S.psum()
                    for j in range(4):
                        self.mm(ps2[:, j * 128:(j + 1) * 128], kh[:, j * 128:(j + 1) * 128], self.ident_b[:], True, True,
                                [kk] + self.CK, [pk2])
                    for j in range(4):
                        self.ts("dve", self.KT[:, q * 4 + j, mt * 128:(mt + 1) * 128], ps2[:, j * 128:(j + 1) * 128],
                                self.gcol("xk_norm", j % 2), None, ALU.mult, None, [pk2] + self.CK, ["KT"])

    def stage_A(self, b):
        for t in range(NT):
            self.A_tile(b, t)

    def A_tile(self, b, t):
        xt, xk = self.load_x_tile(self.x[b, t * 128:(t + 1) * 128, :])
        self.norm_tile_to_T(xt[:], [xk], "norm_mix", self.xnT, t * 128, [("xnT", t // 4)])

    def stage_gdn_prep(self, b):
        self.prep_begin(b)
        for t in range(NT):
            self.prep_tile(b, t)

    def prep_begin(self, b):
        S = self.S
        S.dma("sp", self.wab[:], self.WB["w_in"][:, C_A:C_A + 16].rearrange("(kc p) n -> p kc n", p=128), self.wabsem,
              reads=[("wcast", self.WB["w_in"].name)], writes=["wab"])

    def prep_tile(self, b, t):
        S = self.S
        if True:
            C = self.COLS
            ck = ("COLS", t)
            ps, pk = S.psum()
            for kc in range(8):
                self.mm(ps[:, 0:16], self.xnT[:, kc, t * 128:(t + 1) * 128], self.wab[:, kc, :], kc == 0, kc == 7,
                        [("xnT", t // 4), "wab"], [pk])
            tmp, tk = self.t5()
            self.tt("dve", tmp[:, 0:8], ps[:, 0:8], self.dtb_row[:], ALU.add, [pk] + self.CK, [tk])
            self.act(tmp[:, 0:8], tmp[:, 0:8], AF.Exp, [tk], [tk])
            self.act(tmp[:, 0:8], tmp[:, 0:8], AF.Ln, [tk], [tk], bias=1.0)
            self.tt("dve", tmp[:, 0:8], tmp[:, 0:8], self.nAexp_row[:], ALU.mult, [tk] + self.CK, [tk])
            self.act(tmp[:, 8:16], ps[:, 8:16], AF.Exp, [pk], [tk], scale=-1.0)
            self.act(tmp[:, 8:16], tmp[:, 8:16], AF.Ln, [tk], [tk], bias=1.0)
            self.act(C[:, t, 32:40], tmp[:, 8:16], AF.Exp, [tk], [ck], scale=-1.0)
            self.ts("dve", C[:, t, 40:48], C[:, t, 32:40], -1.0, None, ALU.mult, None, [ck], [ck])
            ps2, pk2 = S.psum()
            self.mm(ps2[:, 0:8], self.LT_f[:], tmp[:, 0:8], True, True, [tk] + self.CK, [pk2])
            self.mm(ps2[:, 8:16], self.ones_f[:], tmp[:, 0:8], True, True, [tk] + self.CK, [pk2])
            self.cp("dve", tmp[:, 16:32], ps2[:, 0:16], [pk2], [tk])
            self.cp("act", C[:, t, 0:8], tmp[:, 16:24], [tk], [ck])
            self.act(C[:, t, 8:16], tmp[:, 16:24], AF.Exp, [tk], [ck])
            self.tt("dve", tmp[:, 32:40], tmp[:, 24:32], tmp[:, 16:24], ALU.subtract, [tk], [tk])
            self.act(C[:, t, 16:24], tmp[:, 32:40], AF.Exp, [tk], [ck])
            self.act(C[:, t, 24:32], tmp[:, 24:32], AF.Exp, [tk], [ck])

    def projT(self, wi, col, evac, src=None, srckey="xnT"):
        S = self.S
        src = self.xnT if src is None else src
        for g in range(NG):
            ps, pk = S.psum()
            for kc in range(8):
                self.mm(ps[:], self.wslots[wi][:, kc, col * 128:(col + 1) * 128], src[:, kc, g * 512:(g + 1) * 512],
                        kc == 0, kc == 7, [(srckey, g), ("w", wi)], [pk])
            evac(g, ps, pk)

    def projT_gen(self, wi, col, evac):
        S = self.S
        src = self.xnT
        for g in range(NG):
            ps, pk = S.psum()
            for kc in range(8):
                self.mm(ps[:], self.wslots[wi][:, kc, col * 128:(col + 1) * 128], src[:, kc, g * 512:(g + 1) * 512],
                        kc == 0, kc == 7, [("xnT", g), ("w", wi)], [pk])
            evac(g, ps, pk)
            yield

    def make_diag(self, cols, nk):
        t, tk = self.b5()
        dg = t[:].rearrange("p (i c) -> p i c", c=128)
        for j, c in enumerate(cols):
            self.act(dg[:, j, :], self.ident_f[:], AF.Copy, self.CK, [tk], scale=c)
        return dg, tk

    def gdn_front_gen(self, b, h):
        S = self.S
        FB = self.FB
        Win = self.WB["w_in"]
        wi = self.wslot()
        for j, c0 in enumerate((C_GQ, C_GK, C_GV, C_Z)):
            self.wload(wi, Win[:, c0 + h * 128:c0 + (h + 1) * 128], 8, 128, col_off=j * 128)
        qT, kT, vT = FB[0], FB[1], FB[2]
        ysq = FB[5]

        def k4(name):
            return [(name, g) for g in range(NG)]

        plan = ((qT, "FB0"), (kT, "FB1"), (vT, "FB2"))

        def proj(j):
            raw = self.raw[j % 2]
            rk = ("raw", j % 2)

            def ev(g, ps, pk):
                self.cp("act", raw[:, g * 512:(g + 1) * 512], ps[:], [pk], [(rk, g)])
            yield from self.projT_gen(wi, j, ev)

        def convnorm(j):
            dst, dname = plan[j]
            raw = self.raw[j % 2]
            rk = ("raw", j % 2)
            ch = j * 8 + h
            dg, dk = self.make_diag([self.convg[:, i * 24 + ch:i * 24 + ch + 1] for i in range(4)], 4)
            for g in range(NG):
                ps, pk = S.psum()
                if g == 0:
                    self.mm(ps[:], dg[:, 3, :], raw[:, 0:512], True, False, [dk, (rk, 0)], [pk])
                    for i in (2, 1, 0):
                        off = 3 - i
                        self.mm(ps[:, off:512], dg[:, i, :], raw[:, 0:512 - off], False, i == 0, [dk, (rk, 0)], [pk])
                else:
                    for i in range(4):
                        self.mm(ps[:], dg[:, i, :], raw[:, g * 512 - 3 + i:g * 512 - 3 + i + 512], i == 0, i == 3,
                                [dk, (rk, g), (rk, g - 1)], [pk])
                sg, sk_ = self.t5()
                self.sigmoid(sg[:], ps[:], sg[:], [pk], sk_, [sk_])
                self.tt("dve", dst[:, g * 512:(g + 1) * 512], ps[:], sg[:], ALU.mult, [pk, sk_], [(dname, g)])
                yield
            if j < 2:
                self.act(ysq[:], dst[:], AF.Square, k4(dname), k4("FB5"))
                for g in range(NG):
                    ps, pk = S.psum()
                    self.mm(ps[:], self.ones_b[:], ysq[:, g * 512:(g + 1) * 512], True, True, [("FB5", g)] + self.CK, [pk])
                    tmp, tk = self.t5()
                    self.act(tmp[:], ps[:], AF.Ln, [pk], [tk], bias=EPS)
                    bias = -0.5 * math.log(128.0) if j == 0 else 0.0
                    self.act(tmp[:], tmp[:], AF.Exp, [tk], [tk], scale=-0.5, bias=bias)
                    self.tt("dve", dst[:, g * 512:(g + 1) * 512], dst[:, g * 512:(g + 1) * 512], tmp[:], ALU.mult,
                            [(dname, g), tk], [(dname, g)])
                    yield
        yield from proj(0)
        yield from proj(1)
        yield from convnorm(0)
        yield from proj(2)
        yield from convnorm(1)
        yield from convnorm(2)
        self._gdn_wi = wi

    def gdn_loop_gen(self, b, h):
        S = self.S
        FB = self.FB
        qT, kT, vT, OT = FB[0], FB[1], FB[2], FB[6]
        C = self.COLS
        def rowbc(g, fld):
            ps, pk = S.psum("neu")
            for tt_ in range(4):
                t = g * 4 + tt_
                gi = self.gbrr % 2
                self.gbrr += 1
                gb = self.gbt[gi]
                self.act(gb[:], self.ones_f[:], AF.Copy, [("COLS", t)] + self.CK, [("gbt", gi)],
                         scale=C[:, t, fld * 8 + h:fld * 8 + h + 1])
                self.mm(ps[:, tt_ * 128:(tt_ + 1) * 128], gb[:], self.ident_f[:], True, True, [("gbt", gi)] + self.CK, [pk])
            return ps, pk
        S.op("pool", lambda e: e.memset(self.Sf[:], 0.0), writes=["Sf"])
        S.op("pool", lambda e: e.memset(self.Sb[:], 0.0), writes=["Sb"])
        C = self.COLS
        sl = lambda cc: slice(cc * 128, (cc + 1) * 128)
        tok = lambda c: slice(c * 128, (c + 1) * 128)
        v3 = lambda ap: ap.rearrange("p (c t) -> p c t", t=128)
        hand = {}

        def neu(bt):
            gk = bt
            par = bt % 2
            cs = [bt * 4 + cc for cc in range(4)]

            def colb(f):
                return C[:, bt * 4:(bt + 1) * 4, f * 8 + h:f * 8 + h + 1].to_broadcast([128, 4, 128])
            ckeys = [("COLS", c) for c in cs]
            pk_, pkk_ = S.psum("neu")
            for cc, c in enumerate(cs):
                self.mm(pk_[:, sl(cc)], kT[:, tok(c)], self.ident_b[:], True, True, [("FB1", gk)] + self.CK, [pkk_])
            Kg, Kgk = self.G5[0], 'G50'
            Kd, Kdk = self.H5[par][0], ('H5', par, 0)
            self.tt("dve", v3(Kg[:]), v3(pk_[:]), colb(1), ALU.mult, [pkk_] + ckeys, [Kgk])
            self.tt("dve", v3(Kd[:]), v3(pk_[:]), colb(2), ALU.mult, [pkk_] + ckeys, [Kdk])
            pv_, pvk_ = S.psum("neu")
            for cc, c in enumerate(cs):
                self.mm(pv_[:, sl(cc)], vT[:, tok(c)], self.ident_b[:], True, True, [("FB2", gk)] + self.CK, [pvk_])
            Vt, Vtk = self.G5[2], 'G52'
            self.cp("act", Vt[:], pv_[:], [pvk_], [Vtk])
            yield
            peg, pegk = rowbc(bt, 1)
            QdT, QdTk = self.H5[par][3], ('H5', par, 3)
            self.tt("dve", QdT[:], qT[:, bt * 512:(bt + 1) * 512], peg[:], ALU.mult, [pegk, ("FB0", bt)], [QdTk])
            yield
            E, Ek = self.t5()
            pgc, pgck = rowbc(bt, 0)
            for cc, c in enumerate(cs):
                self.stt(E[:, sl(cc)], pgc[:, sl(cc)], C[:, c, h:h + 1], self.negmaskT[:], ALU.subtract, ALU.add,
                         [pgck, ("COLS", c)] + self.CK, [Ek])
            self.act(E[:], E[:], AF.Exp, [Ek], [Ek])
            yield
            QKm, QKmk = self.H5[par][1], ('H5', par, 1)
            pqk, pqkk = S.psum("neu")
            for cc, c in enumerate(cs):
                self.mm(pqk[:, sl(cc)], kT[:, tok(c)], qT[:, tok(c)], True, True, [("FB1", gk), ("FB0", gk)], [pqkk])
            self.tt("dve", QKm[:], pqk[:], E[:], ALU.mult, [pqkk, Ek], [QKmk])
            Es, Esk = self.t5()
            S.op("pool", lambda e: e.affine_select(out=v3(Es[:]), in_=v3(E[:]), pattern=[[0, 4], [1, 128]],
                                                    compare_op=ALU.is_gt, fill=0.0, base=0, channel_multiplier=-1),
                 reads=[Ek], writes=[Esk])
            yield
            X, Xk = self.t5()
            pkk, pkkk = S.psum("neu")
            for cc, c in enumerate(cs):
                self.mm(pkk[:, sl(cc)], kT[:, tok(c)], kT[:, tok(c)], True, True, [("FB1", gk)], [pkkk])
            for cc, c in enumerate(cs):
                self.stt(X[:, sl(cc)], pkk[:, sl(cc)], C[:, c, 32 + h:32 + h + 1], Es[:, sl(cc)], ALU.mult, ALU.mult,
                         [pkkk, Esk, ("COLS", c)], [Xk])
            yield
            pt, ptk = S.psum("neu")
            for cc in range(4):
                self.mm(pt[:, sl(cc)], X[:, sl(cc)], self.ident_f[:], True, True, [Xk] + self.CK, [ptk])
            XT, XTk = self.t5()
            self.cp("act", XT[:], pt[:], [ptk], [XTk])
            yield
            Y, Yk = self.t5()
            self.tt("dve", v3(Y[:]), self.ident_f[:].unsqueeze(1).to_broadcast([128, 4, 128]), v3(X[:]), ALU.subtract,
                    [Xk] + self.CK, [Yk])
            for p in (2, 4, 8, 16, 32, 64):
                pb, pbk = S.psum("neu")
                for cc in range(4):
                    self.mm(pb[:, sl(cc)], X[:, sl(cc)], XT[:, sl(cc)], True, True, [Xk, XTk], [pbk])
                XTn, XTnk = self.t5()
                if p < 64:
                    pa, pak = S.psum("neu")
                    for cc in range(4):
                        self.mm(pa[:, sl(cc)], XT[:, sl(cc)], X[:, sl(cc)], True, True, [Xk, XTk], [pak])
                    Xn, Xnk = self.t5()
                    self.cp("act", Xn[:], pa[:], [pak], [Xnk])
                self.cp("dve", XTn[:], pb[:], [pbk], [XTnk])
                if p < 64:
                    X, Xk = Xn, Xnk
                XT, XTk = XTn, XTnk
                yield
                py, pyk = S.psum("neu")
                for cc in range(4):
                    self.mm(py[:, sl(cc)], XT[:, sl(cc)], Y[:, sl(cc)], True, True, [XTk, Yk], [pyk])
                Yn, Ynk = self.t5()
                self.tt("dve", Yn[:], py[:], Y[:], ALU.add, [pyk, Yk], [Ynk])
                Y, Yk = Yn, Ynk
                yield
            Yb, Ybk = self.b5()
            self.cp("act", Yb[:], Y[:], [Yk], [Ybk])
            Y, Yk = Yb, Ybk
            yield
            pu, puk = S.psum("neu")
            for cc in range(4):
                self.mm(pu[:, sl(cc)], Y[:, sl(cc)], Vt[:, sl(cc)], True, True, [Yk, Vtk], [puk])
            Upp, Uppk = self.H5[par][4], ('H5', par, 4)
            self.tt("dve", v3(Upp[:]), v3(pu[:]), colb(4), ALU.mult, [puk] + ckeys, [Uppk])
            pw, pwk = S.psum("neu")
            for cc in range(4):
                self.mm(pw[:, sl(cc)], Kg[:, sl(cc)], Y[:, sl(cc)], True, True, [Kgk, Yk], [pwk])
            WT, WTk = self.H5[par][2], ('H5', par, 2)
            self.cp("act", WT[:], pw[:], [pwk], [WTk])
            hand[bt] = (Kd, Kdk, QdT, QdTk, QKm, QKmk, Upp, Uppk, WT, WTk)
            yield

        def rec(bt):
            cs = [bt * 4 + cc for cc in range(4)]
            Kd, Kdk, QdT, QdTk, QKm, QKmk, Upp, Uppk, WT, WTk = hand[bt]
            po, pok = S.psum_fixed(6)
            for cc, c in enumerate(cs):
                pws, pwsk = S.psum("rec")
                self.mm(pws[:, 0:128], WT[:, sl(cc)], self.Sb[:], True, True, [WTk, "Sb"], [pwsk])
                Vn = self.Vn[c % 2]
                Vnk = ("Vn", c % 2)
                self.stt(Vn[:], pws[:, 0:128], C[:, c, 40 + h:40 + h + 1], Upp[:, sl(cc)], ALU.mult, ALU.add,
                         [pwsk, Uppk, ("COLS", c)], [Vnk])
                yield
                self.mm(po[:, sl(cc)], self.Sb[:], QdT[:, sl(cc)], True, False, ["Sb", QdTk], [pok])
                self.mm(po[:, sl(cc)], Vn[:], QKm[:, sl(cc)], False, True, [Vnk, QKmk], [pok])
                psn, psnk = S.psum("rec")
                self.mm(psn[:, 0:128], Kd[:, sl(cc)], Vn[:], True, True, [Kdk, Vnk], [psnk])
                self.stt(self.Sf[:], self.Sf[:], C[:, c, 24 + h:24 + h + 1], psn[:, 0:128], ALU.mult, ALU.add,
                         [psnk, "Sf", ("COLS", c)], ["Sf"])
                self.cp("act", self.Sb[:], self.Sf[:], ["Sf"], ["Sb"])
                yield
            self.cp("act", OT[:, bt * 512:(bt + 1) * 512], po[:], [pok], [("FB6", bt)])
            yield

        def inter(gs):
            gs = list(gs)
            while gs:
                for gn in list(gs):
                    try:
                        next(gn)
                        yield
                    except StopIteration:
                        gs.remove(gn)
        for bt in range(NG):
            gs = [neu(bt)] + ([rec(bt - 1)] if bt > 0 else [])
            yield from inter(gs)
        yield from inter([rec(NG - 1)])

    def gdn_tail_gen(self, b, h):
        S = self.S
        FB = self.FB
        OT, zs, ysq = FB[6], FB[3], FB[4]
        wi = self._gdn_wi

        def k4(name):
            return [(name, g) for g in range(NG)]

        def evz(g, ps, pk):
            sg, sk_ = self.t5()
            self.sigmoid(sg[:], ps[:], sg[:], [pk], sk_, [sk_])
            self.tt("dve", zs[:, g * 512:(g + 1) * 512], ps[:], sg[:], ALU.mult, [pk, sk_], [("FB3", g)])
        yield from self.projT_gen(wi, 3, evz)
        self.act(ysq[:], OT[:], AF.Square, k4("FB6"), k4("FB4"))
        gn = self.gcol("gdn_out_norm", 0)
        for g in range(NG):
            ps, pk = S.psum()
            self.mm(ps[:], self.ones_b[:], ysq[:, g * 512:(g + 1) * 512], True, True, [("FB4", g)] + self.CK, [pk])
            tmp, tk = self.t5()
            self.act(tmp[:], ps[:], AF.Ln, [pk], [tk], scale=1.0 / 128, bias=EPS)
            self.act(tmp[:], tmp[:], AF.Exp, [tk], [tk], scale=-0.5)
            t2, t2k = self.t5()
            self.stt(t2[:], OT[:, g * 512:(g + 1) * 512], gn, tmp[:], ALU.mult, ALU.mult, [("FB6", g), tk] + self.CK, [t2k])
            self.tt("dve", self.oaT[:, h, g * 512:(g + 1) * 512], t2[:], zs[:, g * 512:(g + 1) * 512], ALU.mult,
                    [t2k, ("FB3", g)], [("oaT", g)])
            yield

    def sb_front(self, b, h):
        S = self.S
        FB = self.FB
        Win = self.WB["w_in"]
        wi = self.wslot()
        for j, c0 in enumerate((C_SQ, C_SK, C_SV)):
            self.wload(wi, Win[:, c0 + h * 128:c0 + (h + 1) * 128], 8, 128, col_off=j * 128)
        sqT, skT = FB[3], FB[4]
        scl = 128.0 ** -0.5

        def evq(g, ps, pk):
            self.act(sqT[:, g * 512:(g + 1) * 512], ps[:], AF.Copy, [pk], [("FB3", g)], scale=scl)
        self.projT(wi, 0, evq)

        def evk(g, ps, pk):
            self.cp("act", skT[:, g * 512:(g + 1) * 512], ps[:], [pk], [("FB4", g)])
        self.projT(wi, 1, evk)
        for t4 in range(4):
            ps, pk = S.psum()
            for tt_ in range(4):
                t = t4 * 4 + tt_
                for kc in range(8):
                    self.mm(ps[:, tt_ * 128:(tt_ + 1) * 128], self.xnT[:, kc, t * 128:(t + 1) * 128],
                            self.wslots[wi][:, kc, 256:384], kc == 0, kc == 7, [("xnT", t4), ("w", wi)], [pk])
            self.cp("dve", self.sv[:, t4 * 4:(t4 + 1) * 4, :].rearrange("p c t -> p (c t)"), ps[:], [pk], [("FB5", t4)])

    def sb_loop_gen(self, b, h):
        S = self.S
        FB = self.FB
        sqT, skT = FB[3], FB[4]
        for qg in range(NG):
            acc, acck = S.psum_fixed(7)
            carry, ck = self.carry, 'carry'
            nsq, nsqk = self.nsq, 'nsq'
            self.act(nsq[:], sqT[:, qg * 512:(qg + 1) * 512], AF.Copy, [("FB3", qg)], [nsqk], scale=-1.0)
            S.op("pool", lambda e: e.memset(carry[:], 0.0), writes=[ck])
            blocks = list(range(4 * qg + 3, -1, -1))
            st1 = {}

            def stage1(kb):
                r = kb - 4 * qg
                c0 = max(r, 0) * 128
                pz, pzk = S.psum("sb")
                self.mm(pz[:, c0:512], skT[:, kb * 128:(kb + 1) * 128], sqT[:, qg * 512 + c0:(qg + 1) * 512], True, True,
                        [("FB4", kb // 4), ("FB3", qg)], [pzk])
                e, ek = self.b5()
                self.act(e[:, c0:512], pz[:, c0:512], AF.Exp, [pzk], [ek])
                if r >= 0:
                    S.op("pool", lambda en: en.affine_select(out=e[:, c0:c0 + 128], in_=e[:, c0:c0 + 128], pattern=[[1, 128]],
                                                              compare_op=ALU.is_gt, fill=0.0, base=0, channel_multiplier=-1),
                         reads=[ek], writes=[ek])
                sp, spk = self.b5()
                self.act(sp[:, c0:512], e[:, c0:512], AF.Ln, [ek], [spk], bias=1.0)
                st1[kb] = (r, c0, sp, spk)

            st2 = {}

            def stage2a(kb, first):
                r, c0, sp, spk = st1.pop(kb)
                pc, pck = S.psum("sb")
                self.mm(pc[:, c0:512], self.LTincl_b[:], sp[:, c0:512], True, False, [spk] + self.CK, [pck])
                if not first:
                    self.mm(pc[:, c0:512], self.ones_b[:], carry[:, c0:512], False, False, [ck] + self.CK, [pck])
                self.mm(pc[:, c0:512], skT[:, kb * 128:(kb + 1) * 128], nsq[:, c0:512], False, True,
                        [("FB4", kb // 4), nsqk], [pck])
                w, wk = self.b5()
                self.act(w[:, c0:512], pc[:, c0:512], AF.Exp, [pck], [wk], scale=-1.0)
                if r >= 0:
                    S.op("pool", lambda en: en.affine_select(out=w[:, c0:c0 + 128], in_=w[:, c0:c0 + 128], pattern=[[1, 128]],
                                                              compare_op=ALU.is_gt, fill=0.0, base=0, channel_multiplier=-1),
                         reads=[wk], writes=[wk])
                st2[kb] = (r, c0, sp, spk, w, wk)

            def stage2b(kb, first, last):
                r, c0, sp, spk, w, wk = st2.pop(kb)
                self.mm(acc[:, c0:512], self.sv[:, kb, :], w[:, c0:512], first, last, [("FB5", kb // 4), wk], [acck])
                if not last:
                    self.tt("dve", carry[:, c0:512], carry[:, c0:512], sp[:, c0:512], ALU.add, [ck, spk], [ck])

            stage1(blocks[0])
            for n, kb in enumerate(blocks):
                if n + 1 < len(blocks):
                    stage1(blocks[n + 1])
                    yield
                stage2a(kb, n == 0)
                yield
                stage2b(kb, n == 0, n == len(blocks) - 1)
                yield
            self.cp("dve", self.obT[:, h, qg * 512:(qg + 1) * 512], acc[:], [acck], [("obT", qg)])

    def run_pipelined(self, gens, depth=2):
        active = []
        it = iter(gens)
        done = False
        while True:
            while not done and len(active) < depth:
                try:
                    active.append(next(it))
                except StopIteration:
                    done = True
            if not active:
                break
            for gn in list(active):
                try:
                    next(gn)
                except StopIteration:
                    active.remove(gn)

    def ffn_chunk_gen(self, g, c_lo, q0, nq, f, hnT, hkeys):
        S = self.S
        Wu = self.WB["w_up"]
        if f == 0:
            self._ffn_wis = []
            for which in range(2):
                wi = self.wslot()
                col = which * DFF + (c_lo + q0) * 128
                self.wload(wi, Wu[:, col:col + nq * 128], 8, nq * 128)
                self._ffn_wis.append(wi)
        wis = list(self._ffn_wis)
        c = c_lo + q0 + f
        raws = []
        for which in range(2):
            wi = wis[which]
            ch = which * 22 + c
            ps, pk = S.psum()
            for kc in range(8):
                self.mm(ps[:], self.wslots[wi][:, kc, f * 128:(f + 1) * 128], hnT[kc // 4][:, kc % 4, :],
                        kc == 0, kc == 7, hkeys + [("w", wi)], [pk])
            ri = self.rawcrr % 6
            self.rawcrr += 1
            rc = self.rawc[ri]
            rck = ("rawc", ri)
            hk = ("halo", ch)
            self.cp("dve", rc[:, 2:4], self.halo[:, ch, :], [hk], [rck])
            self.cp("act", rc[:, 4:516], ps[:], [pk], [rck])
            self.cp("dve", self.halo[:, ch, :], rc[:, 514:516], [rck], [hk])
            dg, dk = self.make_diag([self.convf[:, i * 44 + ch:i * 44 + ch + 1] for i in range(3)], 3)
            raws.append((rc, rck, dg, dk))
        yield
        cps = []
        for which in range(2):
            rc, rck, dg, dk = raws[which]
            pc, pck = S.psum()
            for i in range(3):
                self.mm(pc[:], dg[:, i, :], rc[:, 2 + i:2 + i + 512], i == 0, i == 2, [dk, rck], [pck])
            cps.append((pc, pck))
        sg, sgk = self.t5()
        self.sigmoid(sg[:], cps[0][0][:], sg[:], [cps[0][1]], sgk, [sgk])
        t2, t2k = self.t5()
        self.tt("dve", t2[:], cps[0][0][:], sg[:], ALU.mult, [cps[0][1], sgk], [t2k])
        a_ap, a_key = self.actT_chunk(q0 + f)
        self.tt("dve", a_ap, cps[1][0][:], t2[:], ALU.mult, [cps[1][1], t2k], [a_key])
        yield

    def down_proj_add(self, nkc, k_base):
        S = self.S
        W = self.WB["w_down"]
        for half in range(2):
            slots = []
            k = 0
            while k < nkc:
                nk = min(8, nkc - k)
                wi = self.wslot()
                self.wload(wi, W[(k_base + k) * 128:(k_base + k + nk) * 128, half * 512:(half + 1) * 512], nk, 512)
                slots.append((wi, k, nk))
                k += nk
            for tt_ in range(4):
                ps, pk = S.psum()
                first = True
                for (wi, k0, nk) in slots:
                    for kk in range(nk):
                        kc = k0 + kk
                        a_ap, a_key = self.actT_chunk(kc)
                        self.mm(ps[:], a_ap[:, tt_ * 128:(tt_ + 1) * 128], self.wslots[wi][:, kk, :], first,
                                kc == nkc - 1, [a_key, ("w", wi)], [pk])
                        first = False
                self.tt("dve", self.h[:, tt_, half * 512:(half + 1) * 512], ps[:], self.h[:, tt_, half * 512:(half + 1) * 512],
                        ALU.add, [pk, ("h", tt_)], [("h", tt_)])

    def stage_C(self, b, g):
        S = self.S
        FB = self.FB
        gs = slice(g * 512, (g + 1) * 512)
        v8 = lambda t: t[:].rearrange("p (c t) -> p c t", t=512)
        for tt_ in range(4):
            S.dma("sp", self.h[:, tt_, :], self.x[b, g * 512 + tt_ * 128:g * 512 + (tt_ + 1) * 128, :], self.hsem[tt_],
                  writes=[("h", tt_)])
        xnG = [v8(FB[4]), v8(FB[5])]
        xkeys = [("FB4", f) for f in range(4)] + [("FB5", f) for f in range(4)]
        self.norm_group(xnG, xkeys, "norm_mix")
        mT = [v8(FB[0]), v8(FB[1])]
        sig = v8(FB[2])
        m1 = v8(FB[3])
        Win = self.WB["w_in"]
        for quad in range(2):
            cs = slice(quad * 512, (quad + 1) * 512)
            for (gate_c0, pw, srcT, skey) in ((C_GA, "w_proj_gdn", self.oaT, "oaT"), (C_GB, "w_proj_sb", self.obT, "obT")):
                wi = self.wslot()
                self.wload(wi, Win[:, gate_c0 + quad * 512:gate_c0 + (quad + 1) * 512], 8, 512)
                for f in range(4):
                    ps, pk = S.psum()
                    for kc in range(8):
                        self.mm(ps[:], self.wslots[wi][:, kc, f * 128:(f + 1) * 128], xnG[kc // 4][:, kc % 4, :], kc == 0, kc == 7,
                                xkeys + [("w", wi)], [pk])
                    tmp, tk = self.t5()
                    self.sigmoid(sig[:, f, :], ps[:], tmp[:], [pk], tk, [("FB2", f)])
                wi = self.wslot()
                self.wload(wi, self.WB[pw][:, cs], 8, 512)
                for f in range(4):
                    ps, pk = S.psum()
                    for kc in range(8):
                        self.mm(ps[:], self.wslots[wi][:, kc, f * 128:(f + 1) * 128], srcT[:, kc, gs], kc == 0, kc == 7,
                                [(skey, g), ("w", wi)], [pk])
                    if gate_c0 == C_GA:
                        self.tt("dve", m1[:, f, :], ps[:], sig[:, f, :], ALU.mult, [pk, ("FB2", f)], [("FB3", f)])
                    else:
                        tmp, tk = self.t5()
                        self.tt("dve", tmp[:], ps[:], sig[:, f, :], ALU.mult, [pk, ("FB2", f)], [tk])
                        self.tt("dve", mT[quad][:, f, :], tmp[:], m1[:, f, :], ALU.add, [tk, ("FB3", f)], [("FB%d" % quad, f)])
        mTfull_keys = [("FB0", f) for f in range(4)] + [("FB1", f) for f in range(4)]

        class _TwoBuf:
            def __init__(s, a, b_):
                s.a, s.b = a, b_

            def __getitem__(s, idx):
                p, kc, t = idx
                return (s.a if kc < 4 else s.b)[p, kc % 4, t]
        hnT = [v8(FB[2]), v8(FB[3])]
        hkeys = [("FB2", f) for f in range(4)] + [("FB3", f) for f in range(4)]
        self.tokproj_norm(_TwoBuf(mT[0], mT[1]), mTfull_keys, "w_out", 8, hnT, hkeys, "norm_x")
        if self.dbg and b == 0:
            for tt_ in range(4):
                S.dma("pool", self.dbg_h1[g * 512 + tt_ * 128:g * 512 + (tt_ + 1) * 128, :], self.h[:, tt_, :], self.dbgsem[tt_],
                      reads=[("h", tt_)], writes=["dbgo"])
        qraw = [v8(FB[0]), v8(FB[1])]
        qsq = [v8(FB[4]), v8(FB[5])]
        Wq = self.WB["w_xq"]
        for quad in range(2):
            wi = self.wslot()
            self.wload(wi, Wq[:, quad * 512:(quad + 1) * 512], 8, 512)
            for f in range(4):
                ps, pk = S.psum()
                for kc in range(8):
                    self.mm(ps[:], self.wslots[wi][:, kc, f * 128:(f + 1) * 128], hnT[kc // 4][:, kc % 4, :], kc == 0, kc == 7,
                            hkeys + [("w", wi)], [pk])
                self.cp("act", qraw[quad][:, f, :], ps[:], [pk], [("FB%d" % quad, f)])
                self.act(qsq[quad][:, f, :], ps[:], AF.Square, [pk], [("FB%d" % (4 + quad), f)])
        oxT = [v8(FB[2]), v8(FB[3])]
        def xa_head(hx):
            quad, f0 = hx // 2, (hx % 2) * 2
            stream = 'xa%d' % (hx % 2)
            ps, pk = S.psum(stream)
            for dc in range(2):
                self.mm(ps[:], self.ones_b[:], qsq[quad][:, f0 + dc, :], dc == 0, dc == 1,
                        [("FB%d" % (4 + quad), f0 + dc)] + self.CK, [pk])
            rs, rsk = self.t5()
            self.act(rs[:], ps[:], AF.Ln, [pk], [rsk], scale=1.0 / 256, bias=EPS)
            self.act(rs[:], rs[:], AF.Exp, [rsk], [rsk], scale=-0.5, bias=-0.5 * math.log(16.0) * 2)
            yield
            qn = []
            for dc in range(2):
                qb, qbk = self.b5()
                self.stt(qb[:], qraw[quad][:, f0 + dc, :], self.gcol("xq_norm", dc), rs[:], ALU.mult, ALU.mult,
                         [("FB%d" % quad, f0 + dc), rsk] + self.CK, [qbk])
                qn.append((qb, qbk))
            yield
            PT = []
            for mt in range(2):
                pz, pzk = S.psum(stream)
                for dc in range(2):
                    self.mm(pz[:], self.KT[:, hx * 2 + dc, mt * 128:(mt + 1) * 128], qn[dc][0][:], dc == 0, dc == 1,
                            ["KT", qn[dc][1]], [pzk])
                pt_, ptk = self.b5()
                self.act(pt_[:], pz[:], AF.Exp, [pzk], [ptk])
                PT.append((pt_, ptk))
            yield
            pd, pdk = S.psum(stream)
            for mt in range(2):
                self.mm(pd[:], self.ones_b[:], PT[mt][0][:], mt == 0, mt == 1, [PT[mt][1]] + self.CK, [pdk])
            rd, rdk = self.t5()
            S.op("dve", lambda e: e.reciprocal(out=rd[:], in_=pd[:]), reads=[pdk], writes=[rdk])
            yield
            for dc in range(2):
                po_, pok_ = S.psum(stream)
                for mt in range(2):
                    self.mm(po_[:], self.Vm[:, mt, hx * 256 + dc * 128:hx * 256 + (dc + 1) * 128], PT[mt][0][:], mt == 0, mt == 1,
                            ["Vm", PT[mt][1]], [pok_])
                self.tt("dve", oxT[quad][:, f0 + dc, :], po_[:], rd[:], ALU.mult, [pok_, rdk], [("FB%d" % (2 + quad), f0 + dc)])
        self.run_pipelined([xa_head(hx) for hx in range(4)], 2)
        hnT2 = [v8(FB[0]), v8(FB[1])]
        hkeys2 = [("FB0", f) for f in range(4)] + [("FB1", f) for f in range(4)]
        self.tokproj_norm(_TwoBuf(oxT[0], oxT[1]), hkeys, "w_xo", 8, hnT2, hkeys2, "norm_ffn")
        if self.dbg and b == 0:
            for tt_ in range(4):
                S.dma("pool", self.dbg_h2[g * 512 + tt_ * 128:g * 512 + (tt_ + 1) * 128, :], self.h[:, tt_, :], self.dbgsem[tt_],
                      reads=[("h", tt_)], writes=["dbgo"])
        hnT, hkeys = hnT2, hkeys2
        Wu = self.WB["w_up"]
        if g == 0:
            S.op("pool", lambda e: e.memset(self.halo[:], 0.0), writes=[("halo", c_) for c_ in range(44)])
        S.n_rr = 8
        for half in range(2):
            c_lo = half * 11
            gens = []
            for q0 in range(0, 11, 4):
                nq = min(4, 11 - q0)
                for f in range(nq):
                    gens.append(self.ffn_chunk_gen(g, c_lo, q0, nq, f, hnT, hkeys))
            self.run_pipelined(gens, 3)
            self.down_proj_add(11, c_lo)
        for tt_ in range(4):
            S.dma("pool", self.out[b, g * 512 + tt_ * 128:g * 512 + (tt_ + 1) * 128, :], self.h[:, tt_, :], self.osem[tt_],
                  reads=[("h", tt_)], writes=[("out", tt_)])

    def _tokproj_multi(self, srcT, srckeys, wname, nkc):
        S = self.S
        W = self.WB[wname]
        for half in range(2):
            wi = self.wslot()
            self.wload(wi, W[:, half * 512:(half + 1) * 512], 8, 512)
            for tt_ in range(4):
                ps, pk = S.psum()
                for kc in range(nkc):
                    self.mm(ps[:], srcT[:, kc, tt_ * 128:(tt_ + 1) * 128], self.wslots[wi][:, kc, :], kc == 0, kc == nkc - 1,
                            srckeys + [("w", wi)], [pk])
                self.tt("dve", self.h[:, tt_, half * 512:(half + 1) * 512], ps[:], self.h[:, tt_, half * 512:(half + 1) * 512],
                        ALU.add, [pk, ("h", tt_)], [("h", tt_)])

    def norm_group(self, dstT2, dkeys, gname):
        for tt_ in range(4):
            self.norm_tile(tt_, dstT2, dkeys, gname)

    def norm_tile(self, tt_, dstT2, dkeys, gname):
        S = self.S
        st = self.stat
        src = self.h[:, tt_, :]
        sk = [("h", tt_)]
        i = 0
        xh = self.xh[i]
        self.act(xh[:], src, AF.Square, sk, [("xh", i), "stat"], accum_out=st[:, 0:1])
        self.act(st[:, 1:2], st[:, 0:1], AF.Ln, ["stat"], ["stat"], scale=1.0 / 1024, bias=EPS)
        self.act(st[:, 2:3], st[:, 1:2], AF.Exp, ["stat"], ["stat"], scale=-0.5)
        self.act(xh[:], src, AF.Copy, sk + ["stat"], [("xh", i)], scale=st[:, 2:3])
        for half in range(2):
            ps, pk = S.psum()
            for c4 in range(4):
                c = half * 4 + c4
                self.mm(ps[:, c4 * 128:(c4 + 1) * 128], xh[:, c * 128:(c + 1) * 128], self.ident_b[:], True, True,
                        [("xh", i)] + self.CK, [pk])
            o = self.col_off[gname] + half * 4
            gb = self.colsR1[:, o:o + 4].unsqueeze(2).to_broadcast([128, 4, 128])
            self.tt("dve", dstT2[half][:, :, tt_ * 128:(tt_ + 1) * 128], ps[:].rearrange("p (c t) -> p c t", t=128), gb,
                    ALU.mult, [pk] + self.CK, dkeys[half * 4:(half + 1) * 4])

    def tokproj_norm(self, srcT, srckeys, wname, nkc, dstT2, dkeys, gname):
        S = self.S
        W = self.WB[wname]
        wis = []
        for half in range(2):
            wi = self.wslot()
            self.wload(wi, W[:, half * 512:(half + 1) * 512], 8, 512)
            wis.append(wi)

        def proj(tt_):
            for half in range(2):
                wi = wis[half]
                ps, pk = S.psum()
                for kc in range(nkc):
                    self.mm(ps[:], srcT[:, kc, tt_ * 128:(tt_ + 1) * 128], self.wslots[wi][:, kc, :], kc == 0, kc == nkc - 1,
                            srckeys + [("w", wi)], [pk])
                self.tt("dve", self.h[:, tt_, half * 512:(half + 1) * 512], ps[:], self.h[:, tt_, half * 512:(half + 1) * 512],
                        ALU.add, [pk, ("h", tt_)], [("h", tt_)])
        proj(0)
        proj(1)
        self.norm_tile(0, dstT2, dkeys, gname)
        proj(2)
        self.norm_tile(1, dstT2, dkeys, gname)
        proj(3)
        self.norm_tile(2, dstT2, dkeys, gname)
        self.norm_tile(3, dstT2, dkeys, gname)

    def build(self, stages=("mem", "A", "B", "C")):
        S = self.S
        self.setup_weights()
        self.setup_consts()
        self.setup_tiles()
        for b in range(self.nseq):
            self.fence()
            S.n_rr = 6
            if "A" in stages and ("B" in stages or "P" in stages):
                self.prep_begin(b)
                for t in range(NT):
                    self.A_tile(b, t)
                    if t > 0:
                        self.prep_tile(b, t - 1)
                self.prep_tile(b, NT - 1)
            else:
                if "A" in stages:
                    self.stage_A(b)
                if "B" in stages or "P" in stages:
                    self.stage_gdn_prep(b)
            self.fence()
            doG = "B" in stages or "G" in stages
            doS = "B" in stages or "S" in stages
            if doG:
                self.run_pipelined([self.gdn_front_gen(b, 0)], 1)
            for h in range(self.nheads):
                gens = []
                if doG:
                    gens.append(self.gdn_loop_gen(b, h))
                if doS:
                    self.sb_front(b, h)
                    gens.append(self.sb_loop_gen(b, h))
                self.run_pipelined(gens, 2)
                if doG:
                    g2 = [self.gdn_tail_gen(b, h)]
                    if h + 1 < self.nheads:
                        g2.append(self.gdn_front_gen(b, h + 1))
                    self.run_pipelined(g2, 2)
                self.emit_casts(12)
            if "B" in stages:
                if self.dbg and b == 0:
                    S.dma("pool", self.dbg_oa, self.oaT[:].rearrange("p c t -> p (c t)"), self.dbgsem[0],
                          reads=[("oaT", g) for g in range(NG)], writes=["dbgo"])
                    S.dma("pool", self.dbg_ob, self.obT[:].rearrange("p c t -> p (c t)"), self.dbgsem[1],
                          reads=[("obT", g) for g in range(NG)], writes=["dbgo"])
            self.fence()
            self.emit_casts(len(self.cast_list))
            if "C" in stages:
                S.n_rr = 8
                self.stage_mem(b)
                for g in range(NG):
                    self.stage_C(b, g)
        fin = [("out", t) for t in range(4)] + ["dbgo"]
        S.wait_all("pool", fin)
        S.wait_all("sp", fin)
        self.counts = {k: v.n for k, v in S.eng.items()}
        S.close()
        return self.nc


_CACHE = {}


def kernel(**inputs):
    n_cores = 8
    nseq = 4
    if "nc" not in _CACHE:
        _CACHE["nc"] = Builder(nseq=nseq, dbg=False).build()
    nc = _CACHE["nc"]
    x = np.ascontiguousarray(np.asarray(inputs["x"], dtype=np.float32))
    mem = np.ascontiguousarray(np.asarray(inputs["mem"], dtype=np.float32))
    params = {n: np.ascontiguousarray(np.asarray(inputs[n], dtype=np.float32)) for n in PARAM_NAMES}
    in_maps = []
    for c in range(n_cores):
        m = {"x": x[c * nseq:(c + 1) * nseq], "mem": mem[c * nseq:(c + 1) * nseq]}
        m.update(params)
        in_maps.append(m)
    res = run_bass_kernel_spmd(nc, in_maps, core_ids=list(range(n_cores)))
    return np.concatenate([r["out"] for r in res.results], axis=0).astype(np.float32)
```

```python
import contextlib
import math
import numpy as np
import concourse.bass as bass
import concourse.mybir as mybir
from concourse.bass_utils import run_bass_kernel_spmd

F32 = mybir.dt.float32
BF16 = mybir.dt.bfloat16
AF = mybir.ActivationFunctionType
ALU = mybir.AluOpType

D = 1024
SEQ = 2048
NT = 16
NG = 4
NH = 8
DFF = 2816
MEM = 256
IN_W = 9232
C_GQ, C_GK, C_GV, C_A, C_Z, C_SQ, C_SK, C_SV, C_GA, C_GB = 0, 1024, 2048, 3072, 3088, 4112, 5136, 6160, 7184, 8208
EPS = 1e-6
NEG = -30000.0

PARAM_NAMES = ["norm_mix", "w_in", "conv_gdn", "a_log", "dt_bias", "gdn_out_norm", "w_proj_gdn",
               "w_proj_sb", "w_out", "norm_x", "norm_mem", "w_xq", "w_xkv", "xq_norm", "xk_norm",
               "w_xo", "norm_ffn", "w_up", "conv_ffn", "w_down"]
PARAM_SHAPES = {
    "norm_mix": [1, 1024], "w_in": [1, 1024, IN_W], "conv_gdn": [1, 4, 3072], "a_log": [1, 8],
    "dt_bias": [1, 8], "gdn_out_norm": [1, 128], "w_proj_gdn": [1, 1024, 1024],
    "w_proj_sb": [1, 1024, 1024], "w_out": [1, 1024, 1024], "norm_x": [1, 1024],
    "norm_mem": [1, 1024], "w_xq": [1, 1024, 1024], "w_xkv": [1, 1024, 2048], "xq_norm": [1, 256],
    "xk_norm": [1, 256], "w_xo": [1, 1024, 1024], "norm_ffn": [1, 1024], "w_up": [1, 1024, 2 * DFF],
    "conv_ffn": [1, 3, 2 * DFF], "w_down": [1, DFF, 1024],
}
BIG_W = ["w_in", "w_proj_gdn", "w_proj_sb", "w_out", "w_xq", "w_xkv", "w_xo", "w_up", "w_down"]


class _Eng:
    def __init__(self, name, h, sem):
        self.name = name
        self.h = h
        self.sem = sem
        self.n = 0
        self.waited = {}


class DmaSem:
    def __init__(self, sem):
        self.sem = sem
        self.total = 0


class Sched:
    def __init__(self, nc):
        self.nc = nc
        self.es = contextlib.ExitStack()
        self.eng = {}
        for name, h in (("pe", nc.tensor), ("act", nc.scalar), ("dve", nc.vector),
                        ("pool", nc.gpsimd), ("sp", nc.sync)):
            sem = self.es.enter_context(nc.semaphore("s_" + name))
            self.eng[name] = _Eng(name, h, sem)
        self.lastw = {}
        self.readers = {}
        self.nsem = 0
        self.ntile = 0
        self.psum_tiles = []
        self.psum_rr = 0
        self.n_rr = 6
        self.stream_rr = {}

    def dsem(self):
        self.nsem += 1
        return DmaSem(self.es.enter_context(self.nc.semaphore("d%d" % self.nsem)))

    def sb(self, shape, dtype, name=None):
        self.ntile += 1
        nm = "%s_%d" % (name or "t", self.ntile)
        return self.es.enter_context(self.nc.sbuf_tensor(nm, list(shape), dtype))

    def init_psum(self, n=8, n_rr=6):
        for i in range(n):
            t = self.es.enter_context(self.nc.psum_tensor("ps%d" % i, [128, 512], F32))
            self.psum_tiles.append(t)
        self.n_rr = n_rr

    STREAM_BANKS = {"neu": [0, 1, 2], "rec": [3], "sb": [4, 5], "xa0": [0, 1, 2, 3], "xa1": [4, 5, 6, 7]}

    def psum(self, stream=None):
        if stream is None:
            i = self.psum_rr % self.n_rr
            self.psum_rr += 1
        else:
            banks = self.STREAM_BANKS[stream]
            cnt = self.stream_rr.get(stream, 0)
            self.stream_rr[stream] = cnt + 1
            i = banks[cnt % len(banks)]
        return self.psum_tiles[i], ("ps", i)

    def psum_fixed(self, i):
        return self.psum_tiles[i], ("ps", i)

    def _collect(self, eng, reads, writes):
        waits = {}

        def need(dep, raw):
            if dep[0] == "eng":
                _, fname, cnt = dep
                if fname == eng:
                    if eng in ("pe", "sp") or not raw:
                        return
                F = self.eng[fname]
                key = id(F.sem)
                if waits.get(key, (None, 0))[1] < cnt:
                    waits[key] = (F.sem, cnt)
            else:
                _, ds, val = dep
                key = id(ds.sem)
                if waits.get(key, (None, 0))[1] < val:
                    waits[key] = (ds.sem, val)

        for k in reads:
            d = self.lastw.get(k)
            if d is not None:
                need(d, True)
        for k in writes:
            d = self.lastw.get(k)
            if d is not None:
                need(d, False)
            for r in self.readers.get(k, ()):
                need(r, False)
        return waits

    def _emit_waits(self, E, waits):
        for key, (sem, val) in waits.items():
            if E.waited.get(key, 0) < val:
                E.h.wait_ge(sem, val)
                E.waited[key] = val

    def _record(self, me, reads, writes):
        for k in reads:
            lst = self.readers.setdefault(k, [])
            if me[0] == "eng":
                lst[:] = [r for r in lst if not (r[0] == "eng" and r[1] == me[1])]
            else:
                lst[:] = [r for r in lst if not (r[0] == "dma" and r[1] is me[1])]
            lst.append(me)
        for k in writes:
            self.lastw[k] = me
            self.readers[k] = []

    def op(self, eng, fn, reads=(), writes=()):
        E = self.eng[eng]
        psr = [k for k in reads if isinstance(k, tuple) and k[0] == "ps"]
        if psr:
            writes = list(writes) + [k for k in psr if k not in writes]
        self._emit_waits(E, self._collect(eng, reads, writes))
        ins = fn(E.h)
        E.n += 1
        ins.then_inc(E.sem, 1)
        self._record(("eng", eng, E.n), reads, writes)
        return ins

    def dma(self, queue, out, in_, ds, reads=(), writes=(), **kw):
        Q = self.eng[queue]
        self._emit_waits(Q, self._collect(queue, reads, writes))
        ins = Q.h.dma_start(out=out, in_=in_, **kw)
        ds.total += 16
        ins.then_inc(ds.sem, 16)
        self._record(("dma", ds, ds.total), reads, writes)
        return ins

    def wait_all(self, eng, keys):
        E = self.eng[eng]
        self._emit_waits(E, self._collect(eng, keys, ()))

    def close(self):
        self.es.close()


class Builder:
    def __init__(self, nseq=4, dbg=False):
        self.nseq = nseq
        self.dbg = dbg
        self.nheads = NH
        nc = self.nc = bass.Bass("TRN2", target_bir_lowering=False)
        self.x = nc.dram_tensor("x", [nseq, SEQ, D], F32, kind="ExternalInput").ap()
        self.mem = nc.dram_tensor("mem", [nseq, MEM, D], F32, kind="ExternalInput").ap()
        self.P = {}
        for n in PARAM_NAMES:
            self.P[n] = nc.dram_tensor(n, PARAM_SHAPES[n], F32, kind="ExternalInput").ap()
        self.out = nc.dram_tensor("out", [nseq, SEQ, D], F32, kind="ExternalOutput").ap()
        self.WB = {}
        for n in BIG_W:
            shp = PARAM_SHAPES[n][1:]
            self.WB[n] = nc.dram_tensor(n + "_bf", shp, BF16, kind="Internal").ap()
        if dbg:
            self.dbg_oa = nc.dram_tensor("dbg_oa", [128, 8 * SEQ], F32, kind="ExternalOutput").ap()
            self.dbg_ob = nc.dram_tensor("dbg_ob", [128, 8 * SEQ], F32, kind="ExternalOutput").ap()
            self.dbg_h1 = nc.dram_tensor("dbg_h1", [SEQ, D], F32, kind="ExternalOutput").ap()
            self.dbg_h2 = nc.dram_tensor("dbg_h2", [SEQ, D], F32, kind="ExternalOutput").ap()
        self.S = Sched(nc)
        self.S.init_psum(8, 6)

    def mm(self, out, lhsT, rhs, start, stop, reads, writes):
        self.S.op("pe", lambda e: e.matmul(out, lhsT=lhsT, rhs=rhs, start=start, stop=stop), reads, writes)

    def act(self, out, in_, func, reads, writes, scale=1.0, bias=0.0, accum_out=None):
        if accum_out is None:
            self.S.op("act", lambda e: e.activation(out=out, in_=in_, func=func, bias=bias, scale=scale), reads, writes)
        else:
            self.S.op("act", lambda e: e.activation(out=out, in_=in_, func=func, bias=bias, scale=scale,
                                                    accum_out=accum_out), reads, writes)

    def tt(self, eng, out, in0, in1, op, reads, writes):
        self.S.op(eng, lambda e: e.tensor_tensor(out=out, in0=in0, in1=in1, op=op), reads, writes)

    def ts(self, eng, out, in0, s1, s2, op0, op1, reads, writes):
        if s2 is None:
            self.S.op(eng, lambda e: e.tensor_scalar(out=out, in0=in0, scalar1=s1, scalar2=None, op0=op0), reads, writes)
        else:
            self.S.op(eng, lambda e: e.tensor_scalar(out=out, in0=in0, scalar1=s1, scalar2=s2, op0=op0, op1=op1),
                      reads, writes)

    def stt(self, out, in0, scalar, in1, op0, op1, reads, writes):
        self.S.op("dve", lambda e: e.scalar_tensor_tensor(out=out, in0=in0, scalar=scalar, in1=in1, op0=op0, op1=op1),
                  reads, writes)

    def cp(self, eng, out, in_, reads, writes):
        if eng == "act":
            self.act(out, in_, AF.Copy, reads, writes)
        else:
            self.S.op(eng, lambda e: e.tensor_copy(out=out, in_=in_), reads, writes)

    def setup_weights(self):
        S = self.S
        self.cast_list = []
        self.cast_sem = {}
        self.cast_done = []
        order = ["w_in", "w_proj_gdn", "w_proj_sb", "w_out", "w_xq", "w_xkv", "w_xo", "w_up", "w_down"]
        for n in order:
            src = self.P[n][0]
            dst = self.WB[n]
            self.cast_sem[dst.name] = S.dsem()
            rows, cols = src.shape
            for r0 in range(0, rows, 128):
                for c0 in range(0, cols, 2048):
                    c1 = min(cols, c0 + 2048)
                    self.cast_list.append((dst[r0:r0 + 128, c0:c1], src[r0:r0 + 128, c0:c1], dst.name))
        self.cast_pos = 0
        n_in = sum(1 for c in self.cast_list if c[2] == self.WB["w_in"].name)
        self.emit_casts(n_in)
        self.NSLOT = 3
        self.wslots = [S.sb([128, 8, 512], BF16, "wslot") for _ in range(self.NSLOT)]
        self.wsem = [S.dsem() for _ in range(self.NSLOT)]
        self.wrr = 0

    def emit_casts(self, n):
        S = self.S
        for _ in range(n):
            if self.cast_pos >= len(self.cast_list):
                return
            dst, src, nm = self.cast_list[self.cast_pos]
            k = self.cast_pos
            self.cast_pos += 1
            if k >= 4:
                sem, tot = self.cast_done[k - 4]
                S.eng["pool"].h.wait_ge(sem, tot)
            ds = self.cast_sem[nm]
            S.dma("pool", dst, src, ds, writes=[("wcast", nm)])
            self.cast_done.append((ds.sem, ds.total))

    def wslot(self):
        i = self.wrr % self.NSLOT
        self.wrr += 1
        return i

    def wload(self, i, src, nk, ncols, col_off=0, k_off=0):
        self.S.dma("sp", self.wslots[i][:, k_off:k_off + nk, col_off:col_off + ncols],
                   src.rearrange("(kc p) n -> p kc n", p=128), self.wsem[i],
                   reads=[("wcast", src.name)], writes=[("w", i)])

    def setup_consts(self):
        S = self.S
        sb = S.sb
        P = self.P
        self.ones_f = sb([128, 128], F32, "ones_f")
        self.ones_b = sb([128, 128], BF16, "ones_b")
        self.ident_f = sb([128, 128], F32, "ident_f")
        self.ident_b = sb([128, 128], BF16, "ident_b")
        self.LTincl_b = sb([128, 128], BF16, "LTincl")
        self.LT_f = sb([128, 128], F32, "LT_f")
        self.negmaskT = sb([128, 128], F32, "negmaskT")
        CK = ["consts"]
        S.op("pool", lambda e: e.memset(self.ones_f[:], 1.0), writes=CK)
        S.op("pool", lambda e: e.memset(self.negmaskT[:], 0.0), writes=CK)
        S.op("pool", lambda e: e.memset(self.ones_b[:], 1.0), writes=CK)

        def asel(out, in_, pattern, cm, cmp, fill):
            S.op("pool", lambda e: e.affine_select(out=out, in_=in_, pattern=pattern, compare_op=cmp, fill=fill,
                                                    base=0, channel_multiplier=cm), reads=CK, writes=CK)
        asel(self.ident_f[:], self.ones_f[:], [[-1, 128]], 1, ALU.is_equal, 0.0)
        asel(self.ident_b[:], self.ones_f[:], [[-1, 128]], 1, ALU.is_equal, 0.0)
        asel(self.LTincl_b[:], self.ones_f[:], [[-1, 128]], 1, ALU.is_ge, 0.0)
        asel(self.LT_f[:], self.ones_f[:], [[1, 128]], -1, ALU.is_ge, 0.0)
        asel(self.negmaskT[:], self.negmaskT[:], [[1, 128]], -1, ALU.is_ge, NEG)

        dpar = S.dsem()
        R = sb([128, 128], F32, "Rst")
        self.colsR1 = sb([128, 128], F32, "colsR1")
        self.convg = sb([128, 96], F32, "convg")
        self.convf = sb([128, 132], F32, "convf")

        def rows(p):
            return p.rearrange("o (r q) -> (o r) q", q=128)
        self.col_off = {}

        def finish(nrow, dst):
            ps, pk = S.psum()
            self.mm(ps[:, 0:nrow], R[0:nrow, :], self.ident_f[0:nrow, 0:nrow], True, True, ["Rst"] + CK, [pk])
            self.cp("dve", dst, ps[:, 0:nrow], [pk], CK)
        S.op("pool", lambda e: e.memset(R[:], 0.0), writes=["Rst"])
        r = 0
        for n, nr in (("norm_mix", 8), ("norm_x", 8), ("norm_ffn", 8), ("norm_mem", 8), ("gdn_out_norm", 1),
                      ("xq_norm", 2), ("xk_norm", 2)):
            S.dma("sp", R[r:r + nr, :], rows(P[n]), dpar, reads=["Rst"], writes=["Rst"])
            self.col_off[n] = r
            r += nr
        finish(128, self.colsR1[:])
        S.dma("sp", R[0:96, :], P["conv_gdn"][0].rearrange("i (c q) -> (i c) q", q=128), dpar, reads=["Rst"], writes=["Rst"])
        finish(96, self.convg[:, 0:96])
        S.dma("sp", R[0:88, :], P["conv_ffn"][0, 0:2, :].rearrange("i (c q) -> (i c) q", q=128), dpar, reads=["Rst"], writes=["Rst"])
        finish(88, self.convf[:, 0:88])
        S.dma("sp", R[0:44, :], P["conv_ffn"][0, 2:3, :].rearrange("i (c q) -> (i c) q", q=128), dpar, reads=["Rst"], writes=["Rst"])
        finish(44, self.convf[:, 88:132])
        self.dtb_row = sb([128, 8], F32, "dtb_row")
        self.nAexp_row = sb([128, 8], F32, "nAexp_row")
        S.dma("sp", self.dtb_row[:], P["dt_bias"][0].partition_broadcast(128), dpar, writes=["Rst2"])
        S.dma("sp", self.nAexp_row[:], P["a_log"][0].partition_broadcast(128), dpar, writes=["Rst2"])
        self.act(self.nAexp_row[:], self.nAexp_row[:], AF.Exp, ["Rst2"], CK)
        self.ts("dve", self.nAexp_row[:], self.nAexp_row[:], -1.0, None, ALU.mult, None, CK, CK)
        self.CK = CK

    def gcol(self, name, c):
        o = self.col_off[name] + c
        return self.colsR1[:, o:o + 1]

    def setup_tiles(self):
        sb = self.S.sb
        S = self.S
        self.xnT = sb([128, 8, SEQ], BF16, "xnT")
        self.oaT = sb([128, 8, SEQ], BF16, "oaT")
        self.obT = sb([128, 8, SEQ], BF16, "obT")
        self.h = self.xnT[:, 0:4, :].rearrange("p c t -> p (c t)").bitcast(F32).rearrange("p (a n) -> p a n", a=4)
        self.KT = self.xnT[:, 4, :].rearrange("p (c m) -> p c m", m=MEM)
        self.Vm = self.xnT[:, 5, :].rearrange("p (c m) -> p c m", m=1024)
        self.rawc = [self.xnT[:, 6 + i // 3, (i % 3) * 520:(i % 3) * 520 + 516] for i in range(6)]
        self.rawcrr = 0
        self.alias_keys = ([("xnT", g) for g in range(NG)] + [("h", t) for t in range(4)] + ["KT", "Vm"]
                           + [("rawc", i) for i in range(6)] + [("xt", 0)] + [(("raw", 1), g_) for g_ in range(NG)]
                           + [("xh", 0)] + [(("raw", 0), g_) for g_ in range(NG)])
        self.dummy = sb([128, 2], F32, "dummy")
        self.xt = [sb([128, 1024], F32, "xt") for _ in range(1)]
        self.xsem = [S.dsem() for _ in range(1)]
        self.xrr = 0
        self.stat = sb([128, 8], F32, "stat")
        self.xhrr = 0
        self.osem = [S.dsem() for _ in range(4)]
        self.dbgsem = [S.dsem() for _ in range(4)]
        self.hsem = [S.dsem() for _ in range(4)]
        self.FB = [sb([128, SEQ], BF16, "FB") for _ in range(7)]
        self.T5 = [sb([128, 512], F32, "T5") for _ in range(8)]
        self.t5rr = 0
        self.B5 = [sb([128, 512], BF16, "B5") for _ in range(8)]
        self.b5rr = 0
        self.G5 = [sb([128, 512], BF16, "G5") for _ in range(3)]
        self.H5 = [[sb([128, 512], BF16, "H5") for _ in range(5)] for _ in range(2)]
        self.carry = sb([128, 512], BF16, "carry")
        self.nsq = sb([128, 512], BF16, "nsq")
        self.gbt = [sb([128, 128], F32, "gbt") for _ in range(2)]
        self.gbrr = 0
        self.COLS = sb([128, NT, 48], F32, "COLS")
        self.Sf = sb([128, 128], F32, "Sf")
        self.Sb = sb([128, 128], BF16, "Sb")
        self.Vn = [sb([128, 128], BF16, "Vn") for _ in range(2)]
        self.sv = self.FB[5][:].rearrange("p (c t) -> p c t", t=128)
        self.raw = [sb([128, SEQ], BF16, "raw"), self.xt[0][:].bitcast(BF16)]
        self.xh = [self.raw[0][:, 0:1024]]
        self.wab = sb([128, 8, 16], BF16, "wab")
        self.wabsem = S.dsem()
        self.mnT = self.FB[6][:].rearrange("p (c m) -> p c m", m=MEM)
        self.halo = sb([128, 44, 2], BF16, "halo")

    def fence(self):
        self.S.op("pool", lambda e: e.memset(self.dummy[:], 0.0), writes=self.alias_keys)

    def actT_chunk(self, c):
        return self.FB[4 + c // 4][:, (c % 4) * 512:(c % 4 + 1) * 512], ("FB%d" % (4 + c // 4), c % 4)

    def t5(self):
        i = self.t5rr % len(self.T5)
        self.t5rr += 1
        return self.T5[i], ("T5", i)

    def b5(self):
        i = self.b5rr % len(self.B5)
        self.b5rr += 1
        return self.B5[i], ("B5", i)

    def norm_tile_to_T(self, src_ap, src_keys, gname, dstT, dst_col0, dst_keys, width=1024):
        S = self.S
        st = self.stat
        i = 0
        xh = self.xh[i]
        self.act(xh[:], src_ap, AF.Square, src_keys, [("xh", i), "stat"], accum_out=st[:, 0:1])
        self.act(st[:, 1:2], st[:, 0:1], AF.Ln, ["stat"], ["stat"], scale=1.0 / width, bias=EPS)
        self.act(st[:, 2:3], st[:, 1:2], AF.Exp, ["stat"], ["stat"], scale=-0.5)
        self.act(xh[:], src_ap, AF.Copy, src_keys + ["stat"], [("xh", i)], scale=st[:, 2:3])
        for half in range(2):
            ps, pk = S.psum()
            for c4 in range(4):
                c = half * 4 + c4
                self.mm(ps[:, c4 * 128:(c4 + 1) * 128], xh[:, c * 128:(c + 1) * 128], self.ident_b[:], True, True,
                        [("xh", i)] + self.CK, [pk])
            o = self.col_off[gname] + half * 4
            gb = self.colsR1[:, o:o + 4].unsqueeze(2).to_broadcast([128, 4, 128])
            self.tt("dve", dstT[:, half * 4:(half + 1) * 4, dst_col0:dst_col0 + 128],
                    ps[:].rearrange("p (c t) -> p c t", t=128), gb, ALU.mult, [pk] + self.CK, dst_keys)

    def load_x_tile(self, src):
        i = 0
        self.S.dma("sp", self.xt[i][:], src, self.xsem[i], writes=[("xt", i)])
        return self.xt[i], ("xt", i)

    def sigmoid(self, out, in_, tmp, in_keys, tmp_key, out_keys):
        self.act(tmp, in_, AF.Exp, in_keys, [tmp_key], scale=-1.0)
        self.act(tmp, tmp, AF.Ln, [tmp_key], [tmp_key], bias=1.0)
        self.act(out, tmp, AF.Exp, [tmp_key], out_keys, scale=-1.0)

    def stage_mem(self, b):
        S = self.S
        for mt in range(2):
            xt, xk = self.load_x_tile(self.mem[b, mt * 128:(mt + 1) * 128, :])
            self.norm_tile_to_T(xt[:], [xk], "norm_mem", self.mnT, mt * 128, [("FB6", g_) for g_ in range(NG)])
        wkv = self.WB["w_xkv"]
        st = self.stat
        for q in range(4):
            wi = self.wslot()
            self.wload(wi, wkv[:, q * 512:(q + 1) * 512], 8, 512)
            for mt in range(2):
                ps, pk = S.psum()
                for kc in range(8):
                    self.mm(ps[:], self.mnT[:, kc, mt * 128:(mt + 1) * 128], self.wslots[wi][:, kc, :], kc == 0, kc == 7,
                            [("FB6", g_) for g_ in range(NG)] + [("w", wi)], [pk])
                if q >= 2:
                    self.cp("act", self.Vm[:, mt, (q - 2) * 512:(q - 1) * 512], ps[:], [pk], ["Vm"])
                else:
                    kh, kk = self.b5()
                    for hh in range(2):
                        self.act(kh[:, hh * 256:(hh + 1) * 256], ps[:, hh * 256:(hh + 1) * 256], AF.Square, [pk], [kk, "stat"],
                                 accum_out=st[:, 4:5])
                        self.act(st[:, 5:6], st[:, 4:5], AF.Ln, ["stat"], ["stat"], scale=1.0 / 256, bias=EPS)
                        self.act(st[:, 6:7], st[:, 5:6], AF.Exp, ["stat"], ["stat"], scale=-0.5)
                        self.act(kh[:, hh * 256:(hh + 1) * 256], ps[:, hh * 256:(hh + 1) * 256], AF.Copy, [pk, "stat"], [kk],
                                 scale=st[:, 6:7])
                    ps2, pk2 = S.psum()
                    for j in range(4):
                        self.mm(ps2[:, j * 128:(j + 1) * 128], kh[:, j * 128:(j + 1) * 128], self.ident_b[:], True, True,
                                [kk] + self.CK, [pk2])
                    for j in range(4):
                        self.ts("dve", self.KT[:, q * 4 + j, mt * 128:(mt + 1) * 128], ps2[:, j * 128:(j + 1) * 128],
                                self.gcol("xk_norm", j % 2), None, ALU.mult, None, [pk2] + self.CK, ["KT"])

    def stage_A(self, b):
        for t in range(NT):
            xt, xk = self.load_x_tile(self.x[b, t * 128:(t + 1) * 128, :])
            self.norm_tile_to_T(xt[:], [xk], "norm_mix", self.xnT, t * 128, [("xnT", t // 4)])

    def stage_gdn_prep(self, b):
        S = self.S
        S.dma("sp", self.wab[:], self.WB["w_in"][:, C_A:C_A + 16].rearrange("(kc p) n -> p kc n", p=128), self.wabsem,
              reads=[("wcast", self.WB["w_in"].name)], writes=["wab"])
        for t in range(NT):
            C = self.COLS
            ck = ("COLS", t)
            ps, pk = S.psum()
            for kc in range(8):
                self.mm(ps[:, 0:16], self.xnT[:, kc, t * 128:(t + 1) * 128], self.wab[:, kc, :], kc == 0, kc == 7,
                        [("xnT", t // 4), "wab"], [pk])
            tmp, tk = self.t5()
            self.tt("dve", tmp[:, 0:8], ps[:, 0:8], self.dtb_row[:], ALU.add, [pk] + self.CK, [tk])
            self.act(tmp[:, 0:8], tmp[:, 0:8], AF.Exp, [tk], [tk])
            self.act(tmp[:, 0:8], tmp[:, 0:8], AF.Ln, [tk], [tk], bias=1.0)
            self.tt("dve", tmp[:, 0:8], tmp[:, 0:8], self.nAexp_row[:], ALU.mult, [tk] + self.CK, [tk])
            self.act(tmp[:, 8:16], ps[:, 8:16], AF.Exp, [pk], [tk], scale=-1.0)
            self.act(tmp[:, 8:16], tmp[:, 8:16], AF.Ln, [tk], [tk], bias=1.0)
            self.act(C[:, t, 32:40], tmp[:, 8:16], AF.Exp, [tk], [ck], scale=-1.0)
            self.ts("dve", C[:, t, 40:48], C[:, t, 32:40], -1.0, None, ALU.mult, None, [ck], [ck])
            ps2, pk2 = S.psum()
            self.mm(ps2[:, 0:8], self.LT_f[:], tmp[:, 0:8], True, True, [tk] + self.CK, [pk2])
            self.mm(ps2[:, 8:16], self.ones_f[:], tmp[:, 0:8], True, True, [tk] + self.CK, [pk2])
            self.cp("dve", tmp[:, 16:32], ps2[:, 0:16], [pk2], [tk])
            self.cp("act", C[:, t, 0:8], tmp[:, 16:24], [tk], [ck])
            self.act(C[:, t, 8:16], tmp[:, 16:24], AF.Exp, [tk], [ck])
            self.tt("dve", tmp[:, 32:40], tmp[:, 24:32], tmp[:, 16:24], ALU.subtract, [tk], [tk])
            self.act(C[:, t, 16:24], tmp[:, 32:40], AF.Exp, [tk], [ck])
            self.act(C[:, t, 24:32], tmp[:, 24:32], AF.Exp, [tk], [ck])

    def projT(self, wi, col, evac, src=None, srckey="xnT"):
        S = self.S
        src = self.xnT if src is None else src
        for g in range(NG):
            ps, pk = S.psum()
            for kc in range(8):
                self.mm(ps[:], self.wslots[wi][:, kc, col * 128:(col + 1) * 128], src[:, kc, g * 512:(g + 1) * 512],
                        kc == 0, kc == 7, [(srckey, g), ("w", wi)], [pk])
            evac(g, ps, pk)

    def projT_gen(self, wi, col, evac):
        S = self.S
        src = self.xnT
        for g in range(NG):
            ps, pk = S.psum()
            for kc in range(8):
                self.mm(ps[:], self.wslots[wi][:, kc, col * 128:(col + 1) * 128], src[:, kc, g * 512:(g + 1) * 512],
                        kc == 0, kc == 7, [("xnT", g), ("w", wi)], [pk])
            evac(g, ps, pk)
            yield

    def make_diag(self, cols, nk):
        t, tk = self.b5()
        dg = t[:].rearrange("p (i c) -> p i c", c=128)
        for j, c in enumerate(cols):
            self.act(dg[:, j, :], self.ident_f[:], AF.Copy, self.CK, [tk], scale=c)
        return dg, tk

    def gdn_front_gen(self, b, h):
        S = self.S
        FB = self.FB
        Win = self.WB["w_in"]
        wi = self.wslot()
        for j, c0 in enumerate((C_GQ, C_GK, C_GV, C_Z)):
            self.wload(wi, Win[:, c0 + h * 128:c0 + (h + 1) * 128], 8, 128, col_off=j * 128)
        qT, kT, vT = FB[0], FB[1], FB[2]
        ysq = FB[5]

        def k4(name):
            return [(name, g) for g in range(NG)]

        plan = ((qT, "FB0"), (kT, "FB1"), (vT, "FB2"))

        def proj(j):
            raw = self.raw[j % 2]
            rk = ("raw", j % 2)

            def ev(g, ps, pk):
                self.cp("act", raw[:, g * 512:(g + 1) * 512], ps[:], [pk], [(rk, g)])
            yield from self.projT_gen(wi, j, ev)

        def convnorm(j):
            dst, dname = plan[j]
            raw = self.raw[j % 2]
            rk = ("raw", j % 2)
            ch = j * 8 + h
            dg, dk = self.make_diag([self.convg[:, i * 24 + ch:i * 24 + ch + 1] for i in range(4)], 4)
            for g in range(NG):
                ps, pk = S.psum()
                if g == 0:
                    self.mm(ps[:], dg[:, 3, :], raw[:, 0:512], True, False, [dk, (rk, 0)], [pk])
                    for i in (2, 1, 0):
                        off = 3 - i
                        self.mm(ps[:, off:512], dg[:, i, :], raw[:, 0:512 - off], False, i == 0, [dk, (rk, 0)], [pk])
                else:
                    for i in range(4):
                        self.mm(ps[:], dg[:, i, :], raw[:, g * 512 - 3 + i:g * 512 - 3 + i + 512], i == 0, i == 3,
                                [dk, (rk, g), (rk, g - 1)], [pk])
                sg, sk_ = self.t5()
                self.sigmoid(sg[:], ps[:], sg[:], [pk], sk_, [sk_])
                self.tt("dve", dst[:, g * 512:(g + 1) * 512], ps[:], sg[:], ALU.mult, [pk, sk_], [(dname, g)])
                yield
            if j < 2:
                self.act(ysq[:], dst[:], AF.Square, k4(dname), k4("FB5"))
                for g in range(NG):
                    ps, pk = S.psum()
                    self.mm(ps[:], self.ones_b[:], ysq[:, g * 512:(g + 1) * 512], True, True, [("FB5", g)] + self.CK, [pk])
                    tmp, tk = self.t5()
                    self.act(tmp[:], ps[:], AF.Ln, [pk], [tk], bias=EPS)
                    bias = -0.5 * math.log(128.0) if j == 0 else 0.0
                    self.act(tmp[:], tmp[:], AF.Exp, [tk], [tk], scale=-0.5, bias=bias)
                    self.tt("dve", dst[:, g * 512:(g + 1) * 512], dst[:, g * 512:(g + 1) * 512], tmp[:], ALU.mult,
                            [(dname, g), tk], [(dname, g)])
                    yield
        yield from proj(0)
        yield from proj(1)
        yield from convnorm(0)
        yield from proj(2)
        yield from convnorm(1)
        yield from convnorm(2)
        self._gdn_wi = wi

    def gdn_loop_gen(self, b, h):
        S = self.S
        FB = self.FB
        qT, kT, vT, OT = FB[0], FB[1], FB[2], FB[6]
        C = self.COLS
        def rowbc(g, fld):
            ps, pk = S.psum("neu")
            for tt_ in range(4):
                t = g * 4 + tt_
                gi = self.gbrr % 2
                self.gbrr += 1
                gb = self.gbt[gi]
                self.act(gb[:], self.ones_f[:], AF.Copy, [("COLS", t)] + self.CK, [("gbt", gi)],
                         scale=C[:, t, fld * 8 + h:fld * 8 + h + 1])
                self.mm(ps[:, tt_ * 128:(tt_ + 1) * 128], gb[:], self.ident_f[:], True, True, [("gbt", gi)] + self.CK, [pk])
            return ps, pk
        S.op("pool", lambda e: e.memset(self.Sf[:], 0.0), writes=["Sf"])
        S.op("pool", lambda e: e.memset(self.Sb[:], 0.0), writes=["Sb"])
        C = self.COLS
        sl = lambda cc: slice(cc * 128, (cc + 1) * 128)
        tok = lambda c: slice(c * 128, (c + 1) * 128)
        v3 = lambda ap: ap.rearrange("p (c t) -> p c t", t=128)
        hand = {}

        def neu(bt):
            gk = bt
            par = bt % 2
            cs = [bt * 4 + cc for cc in range(4)]

            def colb(f):
                return C[:, bt * 4:(bt + 1) * 4, f * 8 + h:f * 8 + h + 1].to_broadcast([128, 4, 128])
            ckeys = [("COLS", c) for c in cs]
            pk_, pkk_ = S.psum("neu")
            for cc, c in enumerate(cs):
                self.mm(pk_[:, sl(cc)], kT[:, tok(c)], self.ident_b[:], True, True, [("FB1", gk)] + self.CK, [pkk_])
            Kg, Kgk = self.G5[0], 'G50'
            Kd, Kdk = self.H5[par][0], ('H5', par, 0)
            self.tt("dve", v3(Kg[:]), v3(pk_[:]), colb(1), ALU.mult, [pkk_] + ckeys, [Kgk])
            self.tt("dve", v3(Kd[:]), v3(pk_[:]), colb(2), ALU.mult, [pkk_] + ckeys, [Kdk])
            pv_, pvk_ = S.psum("neu")
            for cc, c in enumerate(cs):
                self.mm(pv_[:, sl(cc)], vT[:, tok(c)], self.ident_b[:], True, True, [("FB2", gk)] + self.CK, [pvk_])
            Vt, Vtk = self.G5[2], 'G52'
            self.cp("act", Vt[:], pv_[:], [pvk_], [Vtk])
            yield
            peg, pegk = rowbc(bt, 1)
            QdT, QdTk = self.H5[par][3], ('H5', par, 3)
            self.tt("dve", QdT[:], qT[:, bt * 512:(bt + 1) * 512], peg[:], ALU.mult, [pegk, ("FB0", bt)], [QdTk])
            yield
            E, Ek = self.t5()
            pgc, pgck = rowbc(bt, 0)
            for cc, c in enumerate(cs):
                self.stt(E[:, sl(cc)], pgc[:, sl(cc)], C[:, c, h:h + 1], self.negmaskT[:], ALU.subtract, ALU.add,
                         [pgck, ("COLS", c)] + self.CK, [Ek])
            self.act(E[:], E[:], AF.Exp, [Ek], [Ek])
            yield
            QKm, QKmk = self.H5[par][1], ('H5', par, 1)
            pqk, pqkk = S.psum("neu")
            for cc, c in enumerate(cs):
                self.mm(pqk[:, sl(cc)], kT[:, tok(c)], qT[:, tok(c)], True, True, [("FB1", gk), ("FB0", gk)], [pqkk])
            self.tt("dve", QKm[:], pqk[:], E[:], ALU.mult, [pqkk, Ek], [QKmk])
            Es, Esk = self.t5()
            S.op("pool", lambda e: e.affine_select(out=v3(Es[:]), in_=v3(E[:]), pattern=[[0, 4], [1, 128]],
                                                    compare_op=ALU.is_gt, fill=0.0, base=0, channel_multiplier=-1),
                 reads=[Ek], writes=[Esk])
            yield
            X, Xk = self.t5()
            pkk, pkkk = S.psum("neu")
            for cc, c in enumerate(cs):
                self.mm(pkk[:, sl(cc)], kT[:, tok(c)], kT[:, tok(c)], True, True, [("FB1", gk)], [pkkk])
            for cc, c in enumerate(cs):
                self.stt(X[:, sl(cc)], pkk[:, sl(cc)], C[:, c, 32 + h:32 + h + 1], Es[:, sl(cc)], ALU.mult, ALU.mult,
                         [pkkk, Esk, ("COLS", c)], [Xk])
            yield
            pt, ptk = S.psum("neu")
            for cc in range(4):
                self.mm(pt[:, sl(cc)], X[:, sl(cc)], self.ident_f[:], True, True, [Xk] + self.CK, [ptk])
            XT, XTk = self.t5()
            self.cp("act", XT[:], pt[:], [ptk], [XTk])
            yield
            Y, Yk = self.t5()
            self.tt("dve", v3(Y[:]), self.ident_f[:].unsqueeze(1).to_broadcast([128, 4, 128]), v3(X[:]), ALU.subtract,
                    [Xk] + self.CK, [Yk])
            for p in (2, 4, 8, 16, 32, 64):
                pb, pbk = S.psum("neu")
                for cc in range(4):
                    self.mm(pb[:, sl(cc)], X[:, sl(cc)], XT[:, sl(cc)], True, True, [Xk, XTk], [pbk])
                XTn, XTnk = self.t5()
                if p < 64:
                    pa, pak = S.psum("neu")
                    for cc in range(4):
                        self.mm(pa[:, sl(cc)], XT[:, sl(cc)], X[:, sl(cc)], True, True, [Xk, XTk], [pak])
                    Xn, Xnk = self.t5()
                    self.cp("act", Xn[:], pa[:], [pak], [Xnk])
                self.cp("dve", XTn[:], pb[:], [pbk], [XTnk])
                if p < 64:
                    X, Xk = Xn, Xnk
                XT, XTk = XTn, XTnk
                yield
                py, pyk = S.psum("neu")
                for cc in range(4):
                    self.mm(py[:, sl(cc)], XT[:, sl(cc)], Y[:, sl(cc)], True, True, [XTk, Yk], [pyk])
                Yn, Ynk = self.t5()
                self.tt("dve", Yn[:], py[:], Y[:], ALU.add, [pyk, Yk], [Ynk])
                Y, Yk = Yn, Ynk
                yield
            Yb, Ybk = self.b5()
            self.cp("act", Yb[:], Y[:], [Yk], [Ybk])
            Y, Yk = Yb, Ybk
            yield
            pu, puk = S.psum("neu")
            for cc in range(4):
                self.mm(pu[:, sl(cc)], Y[:, sl(cc)], Vt[:, sl(cc)], True, True, [Yk, Vtk], [puk])
            Upp, Uppk = self.H5[par][4], ('H5', par, 4)
            self.tt("dve", v3(Upp[:]), v3(pu[:]), colb(4), ALU.mult, [puk] + ckeys, [Uppk])
            pw, pwk = S.psum("neu")
            for cc in range(4):
                self.mm(pw[:, sl(cc)], Kg[:, sl(cc)], Y[:, sl(cc)], True, True, [Kgk, Yk], [pwk])
            WT, WTk = self.H5[par][2], ('H5', par, 2)
            self.cp("act", WT[:], pw[:], [pwk], [WTk])
            hand[bt] = (Kd, Kdk, QdT, QdTk, QKm, QKmk, Upp, Uppk, WT, WTk)
            yield

        def rec(bt):
            cs = [bt * 4 + cc for cc in range(4)]
            Kd, Kdk, QdT, QdTk, QKm, QKmk, Upp, Uppk, WT, WTk = hand[bt]
            po, pok = S.psum_fixed(6)
            for cc, c in enumerate(cs):
                pws, pwsk = S.psum("rec")
                self.mm(pws[:, 0:128], WT[:, sl(cc)], self.Sb[:], True, True, [WTk, "Sb"], [pwsk])
                Vn = self.Vn[c % 2]
                Vnk = ("Vn", c % 2)
                self.stt(Vn[:], pws[:, 0:128], C[:, c, 40 + h:40 + h + 1], Upp[:, sl(cc)], ALU.mult, ALU.add,
                         [pwsk, Uppk, ("COLS", c)], [Vnk])
                yield
                self.mm(po[:, sl(cc)], self.Sb[:], QdT[:, sl(cc)], True, False, ["Sb", QdTk], [pok])
                self.mm(po[:, sl(cc)], Vn[:], QKm[:, sl(cc)], False, True, [Vnk, QKmk], [pok])
                psn, psnk = S.psum("rec")
                self.mm(psn[:, 0:128], Kd[:, sl(cc)], Vn[:], True, True, [Kdk, Vnk], [psnk])
                self.stt(self.Sf[:], self.Sf[:], C[:, c, 24 + h:24 + h + 1], psn[:, 0:128], ALU.mult, ALU.add,
                         [psnk, "Sf", ("COLS", c)], ["Sf"])
                self.cp("act", self.Sb[:], self.Sf[:], ["Sf"], ["Sb"])
                yield
            self.cp("act", OT[:, bt * 512:(bt + 1) * 512], po[:], [pok], [("FB6", bt)])
            yield

        def inter(gs):
            gs = list(gs)
            while gs:
                for gn in list(gs):
                    try:
                        next(gn)
                        yield
                    except StopIteration:
                        gs.remove(gn)
        for bt in range(NG):
            gs = [neu(bt)] + ([rec(bt - 1)] if bt > 0 else [])
            yield from inter(gs)
        yield from inter([rec(NG - 1)])

    def gdn_tail_gen(self, b, h):
        S = self.S
        FB = self.FB
        OT, zs, ysq = FB[6], FB[3], FB[4]
        wi = self._gdn_wi

        def k4(name):
            return [(name, g) for g in range(NG)]

        def evz(g, ps, pk):
            sg, sk_ = self.t5()
            self.sigmoid(sg[:], ps[:], sg[:], [pk], sk_, [sk_])
            self.tt("dve", zs[:, g * 512:(g + 1) * 512], ps[:], sg[:], ALU.mult, [pk, sk_], [("FB3", g)])
        yield from self.projT_gen(wi, 3, evz)
        self.act(ysq[:], OT[:], AF.Square, k4("FB6"), k4("FB4"))
        gn = self.gcol("gdn_out_norm", 0)
        for g in range(NG):
            ps, pk = S.psum()
            self.mm(ps[:], self.ones_b[:], ysq[:, g * 512:(g + 1) * 512], True, True, [("FB4", g)] + self.CK, [pk])
            tmp, tk = self.t5()
            self.act(tmp[:], ps[:], AF.Ln, [pk], [tk], scale=1.0 / 128, bias=EPS)
            self.act(tmp[:], tmp[:], AF.Exp, [tk], [tk], scale=-0.5)
            t2, t2k = self.t5()
            self.stt(t2[:], OT[:, g * 512:(g + 1) * 512], gn, tmp[:], ALU.mult, ALU.mult, [("FB6", g), tk] + self.CK, [t2k])
            self.tt("dve", self.oaT[:, h, g * 512:(g + 1) * 512], t2[:], zs[:, g * 512:(g + 1) * 512], ALU.mult,
                    [t2k, ("FB3", g)], [("oaT", g)])
            yield

    def sb_front(self, b, h):
        S = self.S
        FB = self.FB
        Win = self.WB["w_in"]
        wi = self.wslot()
        for j, c0 in enumerate((C_SQ, C_SK, C_SV)):
            self.wload(wi, Win[:, c0 + h * 128:c0 + (h + 1) * 128], 8, 128, col_off=j * 128)
        sqT, skT = FB[3], FB[4]
        scl = 128.0 ** -0.5

        def evq(g, ps, pk):
            self.act(sqT[:, g * 512:(g + 1) * 512], ps[:], AF.Copy, [pk], [("FB3", g)], scale=scl)
        self.projT(wi, 0, evq)

        def evk(g, ps, pk):
            self.cp("act", skT[:, g * 512:(g + 1) * 512], ps[:], [pk], [("FB4", g)])
        self.projT(wi, 1, evk)
        for t4 in range(4):
            ps, pk = S.psum()
            for tt_ in range(4):
                t = t4 * 4 + tt_
                for kc in range(8):
                    self.mm(ps[:, tt_ * 128:(tt_ + 1) * 128], self.xnT[:, kc, t * 128:(t + 1) * 128],
                            self.wslots[wi][:, kc, 256:384], kc == 0, kc == 7, [("xnT", t4), ("w", wi)], [pk])
            self.cp("dve", self.sv[:, t4 * 4:(t4 + 1) * 4, :].rearrange("p c t -> p (c t)"), ps[:], [pk], [("FB5", t4)])

    def sb_loop_gen(self, b, h):
        S = self.S
        FB = self.FB
        sqT, skT = FB[3], FB[4]
        for qg in range(NG):
            acc, acck = S.psum_fixed(7)
            carry, ck = self.carry, 'carry'
            nsq, nsqk = self.nsq, 'nsq'
            self.act(nsq[:], sqT[:, qg * 512:(qg + 1) * 512], AF.Copy, [("FB3", qg)], [nsqk], scale=-1.0)
            S.op("pool", lambda e: e.memset(carry[:], 0.0), writes=[ck])
            blocks = list(range(4 * qg + 3, -1, -1))
            st1 = {}

            def stage1(kb):
                r = kb - 4 * qg
                c0 = max(r, 0) * 128
                pz, pzk = S.psum("sb")
                self.mm(pz[:, c0:512], skT[:, kb * 128:(kb + 1) * 128], sqT[:, qg * 512 + c0:(qg + 1) * 512], True, True,
                        [("FB4", kb // 4), ("FB3", qg)], [pzk])
                e, ek = self.b5()
                self.act(e[:, c0:512], pz[:, c0:512], AF.Exp, [pzk], [ek])
                if r >= 0:
                    S.op("pool", lambda en: en.affine_select(out=e[:, c0:c0 + 128], in_=e[:, c0:c0 + 128], pattern=[[1, 128]],
                                                              compare_op=ALU.is_gt, fill=0.0, base=0, channel_multiplier=-1),
                         reads=[ek], writes=[ek])
                sp, spk = self.b5()
                self.act(sp[:, c0:512], e[:, c0:512], AF.Ln, [ek], [spk], bias=1.0)
                st1[kb] = (r, c0, sp, spk)

            st2 = {}

            def stage2a(kb, first):
                r, c0, sp, spk = st1.pop(kb)
                pc, pck = S.psum("sb")
                self.mm(pc[:, c0:512], self.LTincl_b[:], sp[:, c0:512], True, False, [spk] + self.CK, [pck])
                if not first:
                    self.mm(pc[:, c0:512], self.ones_b[:], carry[:, c0:512], False, False, [ck] + self.CK, [pck])
                self.mm(pc[:, c0:512], skT[:, kb * 128:(kb + 1) * 128], nsq[:, c0:512], False, True,
                        [("FB4", kb // 4), nsqk], [pck])
                w, wk = self.b5()
                self.act(w[:, c0:512], pc[:, c0:512], AF.Exp, [pck], [wk], scale=-1.0)
                if r >= 0:
                    S.op("pool", lambda en: en.affine_select(out=w[:, c0:c0 + 128], in_=w[:, c0:c0 + 128], pattern=[[1, 128]],
                                                              compare_op=ALU.is_gt, fill=0.0, base=0, channel_multiplier=-1),
                         reads=[wk], writes=[wk])
                st2[kb] = (r, c0, sp, spk, w, wk)

            def stage2b(kb, first, last):
                r, c0, sp, spk, w, wk = st2.pop(kb)
                self.mm(acc[:, c0:512], self.sv[:, kb, :], w[:, c0:512], first, last, [("FB5", kb // 4), wk], [acck])
                if not last:
                    self.tt("dve", carry[:, c0:512], carry[:, c0:512], sp[:, c0:512], ALU.add, [ck, spk], [ck])

            stage1(blocks[0])
            for n, kb in enumerate(blocks):
                if n + 1 < len(blocks):
                    stage1(blocks[n + 1])
                    yield
                stage2a(kb, n == 0)
                yield
                stage2b(kb, n == 0, n == len(blocks) - 1)
                yield
            self.cp("dve", self.obT[:, h, qg * 512:(qg + 1) * 512], acc[:], [acck], [("obT", qg)])

    def run_pipelined(self, gens, depth=2):
        active = []
        it = iter(gens)
        done = False
        while True:
            while not done and len(active) < depth:
                try:
                    active.append(next(it))
                except StopIteration:
                    done = True
            if not active:
                break
            for gn in list(active):
                try:
                    next(gn)
                except StopIteration:
                    active.remove(gn)

    def ffn_chunk_gen(self, g, c_lo, q0, nq, f, hnT, hkeys):
        S = self.S
        Wu = self.WB["w_up"]
        if f == 0:
            self._ffn_wis = []
            for which in range(2):
                wi = self.wslot()
                col = which * DFF + (c_lo + q0) * 128
                self.wload(wi, Wu[:, col:col + nq * 128], 8, nq * 128)
                self._ffn_wis.append(wi)
        wis = list(self._ffn_wis)
        c = c_lo + q0 + f
        raws = []
        for which in range(2):
            wi = wis[which]
            ch = which * 22 + c
            ps, pk = S.psum()
            for kc in range(8):
                self.mm(ps[:], self.wslots[wi][:, kc, f * 128:(f + 1) * 128], hnT[kc // 4][:, kc % 4, :],
                        kc == 0, kc == 7, hkeys + [("w", wi)], [pk])
            ri = self.rawcrr % 6
            self.rawcrr += 1
            rc = self.rawc[ri]
            rck = ("rawc", ri)
            hk = ("halo", ch)
            self.cp("dve", rc[:, 2:4], self.halo[:, ch, :], [hk], [rck])
            self.cp("act", rc[:, 4:516], ps[:], [pk], [rck])
            self.cp("dve", self.halo[:, ch, :], rc[:, 514:516], [rck], [hk])
            dg, dk = self.make_diag([self.convf[:, i * 44 + ch:i * 44 + ch + 1] for i in range(3)], 3)
            raws.append((rc, rck, dg, dk))
        yield
        cps = []
        for which in range(2):
            rc, rck, dg, dk = raws[which]
            pc, pck = S.psum()
            for i in range(3):
                self.mm(pc[:], dg[:, i, :], rc[:, 2 + i:2 + i + 512], i == 0, i == 2, [dk, rck], [pck])
            cps.append((pc, pck))
        sg, sgk = self.t5()
        self.sigmoid(sg[:], cps[0][0][:], sg[:], [cps[0][1]], sgk, [sgk])
        t2, t2k = self.t5()
        self.tt("dve", t2[:], cps[0][0][:], sg[:], ALU.mult, [cps[0][1], sgk], [t2k])
        a_ap, a_key = self.actT_chunk(q0 + f)
        self.tt("dve", a_ap, cps[1][0][:], t2[:], ALU.mult, [cps[1][1], t2k], [a_key])
        yield

    def down_proj_add(self, nkc, k_base):
        S = self.S
        W = self.WB["w_down"]
        for half in range(2):
            slots = []
            k = 0
            while k < nkc:
                nk = min(8, nkc - k)
                wi = self.wslot()
                self.wload(wi, W[(k_base + k) * 128:(k_base + k + nk) * 128, half * 512:(half + 1) * 512], nk, 512)
                slots.append((wi, k, nk))
                k += nk
            for tt_ in range(4):
                ps, pk = S.psum()
                first = True
                for (wi, k0, nk) in slots:
                    for kk in range(nk):
                        kc = k0 + kk
                        a_ap, a_key = self.actT_chunk(kc)
                        self.mm(ps[:], a_ap[:, tt_ * 128:(tt_ + 1) * 128], self.wslots[wi][:, kk, :], first,
                                kc == nkc - 1, [a_key, ("w", wi)], [pk])
                        first = False
                self.tt("dve", self.h[:, tt_, half * 512:(half + 1) * 512], ps[:], self.h[:, tt_, half * 512:(half + 1) * 512],
                        ALU.add, [pk, ("h", tt_)], [("h", tt_)])

    def stage_C(self, b, g):
        S = self.S
        FB = self.FB
        gs = slice(g * 512, (g + 1) * 512)
        v8 = lambda t: t[:].rearrange("p (c t) -> p c t", t=512)
        for tt_ in range(4):
            S.dma("sp", self.h[:, tt_, :], self.x[b, g * 512 + tt_ * 128:g * 512 + (tt_ + 1) * 128, :], self.hsem[tt_],
                  writes=[("h", tt_)])
        xnG = [v8(FB[4]), v8(FB[5])]
        xkeys = [("FB4", f) for f in range(4)] + [("FB5", f) for f in range(4)]
        self.norm_group(xnG, xkeys, "norm_mix")
        mT = [v8(FB[0]), v8(FB[1])]
        sig = v8(FB[2])
        m1 = v8(FB[3])
        Win = self.WB["w_in"]
        for quad in range(2):
            cs = slice(quad * 512, (quad + 1) * 512)
            for (gate_c0, pw, srcT, skey) in ((C_GA, "w_proj_gdn", self.oaT, "oaT"), (C_GB, "w_proj_sb", self.obT, "obT")):
                wi = self.wslot()
                self.wload(wi, Win[:, gate_c0 + quad * 512:gate_c0 + (quad + 1) * 512], 8, 512)
                for f in range(4):
                    ps, pk = S.psum()
                    for kc in range(8):
                        self.mm(ps[:], self.wslots[wi][:, kc, f * 128:(f + 1) * 128], xnG[kc // 4][:, kc % 4, :], kc == 0, kc == 7,
                                xkeys + [("w", wi)], [pk])
                    tmp, tk = self.t5()
                    self.sigmoid(sig[:, f, :], ps[:], tmp[:], [pk], tk, [("FB2", f)])
                wi = self.wslot()
                self.wload(wi, self.WB[pw][:, cs], 8, 512)
                for f in range(4):
                    ps, pk = S.psum()
                    for kc in range(8):
                        self.mm(ps[:], self.wslots[wi][:, kc, f * 128:(f + 1) * 128], srcT[:, kc, gs], kc == 0, kc == 7,
                                [(skey, g), ("w", wi)], [pk])
                    if gate_c0 == C_GA:
                        self.tt("dve", m1[:, f, :], ps[:], sig[:, f, :], ALU.mult, [pk, ("FB2", f)], [("FB3", f)])
                    else:
                        tmp, tk = self.t5()
                        self.tt("dve", tmp[:], ps[:], sig[:, f, :], ALU.mult, [pk, ("FB2", f)], [tk])
                        self.tt("dve", mT[quad][:, f, :], tmp[:], m1[:, f, :], ALU.add, [tk, ("FB3", f)], [("FB%d" % quad, f)])
        mTfull_keys = [("FB0", f) for f in range(4)] + [("FB1", f) for f in range(4)]

        class _TwoBuf:
            def __init__(s, a, b_):
                s.a, s.b = a, b_

            def __getitem__(s, idx):
                p, kc, t = idx
                return (s.a if kc < 4 else s.b)[p, kc % 4, t]
        hnT = [v8(FB[2]), v8(FB[3])]
        hkeys = [("FB2", f) for f in range(4)] + [("FB3", f) for f in range(4)]
        self.tokproj_norm(_TwoBuf(mT[0], mT[1]), mTfull_keys, "w_out", 8, hnT, hkeys, "norm_x")
        if self.dbg and b == 0:
            for tt_ in range(4):
                S.dma("pool", self.dbg_h1[g * 512 + tt_ * 128:g * 512 + (tt_ + 1) * 128, :], self.h[:, tt_, :], self.dbgsem[tt_],
                      reads=[("h", tt_)], writes=["dbgo"])
        qraw = [v8(FB[0]), v8(FB[1])]
        qsq = [v8(FB[4]), v8(FB[5])]
        Wq = self.WB["w_xq"]
        for quad in range(2):
            wi = self.wslot()
            self.wload(wi, Wq[:, quad * 512:(quad + 1) * 512], 8, 512)
            for f in range(4):
                ps, pk = S.psum()
                for kc in range(8):
                    self.mm(ps[:], self.wslots[wi][:, kc, f * 128:(f + 1) * 128], hnT[kc // 4][:, kc % 4, :], kc == 0, kc == 7,
                            hkeys + [("w", wi)], [pk])
                self.cp("act", qraw[quad][:, f, :], ps[:], [pk], [("FB%d" % quad, f)])
                self.act(qsq[quad][:, f, :], ps[:], AF.Square, [pk], [("FB%d" % (4 + quad), f)])
        oxT = [v8(FB[2]), v8(FB[3])]
        def xa_head(hx):
            quad, f0 = hx // 2, (hx % 2) * 2
            stream = 'xa%d' % (hx % 2)
            ps, pk = S.psum(stream)
            for dc in range(2):
                self.mm(ps[:], self.ones_b[:], qsq[quad][:, f0 + dc, :], dc == 0, dc == 1,
                        [("FB%d" % (4 + quad), f0 + dc)] + self.CK, [pk])
            rs, rsk = self.t5()
            self.act(rs[:], ps[:], AF.Ln, [pk], [rsk], scale=1.0 / 256, bias=EPS)
            self.act(rs[:], rs[:], AF.Exp, [rsk], [rsk], scale=-0.5, bias=-0.5 * math.log(16.0) * 2)
            yield
            qn = []
            for dc in range(2):
                qb, qbk = self.b5()
                self.stt(qb[:], qraw[quad][:, f0 + dc, :], self.gcol("xq_norm", dc), rs[:], ALU.mult, ALU.mult,
                         [("FB%d" % quad, f0 + dc), rsk] + self.CK, [qbk])
                qn.append((qb, qbk))
            yield
            PT = []
            for mt in range(2):
                pz, pzk = S.psum(stream)
                for dc in range(2):
                    self.mm(pz[:], self.KT[:, hx * 2 + dc, mt * 128:(mt + 1) * 128], qn[dc][0][:], dc == 0, dc == 1,
                            ["KT", qn[dc][1]], [pzk])
                pt_, ptk = self.b5()
                self.act(pt_[:], pz[:], AF.Exp, [pzk], [ptk])
                PT.append((pt_, ptk))
            yield
            pd, pdk = S.psum(stream)
            for mt in range(2):
                self.mm(pd[:], self.ones_b[:], PT[mt][0][:], mt == 0, mt == 1, [PT[mt][1]] + self.CK, [pdk])
            rd, rdk = self.t5()
            S.op("dve", lambda e: e.reciprocal(out=rd[:], in_=pd[:]), reads=[pdk], writes=[rdk])
            yield
            for dc in range(2):
                po_, pok_ = S.psum(stream)
                for mt in range(2):
                    self.mm(po_[:], self.Vm[:, mt, hx * 256 + dc * 128:hx * 256 + (dc + 1) * 128], PT[mt][0][:], mt == 0, mt == 1,
                            ["Vm", PT[mt][1]], [pok_])
                self.tt("dve", oxT[quad][:, f0 + dc, :], po_[:], rd[:], ALU.mult, [pok_, rdk], [("FB%d" % (2 + quad), f0 + dc)])
        self.run_pipelined([xa_head(hx) for hx in range(4)], 2)
        hnT2 = [v8(FB[0]), v8(FB[1])]
        hkeys2 = [("FB0", f) for f in range(4)] + [("FB1", f) for f in range(4)]
        self.tokproj_norm(_TwoBuf(oxT[0], oxT[1]), hkeys, "w_xo", 8, hnT2, hkeys2, "norm_ffn")
        if self.dbg and b == 0:
            for tt_ in range(4):
                S.dma("pool", self.dbg_h2[g * 512 + tt_ * 128:g * 512 + (tt_ + 1) * 128, :], self.h[:, tt_, :], self.dbgsem[tt_],
                      reads=[("h", tt_)], writes=["dbgo"])
        hnT, hkeys = hnT2, hkeys2
        Wu = self.WB["w_up"]
        if g == 0:
            S.op("pool", lambda e: e.memset(self.halo[:], 0.0), writes=[("halo", c_) for c_ in range(44)])
        S.n_rr = 8
        for half in range(2):
            c_lo = half * 11
            gens = []
            for q0 in range(0, 11, 4):
                nq = min(4, 11 - q0)
                for f in range(nq):
                    gens.append(self.ffn_chunk_gen(g, c_lo, q0, nq, f, hnT, hkeys))
            self.run_pipelined(gens, 3)
            self.down_proj_add(11, c_lo)
        for tt_ in range(4):
            S.dma("pool", self.out[b, g * 512 + tt_ * 128:g * 512 + (tt_ + 1) * 128, :], self.h[:, tt_, :], self.osem[tt_],
                  reads=[("h", tt_)], writes=[("out", tt_)])

    def _tokproj_multi(self, srcT, srckeys, wname, nkc):
        S = self.S
        W = self.WB[wname]
        for half in range(2):
            wi = self.wslot()
            self.wload(wi, W[:, half * 512:(half + 1) * 512], 8, 512)
            for tt_ in range(4):
                ps, pk = S.psum()
                for kc in range(nkc):
                    self.mm(ps[:], srcT[:, kc, tt_ * 128:(tt_ + 1) * 128], self.wslots[wi][:, kc, :], kc == 0, kc == nkc - 1,
                            srckeys + [("w", wi)], [pk])
                self.tt("dve", self.h[:, tt_, half * 512:(half + 1) * 512], ps[:], self.h[:, tt_, half * 512:(half + 1) * 512],
                        ALU.add, [pk, ("h", tt_)], [("h", tt_)])

    def norm_group(self, dstT2, dkeys, gname):
        for tt_ in range(4):
            self.norm_tile(tt_, dstT2, dkeys, gname)

    def norm_tile(self, tt_, dstT2, dkeys, gname):
        S = self.S
        st = self.stat
        src = self.h[:, tt_, :]
        sk = [("h", tt_)]
        i = 0
        xh = self.xh[i]
        self.act(xh[:], src, AF.Square, sk, [("xh", i), "stat"], accum_out=st[:, 0:1])
        self.act(st[:, 1:2], st[:, 0:1], AF.Ln, ["stat"], ["stat"], scale=1.0 / 1024, bias=EPS)
        self.act(st[:, 2:3], st[:, 1:2], AF.Exp, ["stat"], ["stat"], scale=-0.5)
        self.act(xh[:], src, AF.Copy, sk + ["stat"], [("xh", i)], scale=st[:, 2:3])
        for half in range(2):
            ps, pk = S.psum()
            for c4 in range(4):
                c = half * 4 + c4
                self.mm(ps[:, c4 * 128:(c4 + 1) * 128], xh[:, c * 128:(c + 1) * 128], self.ident_b[:], True, True,
                        [("xh", i)] + self.CK, [pk])
            o = self.col_off[gname] + half * 4
            gb = self.colsR1[:, o:o + 4].unsqueeze(2).to_broadcast([128, 4, 128])
            self.tt("dve", dstT2[half][:, :, tt_ * 128:(tt_ + 1) * 128], ps[:].rearrange("p (c t) -> p c t", t=128), gb,
                    ALU.mult, [pk] + self.CK, dkeys[half * 4:(half + 1) * 4])

    def tokproj_norm(self, srcT, srckeys, wname, nkc, dstT2, dkeys, gname):
        S = self.S
        W = self.WB[wname]
        wis = []
        for half in range(2):
            wi = self.wslot()
            self.wload(wi, W[:, half * 512:(half + 1) * 512], 8, 512)
            wis.append(wi)

        def proj(tt_):
            for half in range(2):
                wi = wis[half]
                ps, pk = S.psum()
                for kc in range(nkc):
                    self.mm(ps[:], srcT[:, kc, tt_ * 128:(tt_ + 1) * 128], self.wslots[wi][:, kc, :], kc == 0, kc == nkc - 1,
                            srckeys + [("w", wi)], [pk])
                self.tt("dve", self.h[:, tt_, half * 512:(half + 1) * 512], ps[:], self.h[:, tt_, half * 512:(half + 1) * 512],
                        ALU.add, [pk, ("h", tt_)], [("h", tt_)])
        proj(0)
        proj(1)
        self.norm_tile(0, dstT2, dkeys, gname)
        proj(2)
        self.norm_tile(1, dstT2, dkeys, gname)
        proj(3)
        self.norm_tile(2, dstT2, dkeys, gname)
        self.norm_tile(3, dstT2, dkeys, gname)

    def build(self, stages=("mem", "A", "B", "C")):
        S = self.S
        self.setup_weights()
        self.setup_consts()
        self.setup_tiles()
        for b in range(self.nseq):
            self.fence()
            if "A" in stages:
                self.stage_A(b)
            S.n_rr = 6
            self.fence()
            if "B" in stages or "P" in stages:
                self.stage_gdn_prep(b)
            doG = "B" in stages or "G" in stages
            doS = "B" in stages or "S" in stages
            if doG:
                self.run_pipelined([self.gdn_front_gen(b, 0)], 1)
            for h in range(self.nheads):
                gens = []
                if doS:
                    self.sb_front(b, h)
                    gens.append(self.sb_loop_gen(b, h))
                if doG:
                    gens.append(self.gdn_loop_gen(b, h))
                self.run_pipelined(gens, 2)
                if doG:
                    g2 = [self.gdn_tail_gen(b, h)]
                    if h + 1 < self.nheads:
                        g2.append(self.gdn_front_gen(b, h + 1))
                    self.run_pipelined(g2, 2)
                self.emit_casts(12)
            if "B" in stages:
                if self.dbg and b == 0:
                    S.dma("pool", self.dbg_oa, self.oaT[:].rearrange("p c t -> p (c t)"), self.dbgsem[0],
                          reads=[("oaT", g) for g in range(NG)], writes=["dbgo"])
                    S.dma("pool", self.dbg_ob, self.obT[:].rearrange("p c t -> p (c t)"), self.dbgsem[1],
                          reads=[("obT", g) for g in range(NG)], writes=["dbgo"])
            self.fence()
            self.emit_casts(len(self.cast_list))
            if "C" in stages:
                S.n_rr = 8
                self.stage_mem(b)
                for g in range(NG):
                    self.stage_C(b, g)
        fin = [("out", t) for t in range(4)] + ["dbgo"]
        S.wait_all("pool", fin)
        S.wait_all("sp", fin)
        self.counts = {k: v.n for k, v in S.eng.items()}
        S.close()
        return self.nc


_CACHE = {}


def kernel(**inputs):
    n_cores = 8
    nseq = 4
    if "nc" not in _CACHE:
        _CACHE["nc"] = Builder(nseq=nseq, dbg=False).build()
    nc = _CACHE["nc"]
    x = np.ascontiguousarray(np.asarray(inputs["x"], dtype=np.float32))
    mem = np.ascontiguousarray(np.asarray(inputs["mem"], dtype=np.float32))
    params = {n: np.ascontiguousarray(np.asarray(inputs[n], dtype=np.float32)) for n in PARAM_NAMES}
    in_maps = []
    for c in range(n_cores):
        m = {"x": x[c * nseq:(c + 1) * nseq], "mem": mem[c * nseq:(c + 1) * nseq]}
        m.update(params)
        in_maps.append(m)
    res = run_bass_kernel_spmd(nc, in_maps, core_ids=list(range(n_cores)))
    return np.concatenate([r["out"] for r in res.results], axis=0).astype(np.float32)
```

```python
import contextlib
import math
import numpy as np
import concourse.bass as bass
import concourse.mybir as mybir
from concourse.bass_utils import run_bass_kernel_spmd

F32 = mybir.dt.float32
BF16 = mybir.dt.bfloat16
AF = mybir.ActivationFunctionType
ALU = mybir.AluOpType

D = 1024
SEQ = 2048
NT = 16
NG = 4
NH = 8
DFF = 2816
MEM = 256
IN_W = 9232
C_GQ, C_GK, C_GV, C_A, C_Z, C_SQ, C_SK, C_SV, C_GA, C_GB = 0, 1024, 2048, 3072, 3088, 4112, 5136, 6160, 7184, 8208
EPS = 1e-6
NEG = -30000.0

PARAM_NAMES = ["norm_mix", "w_in", "conv_gdn", "a_log", "dt_bias", "gdn_out_norm", "w_proj_gdn",
               "w_proj_sb", "w_out", "norm_x", "norm_mem", "w_xq", "w_xkv", "xq_norm", "xk_norm",
               "w_xo", "norm_ffn", "w_up", "conv_ffn", "w_down"]
PARAM_SHAPES = {
    "norm_mix": [1, 1024], "w_in": [1, 1024, IN_W], "conv_gdn": [1, 4, 3072], "a_log": [1, 8],
    "dt_bias": [1, 8], "gdn_out_norm": [1, 128], "w_proj_gdn": [1, 1024, 1024],
    "w_proj_sb": [1, 1024, 1024], "w_out": [1, 1024, 1024], "norm_x": [1, 1024],
    "norm_mem": [1, 1024], "w_xq": [1, 1024, 1024], "w_xkv": [1, 1024, 2048], "xq_norm": [1, 256],
    "xk_norm": [1, 256], "w_xo": [1, 1024, 1024], "norm_ffn": [1, 1024], "w_up": [1, 1024, 2 * DFF],
    "conv_ffn": [1, 3, 2 * DFF], "w_down": [1, DFF, 1024],
}
BIG_W = ["w_in", "w_proj_gdn", "w_proj_sb", "w_out", "w_xq", "w_xkv", "w_xo", "w_up", "w_down"]


class _Eng:
    def __init__(self, name, h, sem):
        self.name = name
        self.h = h
        self.sem = sem
        self.n = 0
        self.waited = {}


class DmaSem:
    def __init__(self, sem):
        self.sem = sem
        self.total = 0


class Sched:
    def __init__(self, nc):
        self.nc = nc
        self.es = contextlib.ExitStack()
        self.eng = {}
        for name, h in (("pe", nc.tensor), ("act", nc.scalar), ("dve", nc.vector),
                        ("pool", nc.gpsimd), ("sp", nc.sync)):
            sem = self.es.enter_context(nc.semaphore("s_" + name))
            self.eng[name] = _Eng(name, h, sem)
        self.lastw = {}
        self.readers = {}
        self.nsem = 0
        self.ntile = 0
        self.psum_tiles = []
        self.psum_rr = 0
        self.n_rr = 6
        self.stream_rr = {}

    def dsem(self):
        self.nsem += 1
        return DmaSem(self.es.enter_context(self.nc.semaphore("d%d" % self.nsem)))

    def sb(self, shape, dtype, name=None):
        self.ntile += 1
        nm = "%s_%d" % (name or "t", self.ntile)
        return self.es.enter_context(self.nc.sbuf_tensor(nm, list(shape), dtype))

    def init_psum(self, n=8, n_rr=6):
        for i in range(n):
            t = self.es.enter_context(self.nc.psum_tensor("ps%d" % i, [128, 512], F32))
            self.psum_tiles.append(t)
        self.n_rr = n_rr

    STREAM_BANKS = {"neu": [0, 1, 2], "rec": [3], "sb": [4, 5], "xa0": [0, 1, 2, 3], "xa1": [4, 5, 6, 7]}

    def psum(self, stream=None):
        if stream is None:
            i = self.psum_rr % self.n_rr
            self.psum_rr += 1
        else:
            banks = self.STREAM_BANKS[stream]
            cnt = self.stream_rr.get(stream, 0)
            self.stream_rr[stream] = cnt + 1
            i = banks[cnt % len(banks)]
        return self.psum_tiles[i], ("ps", i)

    def psum_fixed(self, i):
        return self.psum_tiles[i], ("ps", i)

    def _collect(self, eng, reads, writes):
        waits = {}

        def need(dep, raw):
            if dep[0] == "eng":
                _, fname, cnt = dep
                if fname == eng:
                    if eng in ("pe", "sp") or not raw:
                        return
                F = self.eng[fname]
                key = id(F.sem)
                if waits.get(key, (None, 0))[1] < cnt:
                    waits[key] = (F.sem, cnt)
            else:
                _, ds, val = dep
                key = id(ds.sem)
                if waits.get(key, (None, 0))[1] < val:
                    waits[key] = (ds.sem, val)

        for k in reads:
            d = self.lastw.get(k)
            if d is not None:
                need(d, True)
        for k in writes:
            d = self.lastw.get(k)
            if d is not None:
                need(d, False)
            for r in self.readers.get(k, ()):
                need(r, False)
        return waits

    def _emit_waits(self, E, waits):
        for key, (sem, val) in waits.items():
            if E.waited.get(key, 0) < val:
                E.h.wait_ge(sem, val)
                E.waited[key] = val

    def _record(self, me, reads, writes):
        for k in reads:
            lst = self.readers.setdefault(k, [])
            if me[0] == "eng":
                lst[:] = [r for r in lst if not (r[0] == "eng" and r[1] == me[1])]
            else:
                lst[:] = [r for r in lst if not (r[0] == "dma" and r[1] is me[1])]
            lst.append(me)
        for k in writes:
            self.lastw[k] = me
            self.readers[k] = []

    def op(self, eng, fn, reads=(), writes=()):
        E = self.eng[eng]
        psr = [k for k in reads if isinstance(k, tuple) and k[0] == "ps"]
        if psr:
            writes = list(writes) + [k for k in psr if k not in writes]
        self._emit_waits(E, self._collect(eng, reads, writes))
        ins = fn(E.h)
        E.n += 1
        ins.then_inc(E.sem, 1)
        self._record(("eng", eng, E.n), reads, writes)
        return ins

    def dma(self, queue, out, in_, ds, reads=(), writes=(), **kw):
        Q = self.eng[queue]
        self._emit_waits(Q, self._collect(queue, reads, writes))
        ins = Q.h.dma_start(out=out, in_=in_, **kw)
        ds.total += 16
        ins.then_inc(ds.sem, 16)
        self._record(("dma", ds, ds.total), reads, writes)
        return ins

    def wait_all(self, eng, keys):
        E = self.eng[eng]
        self._emit_waits(E, self._collect(eng, keys, ()))

    def close(self):
        self.es.close()


class Builder:
    def __init__(self, nseq=4, dbg=False):
        self.nseq = nseq
        self.dbg = dbg
        self.nheads = NH
        nc = self.nc = bass.Bass("TRN2", target_bir_lowering=False)
        self.x = nc.dram_tensor("x", [nseq, SEQ, D], F32, kind="ExternalInput").ap()
        self.mem = nc.dram_tensor("mem", [nseq, MEM, D], F32, kind="ExternalInput").ap()
        self.P = {}
        for n in PARAM_NAMES:
            self.P[n] = nc.dram_tensor(n, PARAM_SHAPES[n], F32, kind="ExternalInput").ap()
        self.out = nc.dram_tensor("out", [nseq, SEQ, D], F32, kind="ExternalOutput").ap()
        self.WB = {}
        for n in BIG_W:
            shp = PARAM_SHAPES[n][1:]
            self.WB[n] = nc.dram_tensor(n + "_bf", shp, BF16, kind="Internal").ap()
        if dbg:
            self.dbg_oa = nc.dram_tensor("dbg_oa", [128, 8 * SEQ], F32, kind="ExternalOutput").ap()
            self.dbg_ob = nc.dram_tensor("dbg_ob", [128, 8 * SEQ], F32, kind="ExternalOutput").ap()
            self.dbg_h1 = nc.dram_tensor("dbg_h1", [SEQ, D], F32, kind="ExternalOutput").ap()
            self.dbg_h2 = nc.dram_tensor("dbg_h2", [SEQ, D], F32, kind="ExternalOutput").ap()
        self.S = Sched(nc)
        self.S.init_psum(8, 6)

    def mm(self, out, lhsT, rhs, start, stop, reads, writes):
        self.S.op("pe", lambda e: e.matmul(out, lhsT=lhsT, rhs=rhs, start=start, stop=stop), reads, writes)

    def act(self, out, in_, func, reads, writes, scale=1.0, bias=0.0, accum_out=None):
        if accum_out is None:
            self.S.op("act", lambda e: e.activation(out=out, in_=in_, func=func, bias=bias, scale=scale), reads, writes)
        else:
            self.S.op("act", lambda e: e.activation(out=out, in_=in_, func=func, bias=bias, scale=scale,
                                                    accum_out=accum_out), reads, writes)

    def tt(self, eng, out, in0, in1, op, reads, writes):
        self.S.op(eng, lambda e: e.tensor_tensor(out=out, in0=in0, in1=in1, op=op), reads, writes)

    def ts(self, eng, out, in0, s1, s2, op0, op1, reads, writes):
        if s2 is None:
            self.S.op(eng, lambda e: e.tensor_scalar(out=out, in0=in0, scalar1=s1, scalar2=None, op0=op0), reads, writes)
        else:
            self.S.op(eng, lambda e: e.tensor_scalar(out=out, in0=in0, scalar1=s1, scalar2=s2, op0=op0, op1=op1),
                      reads, writes)

    def stt(self, out, in0, scalar, in1, op0, op1, reads, writes):
        self.S.op("dve", lambda e: e.scalar_tensor_tensor(out=out, in0=in0, scalar=scalar, in1=in1, op0=op0, op1=op1),
                  reads, writes)

    def cp(self, eng, out, in_, reads, writes):
        if eng == "act":
            self.act(out, in_, AF.Copy, reads, writes)
        else:
            self.S.op(eng, lambda e: e.tensor_copy(out=out, in_=in_), reads, writes)

    def setup_weights(self):
        S = self.S
        self.cast_list = []
        self.cast_sem = {}
        self.cast_done = []
        order = ["w_in", "w_proj_gdn", "w_proj_sb", "w_out", "w_xq", "w_xkv", "w_xo", "w_up", "w_down"]
        for n in order:
            src = self.P[n][0]
            dst = self.WB[n]
            self.cast_sem[dst.name] = S.dsem()
            rows, cols = src.shape
            for r0 in range(0, rows, 128):
                for c0 in range(0, cols, 2048):
                    c1 = min(cols, c0 + 2048)
                    self.cast_list.append((dst[r0:r0 + 128, c0:c1], src[r0:r0 + 128, c0:c1], dst.name))
        self.cast_pos = 0
        n_in = sum(1 for c in self.cast_list if c[2] == self.WB["w_in"].name)
        self.emit_casts(n_in)
        self.NSLOT = 3
        self.wslots = [S.sb([128, 8, 512], BF16, "wslot") for _ in range(self.NSLOT)]
        self.wsem = [S.dsem() for _ in range(self.NSLOT)]
        self.wrr = 0

    def emit_casts(self, n):
        S = self.S
        for _ in range(n):
            if self.cast_pos >= len(self.cast_list):
                return
            dst, src, nm = self.cast_list[self.cast_pos]
            k = self.cast_pos
            self.cast_pos += 1
            if k >= 4:
                sem, tot = self.cast_done[k - 4]
                S.eng["pool"].h.wait_ge(sem, tot)
            ds = self.cast_sem[nm]
            S.dma("pool", dst, src, ds, writes=[("wcast", nm)])
            self.cast_done.append((ds.sem, ds.total))

    def wslot(self):
        i = self.wrr % self.NSLOT
        self.wrr += 1
        return i

    def wload(self, i, src, nk, ncols, col_off=0, k_off=0):
        self.S.dma("sp", self.wslots[i][:, k_off:k_off + nk, col_off:col_off + ncols],
                   src.rearrange("(kc p) n -> p kc n", p=128), self.wsem[i],
                   reads=[("wcast", src.name)], writes=[("w", i)])

    def setup_consts(self):
        S = self.S
        sb = S.sb
        P = self.P
        self.ones_f = sb([128, 128], F32, "ones_f")
        self.ones_b = sb([128, 128], BF16, "ones_b")
        self.ident_f = sb([128, 128], F32, "ident_f")
        self.ident_b = sb([128, 128], BF16, "ident_b")
        self.LTincl_b = sb([128, 128], BF16, "LTincl")
        self.LT_f = sb([128, 128], F32, "LT_f")
        self.negmaskT = sb([128, 128], F32, "negmaskT")
        CK = ["consts"]
        S.op("pool", lambda e: e.memset(self.ones_f[:], 1.0), writes=CK)
        S.op("pool", lambda e: e.memset(self.negmaskT[:], 0.0), writes=CK)
        S.op("pool", lambda e: e.memset(self.ones_b[:], 1.0), writes=CK)

        def asel(out, in_, pattern, cm, cmp, fill):
            S.op("pool", lambda e: e.affine_select(out=out, in_=in_, pattern=pattern, compare_op=cmp, fill=fill,
                                                    base=0, channel_multiplier=cm), reads=CK, writes=CK)
        asel(self.ident_f[:], self.ones_f[:], [[-1, 128]], 1, ALU.is_equal, 0.0)
        asel(self.ident_b[:], self.ones_f[:], [[-1, 128]], 1, ALU.is_equal, 0.0)
        asel(self.LTincl_b[:], self.ones_f[:], [[-1, 128]], 1, ALU.is_ge, 0.0)
        asel(self.LT_f[:], self.ones_f[:], [[1, 128]], -1, ALU.is_ge, 0.0)
        asel(self.negmaskT[:], self.negmaskT[:], [[1, 128]], -1, ALU.is_ge, NEG)

        dpar = S.dsem()
        R = sb([128, 128], F32, "Rst")
        self.colsR1 = sb([128, 128], F32, "colsR1")
        self.convg = sb([128, 96], F32, "convg")
        self.convf = sb([128, 132], F32, "convf")

        def rows(p):
            return p.rearrange("o (r q) -> (o r) q", q=128)
        self.col_off = {}

        def finish(nrow, dst):
            ps, pk = S.psum()
            self.mm(ps[:, 0:nrow], R[0:nrow, :], self.ident_f[0:nrow, 0:nrow], True, True, ["Rst"] + CK, [pk])
            self.cp("dve", dst, ps[:, 0:nrow], [pk], CK)
        S.op("pool", lambda e: e.memset(R[:], 0.0), writes=["Rst"])
        r = 0
        for n, nr in (("norm_mix", 8), ("norm_x", 8), ("norm_ffn", 8), ("norm_mem", 8), ("gdn_out_norm", 1),
                      ("xq_norm", 2), ("xk_norm", 2)):
            S.dma("sp", R[r:r + nr, :], rows(P[n]), dpar, reads=["Rst"], writes=["Rst"])
            self.col_off[n] = r
            r += nr
        finish(128, self.colsR1[:])
        S.dma("sp", R[0:96, :], P["conv_gdn"][0].rearrange("i (c q) -> (i c) q", q=128), dpar, reads=["Rst"], writes=["Rst"])
        finish(96, self.convg[:, 0:96])
        S.dma("sp", R[0:88, :], P["conv_ffn"][0, 0:2, :].rearrange("i (c q) -> (i c) q", q=128), dpar, reads=["Rst"], writes=["Rst"])
        finish(88, self.convf[:, 0:88])
        S.dma("sp", R[0:44, :], P["conv_ffn"][0, 2:3, :].rearrange("i (c q) -> (i c) q", q=128), dpar, reads=["Rst"], writes=["Rst"])
        finish(44, self.convf[:, 88:132])
        self.dtb_row = sb([128, 8], F32, "dtb_row")
        self.nAexp_row = sb([128, 8], F32, "nAexp_row")
        S.dma("sp", self.dtb_row[:], P["dt_bias"][0].partition_broadcast(128), dpar, writes=["Rst2"])
        S.dma("sp", self.nAexp_row[:], P["a_log"][0].partition_broadcast(128), dpar, writes=["Rst2"])
        self.act(self.nAexp_row[:], self.nAexp_row[:], AF.Exp, ["Rst2"], CK)
        self.ts("dve", self.nAexp_row[:], self.nAexp_row[:], -1.0, None, ALU.mult, None, CK, CK)
        self.CK = CK

    def gcol(self, name, c):
        o = self.col_off[name] + c
        return self.colsR1[:, o:o + 1]

    def setup_tiles(self):
        sb = self.S.sb
        S = self.S
        self.xnT = sb([128, 8, SEQ], BF16, "xnT")
        self.oaT = sb([128, 8, SEQ], BF16, "oaT")
        self.obT = sb([128, 8, SEQ], BF16, "obT")
        self.h = self.xnT[:, 0:4, :].rearrange("p c t -> p (c t)").bitcast(F32).rearrange("p (a n) -> p a n", a=4)
        self.KT = self.xnT[:, 4, :].rearrange("p (c m) -> p c m", m=MEM)
        self.Vm = self.xnT[:, 5, :].rearrange("p (c m) -> p c m", m=1024)
        self.rawc = [self.xnT[:, 6 + i // 3, (i % 3) * 520:(i % 3) * 520 + 516] for i in range(6)]
        self.rawcrr = 0
        self.alias_keys = ([("xnT", g) for g in range(NG)] + [("h", t) for t in range(4)] + ["KT", "Vm"]
                           + [("rawc", i) for i in range(6)] + [("xt", 0)] + [(("raw", 1), g_) for g_ in range(NG)]
                           + [("xh", 0)] + [(("raw", 0), g_) for g_ in range(NG)])
        self.dummy = sb([128, 2], F32, "dummy")
        self.xt = [sb([128, 1024], F32, "xt") for _ in range(1)]
        self.xsem = [S.dsem() for _ in range(1)]
        self.xrr = 0
        self.stat = sb([128, 8], F32, "stat")
        self.xhrr = 0
        self.osem = [S.dsem() for _ in range(4)]
        self.dbgsem = [S.dsem() for _ in range(4)]
        self.hsem = [S.dsem() for _ in range(4)]
        self.FB = [sb([128, SEQ], BF16, "FB") for _ in range(7)]
        self.T5 = [sb([128, 512], F32, "T5") for _ in range(8)]
        self.t5rr = 0
        self.B5 = [sb([128, 512], BF16, "B5") for _ in range(8)]
        self.b5rr = 0
        self.G5 = [sb([128, 512], BF16, "G5") for _ in range(3)]
        self.H5 = [[sb([128, 512], BF16, "H5") for _ in range(5)] for _ in range(2)]
        self.carry = sb([128, 512], BF16, "carry")
        self.nsq = sb([128, 512], BF16, "nsq")
        self.gbt = [sb([128, 128], F32, "gbt") for _ in range(2)]
        self.gbrr = 0
        self.COLS = sb([128, NT, 48], F32, "COLS")
        self.Sf = sb([128, 128], F32, "Sf")
        self.Sb = sb([128, 128], BF16, "Sb")
        self.Vn = [sb([128, 128], BF16, "Vn") for _ in range(2)]
        self.sv = self.FB[5][:].rearrange("p (c t) -> p c t", t=128)
        self.raw = [sb([128, SEQ], BF16, "raw"), self.xt[0][:].bitcast(BF16)]
        self.xh = [self.raw[0][:, 0:1024]]
        self.wab = sb([128, 8, 16], BF16, "wab")
        self.wabsem = S.dsem()
        self.mnT = self.FB[6][:].rearrange("p (c m) -> p c m", m=MEM)
        self.halo = sb([128, 44, 2], BF16, "halo")

    def fence(self):
        self.S.op("pool", lambda e: e.memset(self.dummy[:], 0.0), writes=self.alias_keys)

    def actT_chunk(self, c):
        return self.FB[4 + c // 4][:, (c % 4) * 512:(c % 4 + 1) * 512], ("FB%d" % (4 + c // 4), c % 4)

    def t5(self):
        i = self.t5rr % len(self.T5)
        self.t5rr += 1
        return self.T5[i], ("T5", i)

    def b5(self):
        i = self.b5rr % len(self.B5)
        self.b5rr += 1
        return self.B5[i], ("B5", i)

    def norm_tile_to_T(self, src_ap, src_keys, gname, dstT, dst_col0, dst_keys, width=1024):
        S = self.S
        st = self.stat
        i = 0
        xh = self.xh[i]
        self.act(xh[:], src_ap, AF.Square, src_keys, [("xh", i), "stat"], accum_out=st[:, 0:1])
        self.act(st[:, 1:2], st[:, 0:1], AF.Ln, ["stat"], ["stat"], scale=1.0 / width, bias=EPS)
        self.act(st[:, 2:3], st[:, 1:2], AF.Exp, ["stat"], ["stat"], scale=-0.5)
        self.act(xh[:], src_ap, AF.Copy, src_keys + ["stat"], [("xh", i)], scale=st[:, 2:3])
        for half in range(2):
            ps, pk = S.psum()
            for c4 in range(4):
                c = half * 4 + c4
                self.mm(ps[:, c4 * 128:(c4 + 1) * 128], xh[:, c * 128:(c + 1) * 128], self.ident_b[:], True, True,
                        [("xh", i)] + self.CK, [pk])
            o = self.col_off[gname] + half * 4
            gb = self.colsR1[:, o:o + 4].unsqueeze(2).to_broadcast([128, 4, 128])
            self.tt("dve", dstT[:, half * 4:(half + 1) * 4, dst_col0:dst_col0 + 128],
                    ps[:].rearrange("p (c t) -> p c t", t=128), gb, ALU.mult, [pk] + self.CK, dst_keys)

    def load_x_tile(self, src):
        i = 0
        self.S.dma("sp", self.xt[i][:], src, self.xsem[i], writes=[("xt", i)])
        return self.xt[i], ("xt", i)

    def sigmoid(self, out, in_, tmp, in_keys, tmp_key, out_keys):
        self.act(tmp, in_, AF.Exp, in_keys, [tmp_key], scale=-1.0)
        self.act(tmp, tmp, AF.Ln, [tmp_key], [tmp_key], bias=1.0)
        self.act(out, tmp, AF.Exp, [tmp_key], out_keys, scale=-1.0)

    def stage_mem(self, b):
        S = self.S
        for mt in range(2):
            xt, xk = self.load_x_tile(self.mem[b, mt * 128:(mt + 1) * 128, :])
            self.norm_tile_to_T(xt[:], [xk], "norm_mem", self.mnT, mt * 128, [("FB6", g_) for g_ in range(NG)])
        wkv = self.WB["w_xkv"]
        st = self.stat
        for q in range(4):
            wi = self.wslot()
            self.wload(wi, wkv[:, q * 512:(q + 1) * 512], 8, 512)
            for mt in range(2):
                ps, pk = S.psum()
                for kc in range(8):
                    self.mm(ps[:], self.mnT[:, kc, mt * 128:(mt + 1) * 128], self.wslots[wi][:, kc, :], kc == 0, kc == 7,
                            [("FB6", g_) for g_ in range(NG)] + [("w", wi)], [pk])
                if q >= 2:
                    self.cp("act", self.Vm[:, mt, (q - 2) * 512:(q - 1) * 512], ps[:], [pk], ["Vm"])
                else:
                    kh, kk = self.b5()
                    for hh in range(2):
                        self.act(kh[:, hh * 256:(hh + 1) * 256], ps[:, hh * 256:(hh + 1) * 256], AF.Square, [pk], [kk, "stat"],
                                 accum_out=st[:, 4:5])
                        self.act(st[:, 5:6], st[:, 4:5], AF.Ln, ["stat"], ["stat"], scale=1.0 / 256, bias=EPS)
                        self.act(st[:, 6:7], st[:, 5:6], AF.Exp, ["stat"], ["stat"], scale=-0.5)
                        self.act(kh[:, hh * 256:(hh + 1) * 256], ps[:, hh * 256:(hh + 1) * 256], AF.Copy, [pk, "stat"], [kk],
                                 scale=st[:, 6:7])
                    ps2, pk2 = S.psum()
                    for j in range(4):
                        self.mm(ps2[:, j * 128:(j + 1) * 128], kh[:, j * 128:(j + 1) * 128], self.ident_b[:], True, True,
                                [kk] + self.CK, [pk2])
                    for j in range(4):
                        self.ts("dve", self.KT[:, q * 4 + j, mt * 128:(mt + 1) * 128], ps2[:, j * 128:(j + 1) * 128],
                                self.gcol("xk_norm", j % 2), None, ALU.mult, None, [pk2] + self.CK, ["KT"])

    def stage_A(self, b):
        for t in range(NT):
            xt, xk = self.load_x_tile(self.x[b, t * 128:(t + 1) * 128, :])
            self.norm_tile_to_T(xt[:], [xk], "norm_mix", self.xnT, t * 128, [("xnT", t // 4)])

    def stage_gdn_prep(self, b):
        S = self.S
        S.dma("sp", self.wab[:], self.WB["w_in"][:, C_A:C_A + 16].rearrange("(kc p) n -> p kc n", p=128), self.wabsem,
              reads=[("wcast", self.WB["w_in"].name)], writes=["wab"])
        for t in range(NT):
            C = self.COLS
            ck = ("COLS", t)
            ps, pk = S.psum()
            for kc in range(8):
                self.mm(ps[:, 0:16], self.xnT[:, kc, t * 128:(t + 1) * 128], self.wab[:, kc, :], kc == 0, kc == 7,
                        [("xnT", t // 4), "wab"], [pk])
            tmp, tk = self.t5()
            self.tt("dve", tmp[:, 0:8], ps[:, 0:8], self.dtb_row[:], ALU.add, [pk] + self.CK, [tk])
            self.act(tmp[:, 0:8], tmp[:, 0:8], AF.Exp, [tk], [tk])
            self.act(tmp[:, 0:8], tmp[:, 0:8], AF.Ln, [tk], [tk], bias=1.0)
            self.tt("dve", tmp[:, 0:8], tmp[:, 0:8], self.nAexp_row[:], ALU.mult, [tk] + self.CK, [tk])
            self.act(tmp[:, 8:16], ps[:, 8:16], AF.Exp, [pk], [tk], scale=-1.0)
            self.act(tmp[:, 8:16], tmp[:, 8:16], AF.Ln, [tk], [tk], bias=1.0)
            self.act(C[:, t, 32:40], tmp[:, 8:16], AF.Exp, [tk], [ck], scale=-1.0)
            self.ts("dve", C[:, t, 40:48], C[:, t, 32:40], -1.0, None, ALU.mult, None, [ck], [ck])
            ps2, pk2 = S.psum()
            self.mm(ps2[:, 0:8], self.LT_f[:], tmp[:, 0:8], True, True, [tk] + self.CK, [pk2])
            self.mm(ps2[:, 8:16], self.ones_f[:], tmp[:, 0:8], True, True, [tk] + self.CK, [pk2])
            self.cp("dve", tmp[:, 16:32], ps2[:, 0:16], [pk2], [tk])
            self.cp("act", C[:, t, 0:8], tmp[:, 16:24], [tk], [ck])
            self.act(C[:, t, 8:16], tmp[:, 16:24], AF.Exp, [tk], [ck])
            self.tt("dve", tmp[:, 32:40], tmp[:, 24:32], tmp[:, 16:24], ALU.subtract, [tk], [tk])
            self.act(C[:, t, 16:24], tmp[:, 32:40], AF.Exp, [tk], [ck])
            self.act(C[:, t, 24:32], tmp[:, 24:32], AF.Exp, [tk], [ck])

    def projT(self, wi, col, evac, src=None, srckey="xnT"):
        S = self.S
        src = self.xnT if src is None else src
        for g in range(NG):
            ps, pk = S.psum()
            for kc in range(8):
                self.mm(ps[:], self.wslots[wi][:, kc, col * 128:(col + 1) * 128], src[:, kc, g * 512:(g + 1) * 512],
                        kc == 0, kc == 7, [(srckey, g), ("w", wi)], [pk])
            evac(g, ps, pk)

    def projT_gen(self, wi, col, evac):
        S = self.S
        src = self.xnT
        for g in range(NG):
            ps, pk = S.psum()
            for kc in range(8):
                self.mm(ps[:], self.wslots[wi][:, kc, col * 128:(col + 1) * 128], src[:, kc, g * 512:(g + 1) * 512],
                        kc == 0, kc == 7, [("xnT", g), ("w", wi)], [pk])
            evac(g, ps, pk)
            yield

    def make_diag(self, cols, nk):
        t, tk = self.b5()
        dg = t[:].rearrange("p (i c) -> p i c", c=128)
        for j, c in enumerate(cols):
            self.act(dg[:, j, :], self.ident_f[:], AF.Copy, self.CK, [tk], scale=c)
        return dg, tk

    def gdn_front_gen(self, b, h):
        S = self.S
        FB = self.FB
        Win = self.WB["w_in"]
        wi = self.wslot()
        for j, c0 in enumerate((C_GQ, C_GK, C_GV, C_Z)):
            self.wload(wi, Win[:, c0 + h * 128:c0 + (h + 1) * 128], 8, 128, col_off=j * 128)
        qT, kT, vT = FB[0], FB[1], FB[2]
        ysq = FB[5]

        def k4(name):
            return [(name, g) for g in range(NG)]

        plan = ((qT, "FB0"), (kT, "FB1"), (vT, "FB2"))

        def proj(j):
            raw = self.raw[j % 2]
            rk = ("raw", j % 2)

            def ev(g, ps, pk):
                self.cp("act", raw[:, g * 512:(g + 1) * 512], ps[:], [pk], [(rk, g)])
            yield from self.projT_gen(wi, j, ev)

        def convnorm(j):
            dst, dname = plan[j]
            raw = self.raw[j % 2]
            rk = ("raw", j % 2)
            ch = j * 8 + h
            dg, dk = self.make_diag([self.convg[:, i * 24 + ch:i * 24 + ch + 1] for i in range(4)], 4)
            for g in range(NG):
                ps, pk = S.psum()
                if g == 0:
                    self.mm(ps[:], dg[:, 3, :], raw[:, 0:512], True, False, [dk, (rk, 0)], [pk])
                    for i in (2, 1, 0):
                        off = 3 - i
                        self.mm(ps[:, off:512], dg[:, i, :], raw[:, 0:512 - off], False, i == 0, [dk, (rk, 0)], [pk])
                else:
                    for i in range(4):
                        self.mm(ps[:], dg[:, i, :], raw[:, g * 512 - 3 + i:g * 512 - 3 + i + 512], i == 0, i == 3,
                                [dk, (rk, g), (rk, g - 1)], [pk])
                sg, sk_ = self.t5()
                self.sigmoid(sg[:], ps[:], sg[:], [pk], sk_, [sk_])
                self.tt("dve", dst[:, g * 512:(g + 1) * 512], ps[:], sg[:], ALU.mult, [pk, sk_], [(dname, g)])
                yield
            if j < 2:
                self.act(ysq[:], dst[:], AF.Square, k4(dname), k4("FB5"))
                for g in range(NG):
                    ps, pk = S.psum()
                    self.mm(ps[:], self.ones_b[:], ysq[:, g * 512:(g + 1) * 512], True, True, [("FB5", g)] + self.CK, [pk])
                    tmp, tk = self.t5()
                    self.act(tmp[:], ps[:], AF.Ln, [pk], [tk], bias=EPS)
                    bias = -0.5 * math.log(128.0) if j == 0 else 0.0
                    self.act(tmp[:], tmp[:], AF.Exp, [tk], [tk], scale=-0.5, bias=bias)
                    self.tt("dve", dst[:, g * 512:(g + 1) * 512], dst[:, g * 512:(g + 1) * 512], tmp[:], ALU.mult,
                            [(dname, g), tk], [(dname, g)])
                    yield
        yield from proj(0)
        yield from proj(1)
        yield from convnorm(0)
        yield from proj(2)
        yield from convnorm(1)
        yield from convnorm(2)
        self._gdn_wi = wi

    def gdn_loop_gen(self, b, h):
        S = self.S
        FB = self.FB
        qT, kT, vT, OT = FB[0], FB[1], FB[2], FB[6]
        C = self.COLS
        def rowbc(g, fld):
            ps, pk = S.psum("neu")
            for tt_ in range(4):
                t = g * 4 + tt_
                gi = self.gbrr % 2
                self.gbrr += 1
                gb = self.gbt[gi]
                self.act(gb[:], self.ones_f[:], AF.Copy, [("COLS", t)] + self.CK, [("gbt", gi)],
                         scale=C[:, t, fld * 8 + h:fld * 8 + h + 1])
                self.mm(ps[:, tt_ * 128:(tt_ + 1) * 128], gb[:], self.ident_f[:], True, True, [("gbt", gi)] + self.CK, [pk])
            return ps, pk
        S.op("pool", lambda e: e.memset(self.Sf[:], 0.0), writes=["Sf"])
        S.op("pool", lambda e: e.memset(self.Sb[:], 0.0), writes=["Sb"])
        C = self.COLS
        sl = lambda cc: slice(cc * 128, (cc + 1) * 128)
        tok = lambda c: slice(c * 128, (c + 1) * 128)
        v3 = lambda ap: ap.rearrange("p (c t) -> p c t", t=128)
        hand = {}

        def neu(bt):
            gk = bt
            par = bt % 2
            cs = [bt * 4 + cc for cc in range(4)]

            def colb(f):
                return C[:, bt * 4:(bt + 1) * 4, f * 8 + h:f * 8 + h + 1].to_broadcast([128, 4, 128])
            ckeys = [("COLS", c) for c in cs]
            pk_, pkk_ = S.psum("neu")
            for cc, c in enumerate(cs):
                self.mm(pk_[:, sl(cc)], kT[:, tok(c)], self.ident_b[:], True, True, [("FB1", gk)] + self.CK, [pkk_])
            Kg, Kgk = self.G5[0], 'G50'
            Kd, Kdk = self.H5[par][0], ('H5', par, 0)
            self.tt("dve", v3(Kg[:]), v3(pk_[:]), colb(1), ALU.mult, [pkk_] + ckeys, [Kgk])
            self.tt("dve", v3(Kd[:]), v3(pk_[:]), colb(2), ALU.mult, [pkk_] + ckeys, [Kdk])
            pv_, pvk_ = S.psum("neu")
            for cc, c in enumerate(cs):
                self.mm(pv_[:, sl(cc)], vT[:, tok(c)], self.ident_b[:], True, True, [("FB2", gk)] + self.CK, [pvk_])
            Vt, Vtk = self.G5[2], 'G52'
            self.cp("act", Vt[:], pv_[:], [pvk_], [Vtk])
            yield
            peg, pegk = rowbc(bt, 1)
            QdT, QdTk = self.H5[par][3], ('H5', par, 3)
            self.tt("dve", QdT[:], qT[:, bt * 512:(bt + 1) * 512], peg[:], ALU.mult, [pegk, ("FB0", bt)], [QdTk])
            yield
            E, Ek = self.t5()
            pgc, pgck = rowbc(bt, 0)
            for cc, c in enumerate(cs):
                self.stt(E[:, sl(cc)], pgc[:, sl(cc)], C[:, c, h:h + 1], self.negmaskT[:], ALU.subtract, ALU.add,
                         [pgck, ("COLS", c)] + self.CK, [Ek])
            self.act(E[:], E[:], AF.Exp, [Ek], [Ek])
            yield
            QKm, QKmk = self.H5[par][1], ('H5', par, 1)
            pqk, pqkk = S.psum("neu")
            for cc, c in enumerate(cs):
                self.mm(pqk[:, sl(cc)], kT[:, tok(c)], qT[:, tok(c)], True, True, [("FB1", gk), ("FB0", gk)], [pqkk])
            self.tt("dve", QKm[:], pqk[:], E[:], ALU.mult, [pqkk, Ek], [QKmk])
            Es, Esk = self.t5()
            S.op("pool", lambda e: e.affine_select(out=v3(Es[:]), in_=v3(E[:]), pattern=[[0, 4], [1, 128]],
                                                    compare_op=ALU.is_gt, fill=0.0, base=0, channel_multiplier=-1),
                 reads=[Ek], writes=[Esk])
            yield
            X, Xk = self.t5()
            pkk, pkkk = S.psum("neu")
            for cc, c in enumerate(cs):
                self.mm(pkk[:, sl(cc)], kT[:, tok(c)], kT[:, tok(c)], True, True, [("FB1", gk)], [pkkk])
            for cc, c in enumerate(cs):
                self.stt(X[:, sl(cc)], pkk[:, sl(cc)], C[:, c, 32 + h:32 + h + 1], Es[:, sl(cc)], ALU.mult, ALU.mult,
                         [pkkk, Esk, ("COLS", c)], [Xk])
            yield
            pt, ptk = S.psum("neu")
            for cc in range(4):
                self.mm(pt[:, sl(cc)], X[:, sl(cc)], self.ident_f[:], True, True, [Xk] + self.CK, [ptk])
            XT, XTk = self.t5()
            self.cp("act", XT[:], pt[:], [ptk], [XTk])
            yield
            Y, Yk = self.t5()
            self.tt("dve", v3(Y[:]), self.ident_f[:].unsqueeze(1).to_broadcast([128, 4, 128]), v3(X[:]), ALU.subtract,
                    [Xk] + self.CK, [Yk])
            for p in (2, 4, 8, 16, 32, 64):
                pb, pbk = S.psum("neu")
                for cc in range(4):
                    self.mm(pb[:, sl(cc)], X[:, sl(cc)], XT[:, sl(cc)], True, True, [Xk, XTk], [pbk])
                XTn, XTnk = self.t5()
                if p < 64:
                    pa, pak = S.psum("neu")
                    for cc in range(4):
                        self.mm(pa[:, sl(cc)], XT[:, sl(cc)], X[:, sl(cc)], True, True, [Xk, XTk], [pak])
                    Xn, Xnk = self.t5()
                    self.cp("act", Xn[:], pa[:], [pak], [Xnk])
                self.cp("dve", XTn[:], pb[:], [pbk], [XTnk])
                if p < 64:
                    X, Xk = Xn, Xnk
                XT, XTk = XTn, XTnk
                yield
                py, pyk = S.psum("neu")
                for cc in range(4):
                    self.mm(py[:, sl(cc)], XT[:, sl(cc)], Y[:, sl(cc)], True, True, [XTk, Yk], [pyk])
                Yn, Ynk = self.t5()
                self.tt("dve", Yn[:], py[:], Y[:], ALU.add, [pyk, Yk], [Ynk])
                Y, Yk = Yn, Ynk
                yield
            Yb, Ybk = self.b5()
            self.cp("act", Yb[:], Y[:], [Yk], [Ybk])
            Y, Yk = Yb, Ybk
            yield
            pu, puk = S.psum("neu")
            for cc in range(4):
                self.mm(pu[:, sl(cc)], Y[:, sl(cc)], Vt[:, sl(cc)], True, True, [Yk, Vtk], [puk])
            Upp, Uppk = self.H5[par][4], ('H5', par, 4)
            self.tt("dve", v3(Upp[:]), v3(pu[:]), colb(4), ALU.mult, [puk] + ckeys, [Uppk])
            pw, pwk = S.psum("neu")
            for cc in range(4):
                self.mm(pw[:, sl(cc)], Kg[:, sl(cc)], Y[:, sl(cc)], True, True, [Kgk, Yk], [pwk])
            WT, WTk = self.H5[par][2], ('H5', par, 2)
            self.cp("act", WT[:], pw[:], [pwk], [WTk])
            hand[bt] = (Kd, Kdk, QdT, QdTk, QKm, QKmk, Upp, Uppk, WT, WTk)
            yield

        def rec(bt):
            cs = [bt * 4 + cc for cc in range(4)]
            Kd, Kdk, QdT, QdTk, QKm, QKmk, Upp, Uppk, WT, WTk = hand[bt]
            po, pok = S.psum_fixed(6)
            for cc, c in enumerate(cs):
                pws, pwsk = S.psum("rec")
                self.mm(pws[:, 0:128], WT[:, sl(cc)], self.Sb[:], True, True, [WTk, "Sb"], [pwsk])
                Vn = self.Vn[c % 2]
                Vnk = ("Vn", c % 2)
                self.stt(Vn[:], pws[:, 0:128], C[:, c, 40 + h:40 + h + 1], Upp[:, sl(cc)], ALU.mult, ALU.add,
                         [pwsk, Uppk, ("COLS", c)], [Vnk])
                yield
                self.mm(po[:, sl(cc)], self.Sb[:], QdT[:, sl(cc)], True, False, ["Sb", QdTk], [pok])
                self.mm(po[:, sl(cc)], Vn[:], QKm[:, sl(cc)], False, True, [Vnk, QKmk], [pok])
                psn, psnk = S.psum("rec")
                self.mm(psn[:, 0:128], Kd[:, sl(cc)], Vn[:], True, True, [Kdk, Vnk], [psnk])
                self.stt(self.Sb[:], self.Sf[:], C[:, c, 24 + h:24 + h + 1], psn[:, 0:128], ALU.mult, ALU.add,
                         [psnk, "Sf", ("COLS", c)], ["Sb"])
                self.stt(self.Sf[:], self.Sf[:], C[:, c, 24 + h:24 + h + 1], psn[:, 0:128], ALU.mult, ALU.add,
                         [psnk, "Sf", ("COLS", c)], ["Sf"])
                yield
            self.cp("act", OT[:, bt * 512:(bt + 1) * 512], po[:], [pok], [("FB6", bt)])
            yield

        def inter(gs):
            gs = list(gs)
            while gs:
                for gn in list(gs):
                    try:
                        next(gn)
                        yield
                    except StopIteration:
                        gs.remove(gn)
        for bt in range(NG):
            gs = [neu(bt)] + ([rec(bt - 1)] if bt > 0 else [])
            yield from inter(gs)
        yield from inter([rec(NG - 1)])

    def gdn_tail_gen(self, b, h):
        S = self.S
        FB = self.FB
        OT, zs, ysq = FB[6], FB[3], FB[4]
        wi = self._gdn_wi

        def k4(name):
            return [(name, g) for g in range(NG)]

        def evz(g, ps, pk):
            sg, sk_ = self.t5()
            self.sigmoid(sg[:], ps[:], sg[:], [pk], sk_, [sk_])
            self.tt("dve", zs[:, g * 512:(g + 1) * 512], ps[:], sg[:], ALU.mult, [pk, sk_], [("FB3", g)])
        yield from self.projT_gen(wi, 3, evz)
        self.act(ysq[:], OT[:], AF.Square, k4("FB6"), k4("FB4"))
        gn = self.gcol("gdn_out_norm", 0)
        for g in range(NG):
            ps, pk = S.psum()
            self.mm(ps[:], self.ones_b[:], ysq[:, g * 512:(g + 1) * 512], True, True, [("FB4", g)] + self.CK, [pk])
            tmp, tk = self.t5()
            self.act(tmp[:], ps[:], AF.Ln, [pk], [tk], scale=1.0 / 128, bias=EPS)
            self.act(tmp[:], tmp[:], AF.Exp, [tk], [tk], scale=-0.5)
            t2, t2k = self.t5()
            self.stt(t2[:], OT[:, g * 512:(g + 1) * 512], gn, tmp[:], ALU.mult, ALU.mult, [("FB6", g), tk] + self.CK, [t2k])
            self.tt("dve", self.oaT[:, h, g * 512:(g + 1) * 512], t2[:], zs[:, g * 512:(g + 1) * 512], ALU.mult,
                    [t2k, ("FB3", g)], [("oaT", g)])
            yield

    def sb_front(self, b, h):
        S = self.S
        FB = self.FB
        Win = self.WB["w_in"]
        wi = self.wslot()
        for j, c0 in enumerate((C_SQ, C_SK, C_SV)):
            self.wload(wi, Win[:, c0 + h * 128:c0 + (h + 1) * 128], 8, 128, col_off=j * 128)
        sqT, skT = FB[3], FB[4]
        scl = 128.0 ** -0.5

        def evq(g, ps, pk):
            self.act(sqT[:, g * 512:(g + 1) * 512], ps[:], AF.Copy, [pk], [("FB3", g)], scale=scl)
        self.projT(wi, 0, evq)

        def evk(g, ps, pk):
            self.cp("act", skT[:, g * 512:(g + 1) * 512], ps[:], [pk], [("FB4", g)])
        self.projT(wi, 1, evk)
        for t4 in range(4):
            ps, pk = S.psum()
            for tt_ in range(4):
                t = t4 * 4 + tt_
                for kc in range(8):
                    self.mm(ps[:, tt_ * 128:(tt_ + 1) * 128], self.xnT[:, kc, t * 128:(t + 1) * 128],
                            self.wslots[wi][:, kc, 256:384], kc == 0, kc == 7, [("xnT", t4), ("w", wi)], [pk])
            self.cp("dve", self.sv[:, t4 * 4:(t4 + 1) * 4, :].rearrange("p c t -> p (c t)"), ps[:], [pk], [("FB5", t4)])

    def sb_loop_gen(self, b, h):
        S = self.S
        FB = self.FB
        sqT, skT = FB[3], FB[4]
        for qg in range(NG):
            acc, acck = S.psum_fixed(7)
            carry, ck = self.carry, 'carry'
            nsq, nsqk = self.nsq, 'nsq'
            self.act(nsq[:], sqT[:, qg * 512:(qg + 1) * 512], AF.Copy, [("FB3", qg)], [nsqk], scale=-1.0)
            S.op("pool", lambda e: e.memset(carry[:], 0.0), writes=[ck])
            blocks = list(range(4 * qg + 3, -1, -1))
            st1 = {}

            def stage1(kb):
                r = kb - 4 * qg
                c0 = max(r, 0) * 128
                pz, pzk = S.psum("sb")
                self.mm(pz[:, c0:512], skT[:, kb * 128:(kb + 1) * 128], sqT[:, qg * 512 + c0:(qg + 1) * 512], True, True,
                        [("FB4", kb // 4), ("FB3", qg)], [pzk])
                e, ek = self.b5()
                self.act(e[:, c0:512], pz[:, c0:512], AF.Exp, [pzk], [ek])
                if r >= 0:
                    S.op("pool", lambda en: en.affine_select(out=e[:, c0:c0 + 128], in_=e[:, c0:c0 + 128], pattern=[[1, 128]],
                                                              compare_op=ALU.is_gt, fill=0.0, base=0, channel_multiplier=-1),
                         reads=[ek], writes=[ek])
                sp, spk = self.b5()
                self.act(sp[:, c0:512], e[:, c0:512], AF.Ln, [ek], [spk], bias=1.0)
                st1[kb] = (r, c0, sp, spk)

            st2 = {}

            def stage2a(kb, first):
                r, c0, sp, spk = st1.pop(kb)
                pc, pck = S.psum("sb")
                self.mm(pc[:, c0:512], self.LTincl_b[:], sp[:, c0:512], True, False, [spk] + self.CK, [pck])
                if not first:
                    self.mm(pc[:, c0:512], self.ones_b[:], carry[:, c0:512], False, False, [ck] + self.CK, [pck])
                self.mm(pc[:, c0:512], skT[:, kb * 128:(kb + 1) * 128], nsq[:, c0:512], False, True,
                        [("FB4", kb // 4), nsqk], [pck])
                w, wk = self.b5()
                self.act(w[:, c0:512], pc[:, c0:512], AF.Exp, [pck], [wk], scale=-1.0)
                if r >= 0:
                    S.op("pool", lambda en: en.affine_select(out=w[:, c0:c0 + 128], in_=w[:, c0:c0 + 128], pattern=[[1, 128]],
                                                              compare_op=ALU.is_gt, fill=0.0, base=0, channel_multiplier=-1),
                         reads=[wk], writes=[wk])
                st2[kb] = (r, c0, sp, spk, w, wk)

            def stage2b(kb, first, last):
                r, c0, sp, spk, w, wk = st2.pop(kb)
                self.mm(acc[:, c0:512], self.sv[:, kb, :], w[:, c0:512], first, last, [("FB5", kb // 4), wk], [acck])
                if not last:
                    self.tt("dve", carry[:, c0:512], carry[:, c0:512], sp[:, c0:512], ALU.add, [ck, spk], [ck])

            stage1(blocks[0])
            for n, kb in enumerate(blocks):
                if n + 1 < len(blocks):
                    stage1(blocks[n + 1])
                    yield
                stage2a(kb, n == 0)
                yield
                stage2b(kb, n == 0, n == len(blocks) - 1)
                yield
            self.cp("dve", self.obT[:, h, qg * 512:(qg + 1) * 512], acc[:], [acck], [("obT", qg)])

    def run_pipelined(self, gens, depth=2):
        active = []
        it = iter(gens)
        done = False
        while True:
            while not done and len(active) < depth:
                try:
                    active.append(next(it))
                except StopIteration:
                    done = True
            if not active:
                break
            for gn in list(active):
                try:
                    next(gn)
                except StopIteration:
                    active.remove(gn)

    def ffn_chunk_gen(self, g, c_lo, q0, nq, f, hnT, hkeys):
        S = self.S
        Wu = self.WB["w_up"]
        if f == 0:
            self._ffn_wis = []
            for which in range(2):
                wi = self.wslot()
                col = which * DFF + (c_lo + q0) * 128
                self.wload(wi, Wu[:, col:col + nq * 128], 8, nq * 128)
                self._ffn_wis.append(wi)
        wis = list(self._ffn_wis)
        c = c_lo + q0 + f
        raws = []
        for which in range(2):
            wi = wis[which]
            ch = which * 22 + c
            ps, pk = S.psum()
            for kc in range(8):
                self.mm(ps[:], self.wslots[wi][:, kc, f * 128:(f + 1) * 128], hnT[kc // 4][:, kc % 4, :],
                        kc == 0, kc == 7, hkeys + [("w", wi)], [pk])
            ri = self.rawcrr % 6
            self.rawcrr += 1
            rc = self.rawc[ri]
            rck = ("rawc", ri)
            hk = ("halo", ch)
            self.cp("dve", rc[:, 2:4], self.halo[:, ch, :], [hk], [rck])
            self.cp("act", rc[:, 4:516], ps[:], [pk], [rck])
            self.cp("dve", self.halo[:, ch, :], rc[:, 514:516], [rck], [hk])
            dg, dk = self.make_diag([self.convf[:, i * 44 + ch:i * 44 + ch + 1] for i in range(3)], 3)
            raws.append((rc, rck, dg, dk))
        yield
        cps = []
        for which in range(2):
            rc, rck, dg, dk = raws[which]
            pc, pck = S.psum()
            for i in range(3):
                self.mm(pc[:], dg[:, i, :], rc[:, 2 + i:2 + i + 512], i == 0, i == 2, [dk, rck], [pck])
            cps.append((pc, pck))
        sg, sgk = self.t5()
        self.sigmoid(sg[:], cps[0][0][:], sg[:], [cps[0][1]], sgk, [sgk])
        t2, t2k = self.t5()
        self.tt("dve", t2[:], cps[0][0][:], sg[:], ALU.mult, [cps[0][1], sgk], [t2k])
        a_ap, a_key = self.actT_chunk(q0 + f)
        self.tt("dve", a_ap, cps[1][0][:], t2[:], ALU.mult, [cps[1][1], t2k], [a_key])
        yield

    def down_proj_add(self, nkc, k_base):
        S = self.S
        W = self.WB["w_down"]
        for half in range(2):
            slots = []
            k = 0
            while k < nkc:
                nk = min(8, nkc - k)
                wi = self.wslot()
                self.wload(wi, W[(k_base + k) * 128:(k_base + k + nk) * 128, half * 512:(half + 1) * 512], nk, 512)
                slots.append((wi, k, nk))
                k += nk
            for tt_ in range(4):
                ps, pk = S.psum()
                first = True
                for (wi, k0, nk) in slots:
                    for kk in range(nk):
                        kc = k0 + kk
                        a_ap, a_key = self.actT_chunk(kc)
                        self.mm(ps[:], a_ap[:, tt_ * 128:(tt_ + 1) * 128], self.wslots[wi][:, kk, :], first,
                                kc == nkc - 1, [a_key, ("w", wi)], [pk])
                        first = False
                self.tt("dve", self.h[:, tt_, half * 512:(half + 1) * 512], ps[:], self.h[:, tt_, half * 512:(half + 1) * 512],
                        ALU.add, [pk, ("h", tt_)], [("h", tt_)])

    def stage_C(self, b, g):
        S = self.S
        FB = self.FB
        gs = slice(g * 512, (g + 1) * 512)
        v8 = lambda t: t[:].rearrange("p (c t) -> p c t", t=512)
        for tt_ in range(4):
            S.dma("sp", self.h[:, tt_, :], self.x[b, g * 512 + tt_ * 128:g * 512 + (tt_ + 1) * 128, :], self.hsem[tt_],
                  writes=[("h", tt_)])
        xnG = [v8(FB[4]), v8(FB[5])]
        xkeys = [("FB4", f) for f in range(4)] + [("FB5", f) for f in range(4)]
        self.norm_group(xnG, xkeys, "norm_mix")
        mT = [v8(FB[0]), v8(FB[1])]
        sig = v8(FB[2])
        m1 = v8(FB[3])
        Win = self.WB["w_in"]
        for quad in range(2):
            cs = slice(quad * 512, (quad + 1) * 512)
            for (gate_c0, pw, srcT, skey) in ((C_GA, "w_proj_gdn", self.oaT, "oaT"), (C_GB, "w_proj_sb", self.obT, "obT")):
                wi = self.wslot()
                self.wload(wi, Win[:, gate_c0 + quad * 512:gate_c0 + (quad + 1) * 512], 8, 512)
                for f in range(4):
                    ps, pk = S.psum()
                    for kc in range(8):
                        self.mm(ps[:], self.wslots[wi][:, kc, f * 128:(f + 1) * 128], xnG[kc // 4][:, kc % 4, :], kc == 0, kc == 7,
                                xkeys + [("w", wi)], [pk])
                    tmp, tk = self.t5()
                    self.sigmoid(sig[:, f, :], ps[:], tmp[:], [pk], tk, [("FB2", f)])
                wi = self.wslot()
                self.wload(wi, self.WB[pw][:, cs], 8, 512)
                for f in range(4):
                    ps, pk = S.psum()
                    for kc in range(8):
                        self.mm(ps[:], self.wslots[wi][:, kc, f * 128:(f + 1) * 128], srcT[:, kc, gs], kc == 0, kc == 7,
                                [(skey, g), ("w", wi)], [pk])
                    if gate_c0 == C_GA:
                        self.tt("dve", m1[:, f, :], ps[:], sig[:, f, :], ALU.mult, [pk, ("FB2", f)], [("FB3", f)])
                    else:
                        tmp, tk = self.t5()
                        self.tt("dve", tmp[:], ps[:], sig[:, f, :], ALU.mult, [pk, ("FB2", f)], [tk])
                        self.tt("dve", mT[quad][:, f, :], tmp[:], m1[:, f, :], ALU.add, [tk, ("FB3", f)], [("FB%d" % quad, f)])
        mTfull_keys = [("FB0", f) for f in range(4)] + [("FB1", f) for f in range(4)]

        class _TwoBuf:
            def __init__(s, a, b_):
                s.a, s.b = a, b_

            def __getitem__(s, idx):
                p, kc, t = idx
                return (s.a if kc < 4 else s.b)[p, kc % 4, t]
        hnT = [v8(FB[2]), v8(FB[3])]
        hkeys = [("FB2", f) for f in range(4)] + [("FB3", f) for f in range(4)]
        self.tokproj_norm(_TwoBuf(mT[0], mT[1]), mTfull_keys, "w_out", 8, hnT, hkeys, "norm_x")
        if self.dbg and b == 0:
            for tt_ in range(4):
                S.dma("pool", self.dbg_h1[g * 512 + tt_ * 128:g * 512 + (tt_ + 1) * 128, :], self.h[:, tt_, :], self.dbgsem[tt_],
                      reads=[("h", tt_)], writes=["dbgo"])
        qraw = [v8(FB[0]), v8(FB[1])]
        qsq = [v8(FB[4]), v8(FB[5])]
        Wq = self.WB["w_xq"]
        for quad in range(2):
            wi = self.wslot()
            self.wload(wi, Wq[:, quad * 512:(quad + 1) * 512], 8, 512)
            for f in range(4):
                ps, pk = S.psum()
                for kc in range(8):
                    self.mm(ps[:], self.wslots[wi][:, kc, f * 128:(f + 1) * 128], hnT[kc // 4][:, kc % 4, :], kc == 0, kc == 7,
                            hkeys + [("w", wi)], [pk])
                self.cp("act", qraw[quad][:, f, :], ps[:], [pk], [("FB%d" % quad, f)])
                self.act(qsq[quad][:, f, :], ps[:], AF.Square, [pk], [("FB%d" % (4 + quad), f)])
        oxT = [v8(FB[2]), v8(FB[3])]
        def xa_head(hx):
            quad, f0 = hx // 2, (hx % 2) * 2
            stream = 'xa%d' % (hx % 2)
            ps, pk = S.psum(stream)
            for dc in range(2):
                self.mm(ps[:], self.ones_b[:], qsq[quad][:, f0 + dc, :], dc == 0, dc == 1,
                        [("FB%d" % (4 + quad), f0 + dc)] + self.CK, [pk])
            rs, rsk = self.t5()
            self.act(rs[:], ps[:], AF.Ln, [pk], [rsk], scale=1.0 / 256, bias=EPS)
            self.act(rs[:], rs[:], AF.Exp, [rsk], [rsk], scale=-0.5, bias=-0.5 * math.log(16.0) * 2)
            yield
            qn = []
            for dc in range(2):
                qb, qbk = self.b5()
                self.stt(qb[:], qraw[quad][:, f0 + dc, :], self.gcol("xq_norm", dc), rs[:], ALU.mult, ALU.mult,
                         [("FB%d" % quad, f0 + dc), rsk] + self.CK, [qbk])
                qn.append((qb, qbk))
            yield
            PT = []
            for mt in range(2):
                pz, pzk = S.psum(stream)
                for dc in range(2):
                    self.mm(pz[:], self.KT[:, hx * 2 + dc, mt * 128:(mt + 1) * 128], qn[dc][0][:], dc == 0, dc == 1,
                            ["KT", qn[dc][1]], [pzk])
                pt_, ptk = self.b5()
                self.act(pt_[:], pz[:], AF.Exp, [pzk], [ptk])
                PT.append((pt_, ptk))
            yield
            pd, pdk = S.psum(stream)
            for mt in range(2):
                self.mm(pd[:], self.ones_b[:], PT[mt][0][:], mt == 0, mt == 1, [PT[mt][1]] + self.CK, [pdk])
            rd, rdk = self.t5()
            S.op("dve", lambda e: e.reciprocal(out=rd[:], in_=pd[:]), reads=[pdk], writes=[rdk])
            yield
            for dc in range(2):
                po_, pok_ = S.psum(stream)
                for mt in range(2):
                    self.mm(po_[:], self.Vm[:, mt, hx * 256 + dc * 128:hx * 256 + (dc + 1) * 128], PT[mt][0][:], mt == 0, mt == 1,
                            ["Vm", PT[mt][1]], [pok_])
                self.tt("dve", oxT[quad][:, f0 + dc, :], po_[:], rd[:], ALU.mult, [pok_, rdk], [("FB%d" % (2 + quad), f0 + dc)])
        self.run_pipelined([xa_head(hx) for hx in range(4)], 2)
        hnT2 = [v8(FB[0]), v8(FB[1])]
        hkeys2 = [("FB0", f) for f in range(4)] + [("FB1", f) for f in range(4)]
        self.tokproj_norm(_TwoBuf(oxT[0], oxT[1]), hkeys, "w_xo", 8, hnT2, hkeys2, "norm_ffn")
        if self.dbg and b == 0:
            for tt_ in range(4):
                S.dma("pool", self.dbg_h2[g * 512 + tt_ * 128:g * 512 + (tt_ + 1) * 128, :], self.h[:, tt_, :], self.dbgsem[tt_],
                      reads=[("h", tt_)], writes=["dbgo"])
        hnT, hkeys = hnT2, hkeys2
        Wu = self.WB["w_up"]
        if g == 0:
            S.op("pool", lambda e: e.memset(self.halo[:], 0.0), writes=[("halo", c_) for c_ in range(44)])
        S.n_rr = 8
        for half in range(2):
            c_lo = half * 11
            gens = []
            for q0 in range(0, 11, 4):
                nq = min(4, 11 - q0)
                for f in range(nq):
                    gens.append(self.ffn_chunk_gen(g, c_lo, q0, nq, f, hnT, hkeys))
            self.run_pipelined(gens, 3)
            self.down_proj_add(11, c_lo)
        for tt_ in range(4):
            S.dma("pool", self.out[b, g * 512 + tt_ * 128:g * 512 + (tt_ + 1) * 128, :], self.h[:, tt_, :], self.osem[tt_],
                  reads=[("h", tt_)], writes=[("out", tt_)])

    def _tokproj_multi(self, srcT, srckeys, wname, nkc):
        S = self.S
        W = self.WB[wname]
        for half in range(2):
            wi = self.wslot()
            self.wload(wi, W[:, half * 512:(half + 1) * 512], 8, 512)
            for tt_ in range(4):
                ps, pk = S.psum()
                for kc in range(nkc):
                    self.mm(ps[:], srcT[:, kc, tt_ * 128:(tt_ + 1) * 128], self.wslots[wi][:, kc, :], kc == 0, kc == nkc - 1,
                            srckeys + [("w", wi)], [pk])
                self.tt("dve", self.h[:, tt_, half * 512:(half + 1) * 512], ps[:], self.h[:, tt_, half * 512:(half + 1) * 512],
                        ALU.add, [pk, ("h", tt_)], [("h", tt_)])

    def norm_group(self, dstT2, dkeys, gname):
        for tt_ in range(4):
            self.norm_tile(tt_, dstT2, dkeys, gname)

    def norm_tile(self, tt_, dstT2, dkeys, gname):
        S = self.S
        st = self.stat
        src = self.h[:, tt_, :]
        sk = [("h", tt_)]
        i = 0
        xh = self.xh[i]
        self.act(xh[:], src, AF.Square, sk, [("xh", i), "stat"], accum_out=st[:, 0:1])
        self.act(st[:, 1:2], st[:, 0:1], AF.Ln, ["stat"], ["stat"], scale=1.0 / 1024, bias=EPS)
        self.act(st[:, 2:3], st[:, 1:2], AF.Exp, ["stat"], ["stat"], scale=-0.5)
        self.act(xh[:], src, AF.Copy, sk + ["stat"], [("xh", i)], scale=st[:, 2:3])
        for half in range(2):
            ps, pk = S.psum()
            for c4 in range(4):
                c = half * 4 + c4
                self.mm(ps[:, c4 * 128:(c4 + 1) * 128], xh[:, c * 128:(c + 1) * 128], self.ident_b[:], True, True,
                        [("xh", i)] + self.CK, [pk])
            o = self.col_off[gname] + half * 4
            gb = self.colsR1[:, o:o + 4].unsqueeze(2).to_broadcast([128, 4, 128])
            self.tt("dve", dstT2[half][:, :, tt_ * 128:(tt_ + 1) * 128], ps[:].rearrange("p (c t) -> p c t", t=128), gb,
                    ALU.mult, [pk] + self.CK, dkeys[half * 4:(half + 1) * 4])

    def tokproj_norm(self, srcT, srckeys, wname, nkc, dstT2, dkeys, gname):
        S = self.S
        W = self.WB[wname]
        wis = []
        for half in range(2):
            wi = self.wslot()
            self.wload(wi, W[:, half * 512:(half + 1) * 512], 8, 512)
            wis.append(wi)

        def proj(tt_):
            for half in range(2):
                wi = wis[half]
                ps, pk = S.psum()
                for kc in range(nkc):
                    self.mm(ps[:], srcT[:, kc, tt_ * 128:(tt_ + 1) * 128], self.wslots[wi][:, kc, :], kc == 0, kc == nkc - 1,
                            srckeys + [("w", wi)], [pk])
                self.tt("dve", self.h[:, tt_, half * 512:(half + 1) * 512], ps[:], self.h[:, tt_, half * 512:(half + 1) * 512],
                        ALU.add, [pk, ("h", tt_)], [("h", tt_)])
        proj(0)
        proj(1)
        self.norm_tile(0, dstT2, dkeys, gname)
        proj(2)
        self.norm_tile(1, dstT2, dkeys, gname)
        proj(3)
        self.norm_tile(2, dstT2, dkeys, gname)
        self.norm_tile(3, dstT2, dkeys, gname)

    def build(self, stages=("mem", "A", "B", "C")):
        S = self.S
        self.setup_weights()
        self.setup_consts()
        self.setup_tiles()
        for b in range(self.nseq):
            self.fence()
            if "A" in stages:
                self.stage_A(b)
            S.n_rr = 6
            self.fence()
            if "B" in stages or "P" in stages:
                self.stage_gdn_prep(b)
            doG = "B" in stages or "G" in stages
            doS = "B" in stages or "S" in stages
            if doG:
                self.run_pipelined([self.gdn_front_gen(b, 0)], 1)
            for h in range(self.nheads):
                gens = []
                if doG:
                    gens.append(self.gdn_loop_gen(b, h))
                if doS:
                    self.sb_front(b, h)
                    gens.append(self.sb_loop_gen(b, h))
                self.run_pipelined(gens, 2)
                if doG:
                    g2 = [self.gdn_tail_gen(b, h)]
                    if h + 1 < self.nheads:
                        g2.append(self.gdn_front_gen(b, h + 1))
                    self.run_pipelined(g2, 2)
                self.emit_casts(12)
            if "B" in stages:
                if self.dbg and b == 0:
                    S.dma("pool", self.dbg_oa, self.oaT[:].rearrange("p c t -> p (c t)"), self.dbgsem[0],
                          reads=[("oaT", g) for g in range(NG)], writes=["dbgo"])
                    S.dma("pool", self.dbg_ob, self.obT[:].rearrange("p c t -> p (c t)"), self.dbgsem[1],
                          reads=[("obT", g) for g in range(NG)], writes=["dbgo"])
            self.fence()
            self.emit_casts(len(self.cast_list))
            if "C" in stages:
                S.n_rr = 8
                self.stage_mem(b)
                for g in range(NG):
                    self.stage_C(b, g)
        fin = [("out", t) for t in range(4)] + ["dbgo"]
        S.wait_all("pool", fin)
        S.wait_all("sp", fin)
        self.counts = {k: v.n for k, v in S.eng.items()}
        S.close()
        return self.nc


_CACHE = {}


def kernel(**inputs):
    n_cores = 8
    nseq = 4
    if "nc" not in _CACHE:
        _CACHE["nc"] = Builder(nseq=nseq, dbg=False).build()
    nc = _CACHE["nc"]
    x = np.ascontiguousarray(np.asarray(inputs["x"], dtype=np.float32))
    mem = np.ascontiguousarray(np.asarray(inputs["mem"], dtype=np.float32))
    params = {n: np.ascontiguousarray(np.asarray(inputs[n], dtype=np.float32)) for n in PARAM_NAMES}
    in_maps = []
    for c in range(n_cores):
        m = {"x": x[c * nseq:(c + 1) * nseq], "mem": mem[c * nseq:(c + 1) * nseq]}
        m.update(params)
        in_maps.append(m)
    res = run_bass_kernel_spmd(nc, in_maps, core_ids=list(range(n_cores)))
    return np.concatenate([r["out"] for r in res.results], axis=0).astype(np.float32)
```
